# Optimizing a Trainium2 kernel written in Bass

```python
import jax, jax.numpy as jnp
from jax import lax
import numpy as np

D_MODEL = 1024
BATCH = 8
SEQ = 2048
DEPTH = 1
DEC_BATCH = 128
DEC_SEQ = 8
PAST_LEN = 16384
PAGE_SIZE = 128

POOL_WINDOWS = (2, 4, 8, 16)
N_POOL_GROUPS = 4
D_POOL_GROUP = D_MODEL // 8
D_POOL = N_POOL_GROUPS * D_POOL_GROUP
POOL_PAD = max(POOL_WINDOWS) - 1
CHUNK = 128
N_SGU_HEADS = 4
D_SGU = D_MODEL // 2
D_SGU_HEAD = D_SGU // N_SGU_HEADS
D_IN = D_POOL + 2 * D_SGU + 2 * D_MODEL
D_FF = ((8 * D_MODEL // 3 + 255) // 256) * 256
CONV_K = 3
EPS = 1e-6

kernel_name = "gated_pool_sgu_convffn_step"


def _rmsnorm(x, g):
    xf = x.astype(jnp.float32)
    r = lax.rsqrt(jnp.mean(xf * xf, axis=-1, keepdims=True) + EPS)
    return (xf * r).astype(x.dtype) * g


def _pool_branch(p_ext, T, full_windows, pool_w, pool_scale):
    B = p_ext.shape[0]
    cs = jnp.cumsum(p_ext.astype(jnp.float32), axis=1)
    cs = jnp.pad(cs, ((0, 0), (1, 0), (0, 0)))
    x_tok = p_ext[:, POOL_PAD:].astype(jnp.float32)
    t = jnp.arange(T, dtype=jnp.float32)[:, None]
    outs = []
    for g, w in enumerate(POOL_WINDOWS):
        lo, hi = g * D_POOL_GROUP, (g + 1) * D_POOL_GROUP
        s = cs[:, POOL_PAD + 1:POOL_PAD + 1 + T, lo:hi] - cs[:, POOL_PAD + 1 - w:POOL_PAD + 1 - w + T, lo:hi]
        cnt = jnp.full_like(t, float(w)) if full_windows else jnp.minimum(float(w), t + 1.0)
        outs.append(s / cnt - x_tok[..., lo:hi])
    d = jnp.stack(outs, axis=2).astype(p_ext.dtype)
    y = jnp.einsum('btgc,gcd->btgd', d, pool_w).reshape(B, T, D_POOL)
    return y * pool_scale


def _sgu_branch(z, n_chunks, L, sgu_norm_g, sgu_w, sgu_b):
    B, T, _ = z.shape
    u, v = z[..., :D_SGU], z[..., D_SGU:]
    v = _rmsnorm(v, sgu_norm_g)
    vh = v.reshape(B, n_chunks, L, N_SGU_HEADS, D_SGU_HEAD)
    mask = jnp.tril(jnp.ones((L, L), dtype=bool))
    w_s = jnp.where(mask[None], sgu_w[:, :L, :L], jnp.zeros((), sgu_w.dtype))
    mix = jnp.einsum('hts,bcshd->bcthd', w_s, vh) + jnp.transpose(sgu_b[:, :L])[None, None, :, :, None]
    out = u * mix.reshape(B, T, D_SGU)
    return out, v


def _layer(x, pool_prev, conv_prev, full_windows, n_chunks, L,
           norm1_g, w_in, pool_w, pool_scale, w_pool_out, sgu_norm_g, sgu_w, sgu_b,
           w_sgu_out, w_o, norm2_g, ffn_w_up, ffn_w_gate, ffn_conv_w, ffn_conv_b, ffn_w_down):
    B, T, _ = x.shape
    h = _rmsnorm(x, norm1_g)
    zin = h @ w_in
    p_in = zin[..., :D_POOL]
    z_sgu = zin[..., D_POOL:D_POOL + 2 * D_SGU]
    g_in = zin[..., D_POOL + 2 * D_SGU:]
    p_ext = jnp.concatenate([pool_prev.astype(p_in.dtype), p_in], axis=1)
    a_out = _pool_branch(p_ext, T, full_windows, pool_w, pool_scale) @ w_pool_out
    s_out, v_rows = _sgu_branch(jax.nn.gelu(z_sgu), n_chunks, L, sgu_norm_g, sgu_w, sgu_b)
    b_out = s_out @ w_sgu_out
    gates = jax.nn.sigmoid(g_in.astype(jnp.float32)).astype(x.dtype)
    ga, gb = gates[..., :D_MODEL], gates[..., D_MODEL:]
    x = x + (ga * a_out + gb * b_out) @ w_o
    h2 = _rmsnorm(x, norm2_g)
    a = h2 @ ffn_w_up
    ext = jnp.concatenate([conv_prev.astype(a.dtype), a], axis=1)
    c = ext[:, 0:T] * ffn_conv_w[0]
    for k in range(1, CONV_K):
        c = c + ext[:, k:k + T] * ffn_conv_w[k]
    f = jax.nn.gelu(c + ffn_conv_b) * (h2 @ ffn_w_gate)
    x = x + f @ ffn_w_down
    return x, p_ext[:, -POOL_PAD:], ext[:, -(CONV_K - 1):], v_rows


def setup_inputs(seed: int = 0) -> dict:
    key = jax.random.key(seed)
    ks = jax.random.split(key, 24)
    f32 = jnp.float32
    n = lambda k, s, sc: jax.random.normal(k, s, f32) * sc
    return {
        "x_prompt": n(ks[0], (BATCH, SEQ, D_MODEL), 1.0),
        "x_sample": n(ks[1], (DEC_BATCH, DEC_SEQ, D_MODEL), 1.0),
        "state_pool": n(ks[2], (DEPTH, DEC_BATCH, POOL_PAD, D_POOL), 1.0),
        "state_ffn_conv": n(ks[3], (DEPTH, DEC_BATCH, CONV_K - 1, D_FF), 1.0),
        "norm1_g": 1.0 + n(ks[4], (DEPTH, D_MODEL), 0.02),
        "w_in": n(ks[5], (DEPTH, D_MODEL, D_IN), D_MODEL ** -0.5),
        "pool_w": n(ks[6], (DEPTH, N_POOL_GROUPS, D_POOL_GROUP, D_POOL_GROUP), D_POOL_GROUP ** -0.5),
        "pool_scale": 1.0 + n(ks[7], (DEPTH, D_POOL), 0.02),
        "w_pool_out": n(ks[8], (DEPTH, D_POOL, D_MODEL), D_POOL ** -0.5),
        "sgu_norm_g": 1.0 + n(ks[9], (DEPTH, D_SGU), 0.02),
        "sgu_w": n(ks[10], (DEPTH, N_SGU_HEADS, CHUNK, CHUNK), CHUNK ** -0.5),
        "sgu_b": 1.0 + n(ks[11], (DEPTH, N_SGU_HEADS, CHUNK), 0.1),
        "w_sgu_out": n(ks[12], (DEPTH, D_SGU, D_MODEL), D_SGU ** -0.5),
        "w_o": n(ks[13], (DEPTH, D_MODEL, D_MODEL), D_MODEL ** -0.5),
        "norm2_g": 1.0 + n(ks[14], (DEPTH, D_MODEL), 0.02),
        "ffn_w_up": n(ks[15], (DEPTH, D_MODEL, D_FF), D_MODEL ** -0.5),
        "ffn_w_gate": n(ks[16], (DEPTH, D_MODEL, D_FF), D_MODEL ** -0.5),
        "ffn_conv_w": n(ks[17], (DEPTH, CONV_K, D_FF), CONV_K ** -0.5),
        "ffn_conv_b": n(ks[18], (DEPTH, D_FF), 0.02),
        "ffn_w_down": n(ks[19], (DEPTH, D_FF, D_MODEL), D_FF ** -0.5),
        "final_norm_g": 1.0 + n(ks[20], (D_MODEL,), 0.02),
    }


def reference(x_prompt, x_sample, state_pool, state_ffn_conv, norm1_g, w_in, pool_w, pool_scale,
              w_pool_out, sgu_norm_g, sgu_w, sgu_b, w_sgu_out, w_o, norm2_g, ffn_w_up, ffn_w_gate,
              ffn_conv_w, ffn_conv_b, ffn_w_down, final_norm_g):
    B, T = x_prompt.shape[0], x_prompt.shape[1]
    DB, TS = x_sample.shape[0], x_sample.shape[1]
    xp, xs = x_prompt, x_sample
    pool_p, pool_s, conv_p, conv_s, v_s = [], [], [], [], []
    for l in range(DEPTH):
        w = (norm1_g[l], w_in[l], pool_w[l], pool_scale[l], w_pool_out[l], sgu_norm_g[l], sgu_w[l],
             sgu_b[l], w_sgu_out[l], w_o[l], norm2_g[l], ffn_w_up[l], ffn_w_gate[l], ffn_conv_w[l],
             ffn_conv_b[l], ffn_w_down[l])
        zp_pool = jnp.zeros((B, POOL_PAD, D_POOL), xp.dtype)
        zp_conv = jnp.zeros((B, CONV_K - 1, D_FF), xp.dtype)
        xp, npool_p, nconv_p, _ = _layer(xp, zp_pool, zp_conv, False, T // CHUNK, CHUNK, *w)
        xs, npool_s, nconv_s, v_rows = _layer(xs, state_pool[l], state_ffn_conv[l], True, 1, TS, *w)
        pool_p.append(npool_p); pool_s.append(npool_s)
        conv_p.append(nconv_p); conv_s.append(nconv_s); v_s.append(v_rows)
    y_prompt = _rmsnorm(xp, final_norm_g)
    y_sample = _rmsnorm(xs, final_norm_g)
    new_pool_prompt = jnp.stack(pool_p, axis=0)
    new_pool_sample = jnp.stack(pool_s, axis=0)
    new_conv_prompt = jnp.stack(conv_p, axis=0)
    new_conv_sample = jnp.stack(conv_s, axis=0)
    new_sgu_v_sample = jnp.stack(v_s, axis=0)
    return (y_prompt, y_sample, new_pool_prompt, new_pool_sample, new_conv_prompt, new_conv_sample, new_sgu_v_sample)
```

```python
import contextlib
import numpy as np
import concourse.bass as bass
import concourse.mybir as mybir
from concourse.bass_utils import run_bass_kernel_spmd

F32 = mybir.dt.float32
BF16 = mybir.dt.bfloat16
AF = mybir.ActivationFunctionType
ALU = mybir.AluOpType

D = 1024
DFF = 2816
NFF = 22
EPS = 1e-6
POOL_W = (2, 4, 8, 16)
NSLOT = 6
STOP_AT = None
SKIP = set()


class _Stop(Exception):
    pass
TW = 1152


class _Op:
    __slots__ = ("id", "eng", "emit", "reads", "writes", "dma", "sem", "target",
                 "deps", "ticket", "signal", "waits")


class Prog:
    ENGS = ("pe", "act", "dve", "pool", "sp")

    def __init__(self, nc):
        self.nc = nc
        self.stack = contextlib.ExitStack()
        self.ops = []
        self.last_writer = {}
        self.readers = {}
        self.cnt_sem = {e: self.stack.enter_context(nc.semaphore("c_" + e)) for e in self.ENGS}
        self.dma_count = {}
        self.dma_sems = {}

    def sbuf(self, name, shape, dt):
        return self.stack.enter_context(self.nc.sbuf_tensor(name, list(shape), dt))

    def psum(self, name, shape, dt):
        return self.stack.enter_context(self.nc.psum_tensor(name, list(shape), dt))

    def add(self, eng, emit, reads=(), writes=(), dma_sem=None):
        op = _Op()
        op.id = len(self.ops)
        op.eng = eng
        op.emit = emit
        op.dma = dma_sem is not None
        op.sem = None
        op.target = None
        op.signal = False
        op.ticket = None
        if op.dma:
            if dma_sem not in self.dma_sems:
                self.dma_sems[dma_sem] = self.stack.enter_context(self.nc.semaphore("d_" + dma_sem))
                self.dma_count[dma_sem] = 0
            self.dma_count[dma_sem] += 16
            op.sem = dma_sem
            op.target = self.dma_count[dma_sem]
        raw = set()
        other = set()
        for k in reads:
            w = self.last_writer.get(k)
            if w is not None:
                raw.add(w)
        for k in writes:
            w = self.last_writer.get(k)
            if w is not None:
                other.add(w)
            rd = self.readers.get(k)
            if rd is not None:
                other.update(rd[0].values())
                other.update(rd[1])
        best = {}
        dmas = set()
        for pid in raw | other:
            p = self.ops[pid]
            if p.dma:
                dmas.add(pid)
                continue
            if (not op.dma) and p.eng == eng:
                if eng == "pe" or (pid not in raw and eng != "pool"):
                    continue
            if best.get(p.eng, -1) < pid:
                best[p.eng] = pid
        op.deps = list(best.values()) + list(dmas)
        for pid in best.values():
            self.ops[pid].signal = True
        for k in reads:
            rd = self.readers.setdefault(k, ({}, []))
            if op.dma:
                rd[1].append(op.id)
            else:
                rd[0][eng] = op.id
        for k in writes:
            self.last_writer[k] = op.id
            self.readers[k] = ({}, [])
        self.ops.append(op)
        return op

    def finalize(self):
        ops = self.ops
        tick = {e: 0 for e in self.ENGS}
        for op in ops:
            if op.signal and not op.dma:
                tick[op.eng] += 1
                op.ticket = tick[op.eng]
        seen = {e: {} for e in self.ENGS}
        for op in ops:
            w = {}
            for pid in op.deps:
                p = ops[pid]
                if p.dma:
                    key, val = ("d", p.sem), p.target
                else:
                    key, val = ("c", p.eng), p.ticket
                if seen[op.eng].get(key, 0) >= val:
                    continue
                if w.get(key, 0) < val:
                    w[key] = val
            for key, val in w.items():
                seen[op.eng][key] = val
            op.waits = list(w.items())
        final_waits = [(("d", n), v) for n, v in self.dma_count.items() if v > 0]
        streams = {e: [op for op in ops if op.eng == e] for e in self.ENGS}

        def sem_of(key):
            kind, n = key
            return self.dma_sems[n] if kind == "d" else self.cnt_sem[n]

        def run(ename, e):
            for op in streams[ename]:
                for key, val in op.waits:
                    e.wait_ge(sem_of(key), val)
                inst = op.emit(e)
                if op.dma:
                    inst.then_inc(self.dma_sems[op.sem], 16)
                elif op.signal:
                    inst.then_inc(self.cnt_sem[ename], 1)
            if ename == "sp":
                for key, val in final_waits:
                    e.wait_ge(sem_of(key), val)

        with self.nc.Block() as block:
            @block.tensor
            def _(e):
                run("pe", e)

            @block.scalar
            def _(e):
                run("act", e)

            @block.vector
            def _(e):
                run("dve", e)

            @block.gpsimd
            def _(e):
                run("pool", e)

            @block.sync
            def _(e):
                run("sp", e)
        self.stats = {e: len(streams[e]) for e in self.ENGS}
        self.stats["waits"] = sum(len(op.waits) for op in ops)
        self.stats["signals"] = sum(1 for op in ops if op.signal)

    def close(self):
        self.stack.close()


def build_program():
    nc = bass.Bass("TRN2", target_bir_lowering=False)
    P = Prog(nc)

    def din(name, shape):
        return nc.dram_tensor(name, list(shape), F32, kind="ExternalInput").ap()

    def dout(name, shape):
        return nc.dram_tensor(name, list(shape), F32, kind="ExternalOutput").ap()

    xp = din("xp", [2048, D])
    xs = din("xs", [128, D])
    spfm = din("spfm", [128, 4 * 16 * 23])
    scfm = din("scfm", [128, NFF * 32])
    win = din("win", [128, 7 * 4096])
    wpso = din("wpso", [128, 2 * 4096])
    poolw = din("poolw", [128, 512])
    wo = din("wo", [128, 2 * 4096])
    wup = din("wup", [128, 8 * DFF])
    wgt = din("wgt", [128, 8 * DFF])
    wdn = din("wdn", [128, 2 * NFF * 512])
    cvec = din("cvec", [128, 108])
    gsgu = din("gsgu", [128, 512])
    gfin = din("gfin", [128, D])
    g1bc = din("g1bc", [128, D])
    g2bc = din("g2bc", [128, D])
    sgub = din("sgub", [128, 1024])
    wst_d = din("wst", [128, 512])
    wsts_d = din("wsts", [128, 512])

    yp = dout("yp", [2048, D])
    ys = dout("ys", [128, D])
    npp = dout("npp", [128, 4 * 16])
    nps = dout("nps", [128, 4 * 16 * 23])
    ncp = dout("ncp", [128, NFF * 2])
    ncs = dout("ncs", [128, NFF * 32])
    nv = dout("nv", [128, 512])

    X = P.sbuf("X", [128, 9, D], F32)
    R = P.sbuf("R", [128, NFF * TW], BF16)
    MH = P.sbuf("MH", [128, 8, TW], BF16)
    PEXT = P.sbuf("PEXT", [128, 1040], F32)
    PEXTS = P.sbuf("PEXTS", [128, 4, 16, 23], F32)
    RING = [P.sbuf(f"RING{i}", [128, 4096], BF16) for i in range(NSLOT)]
    TF = [P.sbuf(f"TF{i}", [128, 528], F32) for i in range(6)]
    TB = [P.sbuf(f"TB{i}", [128, 512], BF16) for i in range(4)]
    XN = [P.sbuf(f"XN{i}", [128, D], BF16) for i in range(3)]
    G1B = P.sbuf("G1B", [128, D], F32)
    G2B = P.sbuf("G2B", [128, D], F32)
    CV = P.sbuf("CV", [128, 108], F32)
    GS = P.sbuf("GS", [128, 512], F32)
    GF = P.sbuf("GF", [128, D], F32)
    SGB = P.sbuf("SGB", [128, 1024], BF16)
    WST = P.sbuf("WST", [128, 512], BF16)
    WSTS = P.sbuf("WSTS", [128, 512], BF16)
    IDENT = P.sbuf("IDENT", [128, 128], BF16)
    ONES1 = P.sbuf("ONES1", [128, 128], BF16)
    NEGH = P.sbuf("NEGH", [128, 1], F32)
    INVC = P.sbuf("INVC", [128, 16], F32)
    CARRY = P.sbuf("CARRY", [128, 4, 16], F32)
    HALO = P.sbuf("HALO", [128, NFF, 2], F32)
    CS = P.sbuf("CS", [128, NFF, 16, 2], F32)
    STAT = P.sbuf("STAT", [128, 216], F32)
    TMP16 = P.sbuf("TMP16", [128, 16], F32)
    SCR = P.sbuf("SCR", [128, 4], F32)
    SCR2 = P.sbuf("SCR2", [128, 32], F32)
    PS = [P.psum(f"ps{i}", [128, 512], F32) for i in range(8)]

    O_HT, O_YP, O_U, O_VT, O_DD = 0, 9216, 13824, 18432, 23040

    def HT(k, c0, c1):
        return R[:, O_HT + k * TW + c0: O_HT + k * TW + c1]

    def YP(g, c0, c1):
        return R[:, O_YP + g * TW + c0: O_YP + g * TW + c1]

    def U(h, c0, c1):
        return R[:, O_U + h * TW + c0: O_U + h * TW + c1]

    def VT(i, f0, f1):
        return R[:, O_VT + i * 512 + f0: O_VT + i * 512 + f1]

    def DD(b, c0, c1):
        return R[:, O_DD + b * TW + c0: O_DD + b * TW + c1]

    def FF(j, c0, c1):
        return R[:, j * TW + c0: j * TW + c1]

    REG = "REG"

    def MM(out, lhsT, rhs, start, stop, reads, writes):
        P.add("pe", lambda e: e.matmul(out, lhsT=lhsT, rhs=rhs, start=start, stop=stop), reads, writes)

    def TR(out, in_, reads, writes):
        P.add("pe", lambda e: e.transpose(out, in_, IDENT[:]), list(reads) + ["IDENT"], writes)

    def ACT(out, in_, func, reads, writes, **kw):
        P.add("act", lambda e: e.activation(out=out, in_=in_, func=func, **kw), reads, writes)

    def TT(eng, out, in0, in1, op, reads, writes):
        P.add(eng, lambda e: e.tensor_tensor(out=out, in0=in0, in1=in1, op=op), reads, writes)

    def TS(eng, out, in0, s1, s2, op0, op1, reads, writes):
        if s2 is None:
            P.add(eng, lambda e: e.tensor_scalar(out=out, in0=in0, scalar1=s1, scalar2=None, op0=op0),
                  reads, writes)
        else:
            P.add(eng, lambda e: e.tensor_scalar(out=out, in0=in0, scalar1=s1, scalar2=s2, op0=op0, op1=op1),
                  reads, writes)

    def STT(out, in0, scalar, in1, op0, op1, reads, writes):
        P.add("dve", lambda e: e.scalar_tensor_tensor(out=out, in0=in0, scalar=scalar, in1=in1, op0=op0, op1=op1),
              reads, writes)

    def CP(eng, out, in_, reads, writes):
        P.add(eng, lambda e: e.tensor_copy(out=out, in_=in_), reads, writes)

    def MSET(eng, ap, val, writes):
        P.add(eng, lambda e: e.memset(ap, val), (), writes)

    def DMA(eng, out, in_, sem, reads, writes):
        P.add(eng, lambda e: e.dma_start(out=out, in_=in_), reads, writes, dma_sem=sem)

    cnt = {"ps": 0, "tf": 0, "tb": 0, "xn": 0, "stat": 0}

    def psb():
        b = cnt["ps"] % 8
        cnt["ps"] += 1
        return PS[b], ("ps", b)

    def tf():
        b = cnt["tf"] % len(TF)
        cnt["tf"] += 1
        return TF[b], ("TF", b)

    def tb():
        b = cnt["tb"] % len(TB)
        cnt["tb"] += 1
        return TB[b], ("TB", b)

    def xn():
        b = cnt["xn"] % 3
        cnt["xn"] += 1
        return XN[b], ("XN", b)

    def stat3():
        c = cnt["stat"]
        cnt["stat"] += 3
        assert c + 3 <= 216
        return [(STAT[:, c + i: c + i + 1], ("ST", c + i)) for i in range(3)]

    def tile_list():
        tl = []
        tl.append(("win0", win[:, 0:4096], 4096))
        tl.append(("win1", win[:, 4096:8192], 4096))
        tl.append(("poolw", poolw[:, 0:512], 512))
        tl.append(("win2", win[:, 8192:12288], 4096))
        for cg in range(2):
            tl.append((f"pso{cg}", wpso[:, cg * 4096:(cg + 1) * 4096], 4096))
            tl.append((f"wga{cg}", win[:, (3 + cg) * 4096:(4 + cg) * 4096], 4096))
            tl.append((f"wgb{cg}", win[:, (5 + cg) * 4096:(6 + cg) * 4096], 4096))
        tl.append(("wo0", wo[:, 0:4096], 4096))
        tl.append(("wo1", wo[:, 4096:8192], 4096))
        for jg in range(6):
            ncol = 512 if jg < 5 else 256
            tl.append((f"wup{jg}", wup[:, jg * 4096: jg * 4096 + 8 * ncol], 8 * ncol))
            tl.append((f"wgt{jg}", wgt[:, jg * 4096: jg * 4096 + 8 * ncol], 8 * ncol))
        for half in range(2):
            for kg in range(3):
                k0, k1 = 8 * kg, min(8 * kg + 8, NFF)
                tl.append((f"wdn{half}_{kg}",
                           wdn[:, (half * NFF + k0) * 512:(half * NFF + k1) * 512], (k1 - k0) * 512))
        return tl

    st_tiles = tile_list()
    NT_ST = len(st_tiles)
    all_tiles = st_tiles + st_tiles
    ring_state = {"next_load": 0, "cursor": 0}

    def ring_issue(t):
        if t >= len(all_tiles):
            return
        name, src, n = all_tiles[t]
        s = t % NSLOT
        DMA("pool", RING[s][:, 0:n], src, f"w{s}", (), [("w", s)])

    def ring_take(name):
        t = ring_state["cursor"]
        assert all_tiles[t][0] == name, (all_tiles[t][0], name)
        ring_state["cursor"] += 1
        s = t % NSLOT
        return t, RING[s], ("w", s)

    deferred = []

    def ring_release(t, defer=False):
        if defer:
            deferred.append(t + NSLOT)
        else:
            ring_issue(t + NSLOT)

    def flush_deferred():
        for t in deferred:
            ring_issue(t)
        deferred.clear()

    early_x0 = set()
    MSET("pool", NEGH[:], -0.5, ["NEGH"])
    t0, k0 = tf()
    MSET("pool", t0[:, 0:128], 1.0, [k0])
    P.add("pool", lambda e: e.affine_select(out=IDENT[:], in_=t0[:, 0:128], pattern=[[-1, 128]],
                                            compare_op=ALU.is_equal, fill=0.0, base=0, channel_multiplier=1),
          [k0], ["IDENT"])
    DMA("sp", X[:, 0, :], xp[0:128, :], "x0", (), [("X", 0, 0), ("X", 0, 1)])
    DMA("sp", G1B[:], g1bc, "c10", (), ["G1B"])
    early_x0.add(0)
    for i in range(1, 8):
        DMA("sp", X[:, i, :], xp[128 * i: 128 * i + 128, :], f"x{i}", (), [("X", i, 0), ("X", i, 1)])
        early_x0.add(i)
    ring_issue(0)
    for t in range(1, NSLOT):
        deferred.append(t)
    MSET("pool", ONES1[:], 1.0 / 128.0, ["ONES1"])
    MSET("pool", CARRY[:], 0.0, [("CARRY", g) for g in range(4)])
    MSET("pool", HALO[:], 0.0, [("HALO", j) for j in range(NFF)])
    MSET("pool", SCR[:], 0.0, ["SCR"])
    for t in range(15):
        MSET("pool", INVC[:, t:t + 1], 1.0 / (t + 1), ["INVC"])
    DMA("sp", CV[:], cvec, "c0", (), ["CV"])

    def late_setup():
        DMA("sp", GS[:], gsgu, "c1", (), ["GS"])
        DMA("sp", G2B[:], g2bc, "c11", (), ["G2B"])
        for src, dst, key, sem in ((wst_d, WST, "WST", "c3"), (wsts_d, WSTS, "WSTS", "c4")):
            t1, k1 = tf()
            DMA("sp", t1[:, 0:512], src, sem, (), [k1])
            P.add("pool", lambda e, t1=t1, dst=dst: e.affine_select(
                out=dst[:].rearrange("p (h t) -> p h t", h=4),
                in_=t1[:, 0:512].rearrange("p (h t) -> p h t", h=4),
                pattern=[[0, 4], [1, 128]], compare_op=ALU.is_ge, fill=0.0, base=0, channel_multiplier=-1),
                [k1], [key])
        for half in range(2):
            t1, k1 = tf()
            DMA("sp", t1[:, 0:512], sgub[:, half * 512:(half + 1) * 512], f"c{5 + half}", (), [k1])
            CP("dve", SGB[:, half * 512:(half + 1) * 512], t1[:, 0:512], [k1], ["SGB"])
        DMA("sp", GF[:], gfin, "c2", (), ["GF"])
        DMA("sp", PEXTS[:].rearrange("p g q t -> p (g q t)"), spfm, "c7", (), [("PEXTS", g) for g in range(4)])
        DMA("sp", CS[:].rearrange("p j q r -> p (j q r)"), scfm, "c8", (), [("CS", j) for j in range(NFF)])

    DMA("sp", X[:, 8, :], xs, "x8", (), [("X", 8, 0), ("X", 8, 1)])
    early_x = set()
    pending_reload = []

    def reload_x(i):
        DMA("sp", X[:, i, :], xp[1024 + 128 * i: 1024 + 128 * i + 128, :], f"x{i}", (),
            [("X", i, 0), ("X", i, 1)])
        early_x.add(i)

    G1 = lambda k: CV[:, k:k + 1]
    G2 = lambda k: CV[:, 8 + k:9 + k]
    PSC = lambda g: CV[:, 16 + g:17 + g]
    CW = lambda r, j: CV[:, 20 + r * NFF + j: 21 + r * NFF + j]
    CB = lambda j: CV[:, 86 + j: 87 + j]

    def norm_stats(i):
        xk = [("X", i, 0), ("X", i, 1)]
        (ss, kss), (ms, kms), (rs, krs) = stat3()
        xb, kx = xn()
        ACT(xb[:], X[:, i, :], AF.Square, xk, [kx, kss], accum_out=ss)
        TS("pool", ms, ss, 1.0 / D, EPS, ALU.mult, ALU.add, [kss], [kms])
        TT("pool", rs, ms, NEGH[:], ALU.pow, [kms, "NEGH"], [krs])
        return xb, kx, rs, krs

    def norm_scale(i, xb, kx, rs, krs, GBC, gkey):
        xk = [("X", i, 0), ("X", i, 1)]
        STT(xb[:], X[:, i, :], rs, GBC[:], ALU.mult, ALU.mult, xk + [krs, gkey], [kx])

    def norm_pe(eng, xb, kx, dst3d, dst_keys, extra):
        ps, kp = psb()
        psv = ps[:].bitcast(BF16)
        for k in range(8):
            TR(psv[:, k * 128:(k + 1) * 128], xb[:, k * 128:(k + 1) * 128], [kx], [kp])
        src = psv.rearrange("p (k t) -> p k t", k=8)
        if eng == "act":
            ACT(dst3d, src, AF.Identity, [kp] + extra, dst_keys)
        else:
            CP("dve", dst3d, src, [kp] + extra, dst_keys)

    s0 = {"s": 0, "nst": {}}
    order1 = [8, 0, 1, 2, 3, 4, 5, 6, 7]

    def s0_step():
        sidx = s0["s"]
        n = len(order1)
        if sidx < n:
            t = order1[sidx]
            s0["nst"][t] = norm_stats(t)
        if 0 <= sidx - 1 < n:
            t = order1[sidx - 1]
            norm_scale(t, *s0["nst"][t], G1B, "G1B")
        if 0 <= sidx - 2 < n:
            t = order1[sidx - 2]
            pxb, pkx = s0["nst"].pop(t)[:2]
            norm_pe("act" if sidx % 2 == 0 else "dve", pxb, pkx, MH[:, :, 128 * t:128 * t + 128],
                    [("MH", k, t) for k in range(8)], [])
        s0["s"] += 1

    def s0_done():
        return s0["s"] >= len(order1) + 2

    def pool_adds(ext, ek, Lx, w, three_d, add_eng="pool"):
        S = (lambda ap, a, b: ap[:, :, a:b]) if three_d else (lambda ap, a, b: ap[:, a:b])

        def view(t):
            if three_d:
                return t[:, 0:16 * Lx].rearrange("p (q t) -> p q t", q=16)
            return t[:, 0:Lx]
        cur, ck = ext, list(ek)
        have, vf = 1, 0
        while have < w:
            tbuf, tk = tf()
            nxt = view(tbuf)
            lo = vf + have
            TT(add_eng, S(nxt, lo, Lx), S(cur, lo, Lx), S(cur, lo - have, Lx - have), ALU.add, ck, [tk])
            vf = lo
            have *= 2
            cur, ck = nxt, [tk]
        return cur, ck, S

    def pool_finish(state, ext, ek, Hh, T, w, dd_out, dd_keys, fix_first):
        cur, ck, S = state
        STT(dd_out, S(cur, Hh, Hh + T), 1.0 / w, S(ext, Hh, Hh + T), ALU.mult, ALU.subtract,
            ck + list(ek) + [REG], dd_keys)
        if fix_first and w > 1:
            n = w - 1
            TT("dve", TMP16[:, 0:n], cur[:, Hh:Hh + n], INVC[:, 0:n], ALU.mult, ck + ["INVC"], ["TMP16"])
            TT("dve", dd_out[:, 0:n], TMP16[:, 0:n], ext[:, Hh:Hh + n], ALU.subtract,
               ["TMP16"] + list(ek) + [REG], dd_keys)

    GELU = AF.Gelu_apprx_tanh
    Uall = R[:, O_U:O_U + 4 * TW].rearrange("p (h t) -> p h t", h=4)

    P.marks = []

    def stage(n):
        P.marks.append((n, sum(1 for o in P.ops if o.eng == "pe")))
        if STOP_AT is not None and n >= STOP_AT:
            raise _Stop()

    try:
      for st in range(2):
        stage(10 * st + 0)
        ntile = 8 if st == 0 else 9
        tiles = list(range(ntile))
        blocks = [(0, 512, [0, 1, 2, 3], False), (512, 512, [4, 5, 6, 7], False)]
        if st == 1:
            blocks.append((1024, 128, [8], True))

        def col(i):
            return 128 * i, 128 * i + 128

        if st > 0:
            DMA("sp", SCR2[:, 0:16], cvec[:, 0:16], "bar", (), [REG])

        HT3 = R[:, O_HT:O_HT + 8 * TW].rearrange("p (k t) -> p k t", k=8)
        if st == 0:
            for i in tiles:
                if i == 8 or i in early_x0:
                    continue
                DMA("sp", X[:, i, :], xp[128 * i: 128 * i + 128, :], f"x{i}", (), [("X", i, 0), ("X", i, 1)])
            nst = {}
            for it in range(ntile + 2):
                if it < ntile:
                    nst[it] = norm_stats(it)
                if 0 <= it - 1 < ntile:
                    norm_scale(it - 1, *nst[it - 1], G1B, "G1B")
                if 0 <= it - 2 < ntile:
                    pi = it - 2
                    c0, c1 = col(pi)
                    pxb, pkx = nst.pop(pi)[:2]
                    norm_pe("act" if pi % 2 == 0 else "dve", pxb, pkx, HT3[:, :, c0:c1],
                            [("HT", k, pi) for k in range(8)], [REG])
        else:
            while not s0_done():
                s0_step()
            for bi, (c0, L, tl, smp) in enumerate(blocks):
                rk = [("MH", k, i) for k in range(8) for i in tl] + [REG]
                wk = [("HT", k, i) for k in range(8) for i in tl]
                if bi == 1:
                    ACT(HT3[:, :, c0:c0 + L], MH[:, :, c0:c0 + L], AF.Identity, rk, wk)
                else:
                    CP("dve", HT3[:, :, c0:c0 + L], MH[:, :, c0:c0 + L], rk, wk)

        if st == 0:
            late_setup()
            for t in deferred[:2]:
                ring_issue(t)
            del deferred[:2]
        stage(10 * st + 1)
        t_w0, W0, kw0 = ring_take("win0")
        W0v = W0[:].rearrange("p (k c) -> p k c", k=8)

        def emit_A2(g):
            db = g % 2
            for bi, (c0, L, tl, smp) in enumerate(blocks):
                ps, kp = psb()
                MM(ps[:, 0:L], PW[:, g * 128:(g + 1) * 128], DD(db, c0, c0 + L), True, True,
                   [kpw, ("DD", db, bi), REG], [kp])
                TS("dve", YP(g, c0, c0 + L), ps[:, 0:L], PSC(g), None, ALU.mult, None,
                   [kp, "CV", REG], [("YP", g, bi)])

        def emit_A1(g):
            w = POOL_W[g]
            CP("pool", PEXT[:, 0:16], CARRY[:, g, :], [("CARRY", g)], [("PEXT", "h")])
            for bi, (c0, L, tl, smp) in enumerate(blocks):
                ps, kp = psb()
                for k in range(8):
                    MM(ps[:, 0:L], W0v[:, k, g * 128:(g + 1) * 128], HT(k, c0, c0 + L), k == 0, k == 7,
                       [kw0, REG] + [("HT", k, i) for i in tl], [kp])
                if smp:
                    ACT(PEXTS[:, g, :, 15:23], ps[:, 0:128].rearrange("p (q t) -> p q t", q=16), AF.Identity,
                        [kp], [("PEXTS", g)])
                else:
                    ACT(PEXT[:, 16 + c0:16 + c0 + L], ps[:, 0:L], AF.Identity, [kp], [("PEXT", bi)])
            db = g % 2
            pk = [("PEXT", "h"), ("PEXT", 0), ("PEXT", 1)]
            states = {}
            for bi, (c0, L, tl, smp) in enumerate(blocks):
                if not smp:
                    states[bi] = pool_adds(PEXT[:, c0:c0 + 528], pk, 528, w, False,
                                           add_eng=("pool" if bi == 0 else "dve"))
            for bi, (c0, L, tl, smp) in enumerate(blocks):
                if not smp:
                    pool_finish(states[bi], PEXT[:, c0:c0 + 528], pk, 16, 512, w,
                                DD(db, c0, c0 + L), [("DD", db, bi)], st == 0 and bi == 0)
            for bi, (c0, L, tl, smp) in enumerate(blocks):
                if smp:
                    stt_ = pool_adds(PEXTS[:, g, :, :], [("PEXTS", g)], 23, w, True)
                    pool_finish(stt_, PEXTS[:, g, :, :], [("PEXTS", g)], 15, 8, w,
                                DD(db, c0, c0 + L).rearrange("p (q t) -> p q t", q=16), [("DD", db, bi)], False)
            CP("pool", CARRY[:, g, :], PEXT[:, 1024:1040], [("PEXT", 1)], [("CARRY", g)])

        def emit_A3(h):
            for bi, (c0, L, tl, smp) in enumerate(blocks):
                ps, kp = psb()
                for k in range(8):
                    MM(ps[:, 0:L], W1v[:, k, h * 128:(h + 1) * 128], HT(k, c0, c0 + L), k == 0, k == 7,
                       [kw1, REG] + [("HT", k, i) for i in tl], [kp])
                ACT(U(h, c0, c0 + L), ps[:, 0:L], GELU, [kp, REG], [("U", h, i) for i in tl])

        emit_A1(0)
        flush_deferred()
        emit_A1(1)
        t_w1, W1, kw1 = ring_take("win1")
        W1v = W1[:].rearrange("p (k c) -> p k c", k=8)
        emit_A3(0)
        t_pw, PW, kpw = ring_take("poolw")
        emit_A2(0)
        emit_A1(2)
        emit_A3(1)
        emit_A2(1)
        emit_A1(3)
        if st == 1:
            DMA("sp", npp, CARRY[:].rearrange("p g t -> p (g t)"), "o_npp", [("CARRY", g) for g in range(4)], [])
            DMA("sp", nps, PEXTS[:].rearrange("p g q t -> p (g q t)"), "o_nps", [("PEXTS", g) for g in range(4)], [])
        ring_release(t_w0)
        emit_A3(2)
        emit_A2(2)
        emit_A3(3)
        ring_release(t_w1)

        stage(10 * st + 2)
        stage(10 * st + 3)
        t_w2, W2, kw2 = ring_take("win2")
        W2v = W2[:].rearrange("p (k c) -> p k c", k=8)
        def emit_A5(i):
            c0, c1 = col(i)
            smp = (i == 8)
            WS, kws = (WSTS, "WSTS") if smp else (WST, "WST")
            boff = 512 if smp else 0
            ps, kp = psb()
            for h in range(4):
                MM(ps[:, h * 128:(h + 1) * 128], VT(i, h * 128, (h + 1) * 128), WS[:, h * 128:(h + 1) * 128],
                   True, False, [("VT", i), kws, REG], [kp])
                MM(ps[:, h * 128:(h + 1) * 128], ONES1[:], SGB[:, boff + h * 128: boff + (h + 1) * 128],
                   False, True, ["ONES1", "SGB"], [kp])
            uk = [("U", h, i) for h in range(4)]
            TT("dve", Uall[:, :, c0:c1], Uall[:, :, c0:c1], ps[:, :].rearrange("p (h t) -> p h t", h=4), ALU.mult,
               [kp, REG] + uk, uk)

        a4 = {}

        def a4_front(i):
            c0, c1 = col(i)
            ps, kp = psb()
            for k in range(8):
                MM(ps[:, :], HT(k, c0, c1), W2v[:, k, :], k == 0, k == 7, [kw2, REG, ("HT", k, i)], [kp])
            vg, kvg = tf()
            ACT(vg[:, 0:512], ps[:, :], GELU, [kp], [kvg])
            (ss, kss), (ms, kms), (rs, krs) = stat3()
            jb, kj = tb()
            P.add("dve", lambda e, jb=jb, vg=vg, ss=ss: e.scalar_tensor_tensor(
                out=jb[:], in0=vg[:, 0:512], scalar=1.0, in1=vg[:, 0:512],
                op0=ALU.mult, op1=ALU.mult, accum_out=ss), [kvg], [kj, kss])
            TS("pool", ms, ss, 1.0 / 512, EPS, ALU.mult, ALU.add, [kss], [kms])
            TT("pool", rs, ms, NEGH[:], ALU.pow, [kms, "NEGH"], [krs])
            a4[i] = (vg, kvg, rs, krs)

        def a4_back(i):
            vg, kvg, rs, krs = a4.pop(i)
            if i == 8:
                STT(vg[:, 0:512], vg[:, 0:512], rs, GS[:], ALU.mult, ALU.mult, [kvg, krs, "GS"], [kvg])
                DMA("sp", nv, vg[:, 0:512], "o_nv", [kvg], [])
                CP("dve", VT(i, 0, 512), vg[:, 0:512], [kvg, REG], [("VT", i)])
            else:
                STT(VT(i, 0, 512), vg[:, 0:512], rs, GS[:], ALU.mult, ALU.mult, [kvg, krs, "GS", REG], [("VT", i)])

        for it in range(ntile + 3):
            if it == 2:
                emit_A2(3)
                ring_release(t_pw)
            if it < ntile:
                a4_front(it)
            if 0 <= it - 1 < ntile:
                a4_back(it - 1)
            if 0 <= it - 3 < ntile:
                emit_A5(it - 3)
        ring_release(t_w2)

        stage(10 * st + 4)
        stage(10 * st + 5)
        for cg in range(2):
            t_a, WPS, kwpo = ring_take(f"pso{cg}")
            kwso = kwpo
            t_c, WGA, kwga = ring_take(f"wga{cg}")
            t_d, WGB, kwgb = ring_take(f"wgb{cg}")
            WPOv = WPS[:, 0:2048].rearrange("p (k c) -> p k c", k=4)
            WSOv = WPS[:, 2048:4096].rearrange("p (k c) -> p k c", k=4)
            WGAv = WGA[:].rearrange("p (k c) -> p k c", k=8)
            WGBv = WGB[:].rearrange("p (k c) -> p k c", k=8)
            for c4 in range(4):
                c = cg * 4 + c4
                lo, hi = c4 * 128, (c4 + 1) * 128
                for bi, (c0, L, tl, smp) in enumerate(blocks):
                    psA, kA = psb()
                    psB, kB = psb()
                    psGa, kGa = psb()
                    psGb, kGb = psb()
                    for k in range(4):
                        MM(psA[:, 0:L], WPOv[:, k, lo:hi], YP(k, c0, c0 + L), k == 0, k == 3,
                           [kwpo, ("YP", k, bi), REG], [kA])
                    for k in range(4):
                        MM(psB[:, 0:L], WSOv[:, k, lo:hi], U(k, c0, c0 + L), k == 0, k == 3,
                           [kwso, REG] + [("U", k, i) for i in tl], [kB])
                    for k in range(8):
                        MM(psGa[:, 0:L], WGAv[:, k, lo:hi], HT(k, c0, c0 + L), k == 0, k == 7,
                           [kwga, REG] + [("HT", k, i) for i in tl], [kGa])
                    for k in range(8):
                        MM(psGb[:, 0:L], WGBv[:, k, lo:hi], HT(k, c0, c0 + L), k == 0, k == 7,
                           [kwgb, REG] + [("HT", k, i) for i in tl], [kGb])
                    sa, ksa = tb()
                    sb_, ksb = tb()
                    ACT(sa[:, 0:L], psGa[:, 0:L], AF.Sigmoid, [kGa], [ksa])
                    ACT(sb_[:, 0:L], psGb[:, 0:L], AF.Sigmoid, [kGb], [ksb])
                    t1, kt1 = tf()
                    t2, kt2 = tf()
                    TT("dve", t1[:, 0:L], sa[:, 0:L], psA[:, 0:L], ALU.mult, [ksa, kA], [kt1])
                    TT("dve", t2[:, 0:L], sb_[:, 0:L], psB[:, 0:L], ALU.mult, [ksb, kB], [kt2])
                    TT("pool", MH[:, c, c0:c0 + L], t1[:, 0:L], t2[:, 0:L], ALU.add, [kt1, kt2],
                       [("MH", c, i) for i in tl])
            ring_release(t_a)
            ring_release(t_c)
            ring_release(t_d)

        stage(10 * st + 6)
        t_o0, WO0, kwo0 = ring_take("wo0")
        t_o1, WO1, kwo1 = ring_take("wo1")
        WOv = [WO0[:].rearrange("p (k c) -> p k c", k=8), WO1[:].rearrange("p (k c) -> p k c", k=8)]
        kwo = [kwo0, kwo1]
        nst = {}
        for it in range(ntile + 2):
            if it < ntile:
                i = it
                c0, c1 = col(i)
                for half in range(2):
                    ps, kp = psb()
                    for k in range(8):
                        MM(ps[:, :], MH[:, k, c0:c1], WOv[half][:, k, :], k == 0, k == 7,
                           [kwo[half], ("MH", k, i)], [kp])
                    xh = X[:, i, half * 512:(half + 1) * 512]
                    TT("dve", xh, xh, ps[:, :], ALU.add, [kp, ("X", i, half)], [("X", i, half)])
            if it < ntile:
                nst[it] = norm_stats(it)
            if 0 <= it - 1 < ntile:
                norm_scale(it - 1, *nst[it - 1], G2B, "G2B")
            if 0 <= it - 2 < ntile:
                pi = it - 2
                c0, c1 = col(pi)
                pxb, pkx = nst.pop(pi)[:2]
                norm_pe("act" if pi % 2 == 0 else "dve", pxb, pkx, MH[:, :, c0:c1],
                        [("MH", k, pi) for k in range(8)], [])
        ring_release(t_o0)
        ring_release(t_o1)

        stage(10 * st + 7)
        stage(10 * st + 8)
        DMA("sp", SCR2[:, 16:32], cvec[:, 16:32], "bar", (), [REG])
        b2_prev = [None]

        def b2_back(cv, gsrc, fout, kc1, kg, fkeys):
            ACT(cv, cv, GELU, [kc1], [kc1])
            TT("dve", fout, cv, gsrc, ALU.mult, [kc1, kg, REG], fkeys)

        for jg in range(6):
            ncol = 512 if jg < 5 else 256
            t_u, WU, kwu = ring_take(f"wup{jg}")
            t_g, WG, kwg = ring_take(f"wgt{jg}")
            WUv = WU[:, 0:8 * ncol].rearrange("p (k c) -> p k c", k=8)
            WGv = WG[:, 0:8 * ncol].rearrange("p (k c) -> p k c", k=8)
            for j4 in range(ncol // 128):
                j = jg * 4 + j4
                lo, hi = j4 * 128, (j4 + 1) * 128
                for bi, (c0, L, tl, smp) in enumerate(blocks):
                    psa, ka = psb()
                    psg, kg = psb()
                    for k in range(8):
                        MM(psa[:, 0:L], WUv[:, k, lo:hi], MH[:, k, c0:c0 + L], k == 0, k == 7,
                           [kwu] + [("MH", k, i) for i in tl], [ka])
                    for k in range(8):
                        MM(psg[:, 0:L], WGv[:, k, lo:hi], MH[:, k, c0:c0 + L], k == 0, k == 7,
                           [kwg] + [("MH", k, i) for i in tl], [kg])
                    ae, kae = tf()
                    c1b, kc1 = tf()
                    if smp:
                        ae3 = ae[:, 0:160].rearrange("p (q t) -> p q t", q=16)
                        ACT(ae3[:, :, 2:10], psa[:, 0:128].rearrange("p (q t) -> p q t", q=16), AF.Identity,
                            [ka], [kae])
                        CP("pool", ae3[:, :, 0:2], CS[:, j, :, :], [("CS", j), kae], [kae])
                        CP("pool", CS[:, j, :, :], ae3[:, :, 8:10], [kae], [("CS", j)])
                        a0, a1, a2 = ae3[:, :, 0:8], ae3[:, :, 1:9], ae3[:, :, 2:10]
                        cv = c1b[:, 0:128].rearrange("p (q t) -> p q t", q=16)
                        gsrc = psg[:, 0:128].rearrange("p (q t) -> p q t", q=16)
                        fout = FF(j, c0, c0 + L).rearrange("p (q t) -> p q t", q=16)
                    else:
                        ACT(ae[:, 2:2 + L], psa[:, 0:L], AF.Identity, [ka], [kae])
                        CP("pool", ae[:, 0:2], HALO[:, j, :], [("HALO", j), kae], [kae])
                        CP("pool", HALO[:, j, :], ae[:, L:L + 2], [kae], [("HALO", j)])
                        a0, a1, a2 = ae[:, 0:L], ae[:, 1:L + 1], ae[:, 2:L + 2]
                        cv = c1b[:, 0:L]
                        gsrc = psg[:, 0:L]
                        fout = FF(j, c0, c0 + L)
                    TS("pool", cv, a0, CW(0, j), CB(j), ALU.mult, ALU.add, [kae, "CV"], [kc1])
                    STT(cv, a1, CW(1, j), cv, ALU.mult, ALU.add, [kae, kc1, "CV"], [kc1])
                    STT(cv, a2, CW(2, j), cv, ALU.mult, ALU.add, [kae, kc1, "CV"], [kc1])
                    if b2_prev[0] is not None:
                        b2_back(*b2_prev[0])
                    b2_prev[0] = (cv, gsrc, fout, kc1, kg, [("F", j, i) for i in tl])
            ring_release(t_u)
            ring_release(t_g)

        if b2_prev[0] is not None:
            b2_back(*b2_prev[0])
            b2_prev[0] = None
        if st == 1:
            DMA("sp", ncp, HALO[:].rearrange("p j r -> p (j r)"), "o_ncp", [("HALO", j) for j in range(NFF)], [])
            DMA("sp", ncs, CS[:].rearrange("p j q r -> p (j q r)"), "o_ncs", [("CS", j) for j in range(NFF)], [])
        stage(10 * st + 9)
        for half in range(2):
            wd = [ring_take(f"wdn{half}_{kg}") for kg in range(3)]
            for i in tiles:
                c0, c1 = col(i)
                ps, kp = psb()
                for k in range(NFF):
                    t_, Wd, kwd = wd[k // 8]
                    nk = 8 if k // 8 < 2 else NFF - 16
                    Wdv = Wd[:, 0:nk * 512].rearrange("p (k c) -> p k c", c=512)
                    MM(ps[:, :], FF(k, c0, c1), Wdv[:, k % 8, :], k == 0, k == NFF - 1,
                       [kwd, ("F", k, i), REG], [kp])
                xh = X[:, i, half * 512:(half + 1) * 512]
                TT("dve", xh, xh, ps[:, :], ALU.add, [kp, ("X", i, half)], [("X", i, half)])
                if half == 1:
                    xk = [("X", i, 0), ("X", i, 1)]
                    (ss, kss), (ms, kms), (rs, krs) = stat3()
                    jt, kjt = tf()
                    ACT(jt[:].bitcast(BF16)[:, 0:D], X[:, i, :], AF.Square, xk, [kjt, kss], accum_out=ss)
                    TS("pool", ms, ss, 1.0 / D, EPS, ALU.mult, ALU.add, [kss], [kms])
                    TT("pool", rs, ms, NEGH[:], ALU.pow, [kms, "NEGH"], [krs])
                    dst = ys if i == 8 else yp[st * 1024 + 128 * i: st * 1024 + 128 * i + 128, :]
                    if st == 0 and i >= 6:
                        for hh in range(2):
                            to, kto = tf()
                            STT(to[:, 0:512], X[:, i, hh * 512:(hh + 1) * 512], rs, GF[:, hh * 512:(hh + 1) * 512],
                                ALU.mult, ALU.mult, xk + [krs, "GF"], [kto])
                            DMA("sp", dst[:, hh * 512:(hh + 1) * 512], to[:, 0:512], f"y{i}_{hh}", [kto], [])
                    else:
                        STT(X[:, i, :], X[:, i, :], rs, GF[:], ALU.mult, ALU.mult, xk + [krs, "GF"], xk)
                        DMA("sp", dst, X[:, i, :], f"y{i}", xk, [])
                    if st == 0:
                        if 1 <= i <= 5:
                            reload_x(i - 1)
                        elif i == 6:
                            reload_x(5)
                            reload_x(6)
                        elif i == 7:
                            reload_x(7)
                        nxt = order1[s0["s"]] if s0["s"] < len(order1) else None
                        if nxt is None or nxt == 8 or nxt <= i - 1:
                            s0_step()
            for t_, _, _ in wd:
                ring_release(t_, defer=(half == 1))

    except _Stop:
        pass
    assert STOP_AT is not None or ring_state["cursor"] == len(all_tiles)
    P.finalize()
    P.sbuf_left = nc.sbuf_bytes_remaining
    P.close()
    return nc, P


def _tile_w(W, nk, tw):
    N = W.shape[1]
    Wp = W.reshape(nk, 128, N).transpose(1, 0, 2)
    parts = []
    for c0 in range(0, N, tw):
        w = min(tw, N - c0)
        parts.append(Wp[:, :, c0:c0 + w].reshape(128, nk * w))
    return np.ascontiguousarray(np.concatenate(parts, axis=1), dtype=np.float32)


_CACHE = {}


def kernel(**inputs):
    f = lambda k: np.asarray(inputs[k], dtype=np.float32)
    x_prompt, x_sample = f("x_prompt"), f("x_sample")
    state_pool, state_conv = f("state_pool"), f("state_ffn_conv")
    w_in = f("w_in")[0]
    shared = {}
    shared["win"] = _tile_w(w_in, 8, 512)
    wpo_t = _tile_w(f("w_pool_out")[0], 4, 512).reshape(128, 2, 2048)
    wso_t = _tile_w(f("w_sgu_out")[0], 4, 512).reshape(128, 2, 2048)
    shared["wpso"] = np.ascontiguousarray(np.concatenate([wpo_t, wso_t], axis=2).reshape(128, 2 * 4096))
    shared["poolw"] = np.ascontiguousarray(f("pool_w")[0].transpose(1, 0, 2).reshape(128, 512))
    shared["wo"] = _tile_w(f("w_o")[0], 8, 512)
    shared["wup"] = _tile_w(f("ffn_w_up")[0], 8, 512)
    shared["wgt"] = _tile_w(f("ffn_w_gate")[0], 8, 512)
    shared["wdn"] = _tile_w(f("ffn_w_down")[0], NFF, 512)
    cvec = np.zeros((128, 108), np.float32)
    cvec[:, 0:8] = f("norm1_g")[0].reshape(8, 128).T
    cvec[:, 8:16] = f("norm2_g")[0].reshape(8, 128).T
    cvec[:, 16:20] = f("pool_scale")[0].reshape(4, 128).T
    cvec[:, 20:86] = f("ffn_conv_w")[0].reshape(3, NFF, 128).transpose(2, 0, 1).reshape(128, 66)
    cvec[:, 86:108] = f("ffn_conv_b")[0].reshape(NFF, 128).T
    shared["cvec"] = cvec
    shared["gsgu"] = np.ascontiguousarray(np.broadcast_to(f("sgu_norm_g")[0][None, :], (128, 512)))
    shared["gfin"] = np.ascontiguousarray(np.broadcast_to(f("final_norm_g")[None, :], (128, D)))
    shared["g1bc"] = np.ascontiguousarray(np.broadcast_to(f("norm1_g")[0][None, :], (128, D)))
    shared["g2bc"] = np.ascontiguousarray(np.broadcast_to(f("norm2_g")[0][None, :], (128, D)))
    sgu_b = f("sgu_b")[0]
    sgub = np.zeros((1, 1024), np.float32)
    sgub[0, 0:512] = sgu_b.reshape(512)
    sgub[0, 512:1024] = np.tile(sgu_b[:, :8], (1, 16)).reshape(512)
    shared["sgub"] = np.ascontiguousarray(np.broadcast_to(sgub, (128, 1024)))
    sgu_w = f("sgu_w")[0]
    shared["wst"] = np.ascontiguousarray(sgu_w.transpose(2, 0, 1).reshape(128, 512))
    wsts = np.zeros((128, 4, 128), np.float32)
    blk = sgu_w[:, :8, :8].transpose(2, 0, 1)
    for q in range(16):
        wsts[8 * q:8 * q + 8, :, 8 * q:8 * q + 8] = blk
    shared["wsts"] = wsts.reshape(128, 512)

    in_maps = []
    for c in range(8):
        m = dict(shared)
        m["xp"] = np.ascontiguousarray(x_prompt[c])
        m["xs"] = np.ascontiguousarray(x_sample[16 * c:16 * c + 16].reshape(128, D))
        sp = state_pool[0, 16 * c:16 * c + 16]
        spf = np.zeros((128, 4, 16, 23), np.float32)
        spf[:, :, :, 0:15] = sp.reshape(16, 15, 4, 128).transpose(3, 2, 0, 1)
        m["spfm"] = spf.reshape(128, 4 * 16 * 23)
        sc = state_conv[0, 16 * c:16 * c + 16]
        m["scfm"] = np.ascontiguousarray(sc.reshape(16, 2, NFF, 128).transpose(3, 2, 0, 1).reshape(128, NFF * 32))
        in_maps.append(m)

    if "nc" not in _CACHE:
        _CACHE["nc"] = build_program()[0]
    nc = _CACHE["nc"]
    res = run_bass_kernel_spmd(nc, in_maps, core_ids=list(range(8)))
    rs = res.results

    y_prompt = np.zeros((8, 2048, D), np.float32)
    y_sample = np.zeros((128, 8, D), np.float32)
    new_pool_prompt = np.zeros((1, 8, 15, 512), np.float32)
    new_pool_sample = np.zeros((1, 128, 15, 512), np.float32)
    new_conv_prompt = np.zeros((1, 8, 2, DFF), np.float32)
    new_conv_sample = np.zeros((1, 128, 2, DFF), np.float32)
    new_v = np.zeros((1, 128, 8, 512), np.float32)
    for c in range(8):
        r = rs[c]
        sl = slice(16 * c, 16 * c + 16)
        y_prompt[c] = r["yp"]
        y_sample[sl] = r["ys"].reshape(16, 8, D)
        new_pool_prompt[0, c] = r["npp"].reshape(128, 4, 16)[:, :, 1:16].transpose(2, 1, 0).reshape(15, 512)
        new_pool_sample[0, sl] = r["nps"].reshape(128, 4, 16, 23)[:, :, :, 8:23].transpose(2, 3, 1, 0).reshape(16, 15, 512)
        new_conv_prompt[0, c] = r["ncp"].reshape(128, NFF, 2).transpose(2, 1, 0).reshape(2, DFF)
        new_conv_sample[0, sl] = r["ncs"].reshape(128, NFF, 16, 2).transpose(2, 3, 1, 0).reshape(16, 2, DFF)
        new_v[0, sl] = r["nv"].reshape(16, 8, 512)
    return (y_prompt, y_sample, new_pool_prompt, new_pool_sample, new_conv_prompt, new_conv_sample, new_v)
```

```python
import contextlib
import numpy as np
import concourse.bass as bass
import concourse.mybir as mybir
from concourse.bass_utils import run_bass_kernel_spmd

F32 = mybir.dt.float32
BF16 = mybir.dt.bfloat16
AF = mybir.ActivationFunctionType
ALU = mybir.AluOpType

D = 1024
DFF = 2816
NFF = 22
EPS = 1e-6
POOL_W = (2, 4, 8, 16)
NSLOT = 6
STOP_AT = None
SKIP = set()


class _Stop(Exception):
    pass
TW = 1152


class _Op:
    __slots__ = ("id", "eng", "emit", "reads", "writes", "dma", "sem", "target",
                 "deps", "ticket", "signal", "waits")


class Prog:
    ENGS = ("pe", "act", "dve", "pool", "sp")

    def __init__(self, nc):
        self.nc = nc
        self.stack = contextlib.ExitStack()
        self.ops = []
        self.last_writer = {}
        self.readers = {}
        self.cnt_sem = {e: self.stack.enter_context(nc.semaphore("c_" + e)) for e in self.ENGS}
        self.dma_count = {}
        self.dma_sems = {}

    def sbuf(self, name, shape, dt):
        return self.stack.enter_context(self.nc.sbuf_tensor(name, list(shape), dt))

    def psum(self, name, shape, dt):
        return self.stack.enter_context(self.nc.psum_tensor(name, list(shape), dt))

    def add(self, eng, emit, reads=(), writes=(), dma_sem=None):
        op = _Op()
        op.id = len(self.ops)
        op.eng = eng
        op.emit = emit
        op.dma = dma_sem is not None
        op.sem = None
        op.target = None
        op.signal = False
        op.ticket = None
        if op.dma:
            if dma_sem not in self.dma_sems:
                self.dma_sems[dma_sem] = self.stack.enter_context(self.nc.semaphore("d_" + dma_sem))
                self.dma_count[dma_sem] = 0
            self.dma_count[dma_sem] += 16
            op.sem = dma_sem
            op.target = self.dma_count[dma_sem]
        raw = set()
        other = set()
        for k in reads:
            w = self.last_writer.get(k)
            if w is not None:
                raw.add(w)
        for k in writes:
            w = self.last_writer.get(k)
            if w is not None:
                other.add(w)
            rd = self.readers.get(k)
            if rd is not None:
                other.update(rd[0].values())
                other.update(rd[1])
        best = {}
        dmas = set()
        for pid in raw | other:
            p = self.ops[pid]
            if p.dma:
                dmas.add(pid)
                continue
            if (not op.dma) and p.eng == eng:
                if eng == "pe" or (pid not in raw and eng != "pool"):
                    continue
            if best.get(p.eng, -1) < pid:
                best[p.eng] = pid
        op.deps = list(best.values()) + list(dmas)
        for pid in best.values():
            self.ops[pid].signal = True
        for k in reads:
            rd = self.readers.setdefault(k, ({}, []))
            if op.dma:
                rd[1].append(op.id)
            else:
                rd[0][eng] = op.id
        for k in writes:
            self.last_writer[k] = op.id
            self.readers[k] = ({}, [])
        self.ops.append(op)
        return op

    def finalize(self):
        ops = self.ops
        tick = {e: 0 for e in self.ENGS}
        for op in ops:
            if op.signal and not op.dma:
                tick[op.eng] += 1
                op.ticket = tick[op.eng]
        seen = {e: {} for e in self.ENGS}
        for op in ops:
            w = {}
            for pid in op.deps:
                p = ops[pid]
                if p.dma:
                    key, val = ("d", p.sem), p.target
                else:
                    key, val = ("c", p.eng), p.ticket
                if seen[op.eng].get(key, 0) >= val:
                    continue
                if w.get(key, 0) < val:
                    w[key] = val
            for key, val in w.items():
                seen[op.eng][key] = val
            op.waits = list(w.items())
        final_waits = [(("d", n), v) for n, v in self.dma_count.items() if v > 0]
        streams = {e: [op for op in ops if op.eng == e] for e in self.ENGS}

        def sem_of(key):
            kind, n = key
            return self.dma_sems[n] if kind == "d" else self.cnt_sem[n]

        def run(ename, e):
            for op in streams[ename]:
                for key, val in op.waits:
                    e.wait_ge(sem_of(key), val)
                inst = op.emit(e)
                if op.dma:
                    inst.then_inc(self.dma_sems[op.sem], 16)
                elif op.signal:
                    inst.then_inc(self.cnt_sem[ename], 1)
            if ename == "sp":
                for key, val in final_waits:
                    e.wait_ge(sem_of(key), val)

        with self.nc.Block() as block:
            @block.tensor
            def _(e):
                run("pe", e)

            @block.scalar
            def _(e):
                run("act", e)

            @block.vector
            def _(e):
                run("dve", e)

            @block.gpsimd
            def _(e):
                run("pool", e)

            @block.sync
            def _(e):
                run("sp", e)
        self.stats = {e: len(streams[e]) for e in self.ENGS}
        self.stats["waits"] = sum(len(op.waits) for op in ops)
        self.stats["signals"] = sum(1 for op in ops if op.signal)

    def close(self):
        self.stack.close()


def build_program():
    nc = bass.Bass("TRN2", target_bir_lowering=False)
    P = Prog(nc)

    def din(name, shape):
        return nc.dram_tensor(name, list(shape), F32, kind="ExternalInput").ap()

    def dout(name, shape):
        return nc.dram_tensor(name, list(shape), F32, kind="ExternalOutput").ap()

    xp = din("xp", [2048, D])
    xs = din("xs", [128, D])
    spfm = din("spfm", [128, 4 * 16 * 23])
    scfm = din("scfm", [128, NFF * 32])
    win = din("win", [128, 7 * 4096])
    wpso = din("wpso", [128, 2 * 4096])
    poolw = din("poolw", [128, 512])
    wo = din("wo", [128, 2 * 4096])
    wup = din("wup", [128, 8 * DFF])
    wgt = din("wgt", [128, 8 * DFF])
    wdn = din("wdn", [128, 2 * NFF * 512])
    cvec = din("cvec", [128, 108])
    gsgu = din("gsgu", [128, 512])
    gfin = din("gfin", [128, D])
    g1bc = din("g1bc", [128, D])
    g2bc = din("g2bc", [128, D])
    sgub = din("sgub", [128, 1024])
    wst_d = din("wst", [128, 512])
    wsts_d = din("wsts", [128, 512])

    yp = dout("yp", [2048, D])
    ys = dout("ys", [128, D])
    npp = dout("npp", [128, 4 * 16])
    nps = dout("nps", [128, 4 * 16 * 23])
    ncp = dout("ncp", [128, NFF * 2])
    ncs = dout("ncs", [128, NFF * 32])
    nv = dout("nv", [128, 512])

    X = P.sbuf("X", [128, 9, D], F32)
    R = P.sbuf("R", [128, NFF * TW], BF16)
    MH = P.sbuf("MH", [128, 8, TW], BF16)
    PEXT = P.sbuf("PEXT", [128, 1040], F32)
    PEXTS = P.sbuf("PEXTS", [128, 4, 16, 23], F32)
    RING = [P.sbuf(f"RING{i}", [128, 4096], BF16) for i in range(NSLOT)]
    TF = [P.sbuf(f"TF{i}", [128, 528], F32) for i in range(6)]
    TB = [P.sbuf(f"TB{i}", [128, 512], BF16) for i in range(4)]
    XN = [P.sbuf(f"XN{i}", [128, D], BF16) for i in range(3)]
    G1B = P.sbuf("G1B", [128, D], F32)
    G2B = P.sbuf("G2B", [128, D], F32)
    CV = P.sbuf("CV", [128, 108], F32)
    GS = P.sbuf("GS", [128, 512], F32)
    GF = P.sbuf("GF", [128, D], F32)
    SGB = P.sbuf("SGB", [128, 1024], BF16)
    WST = P.sbuf("WST", [128, 512], BF16)
    WSTS = P.sbuf("WSTS", [128, 512], BF16)
    IDENT = P.sbuf("IDENT", [128, 128], BF16)
    ONES1 = P.sbuf("ONES1", [128, 128], BF16)
    NEGH = P.sbuf("NEGH", [128, 1], F32)
    INVC = P.sbuf("INVC", [128, 16], F32)
    CARRY = P.sbuf("CARRY", [128, 4, 16], F32)
    HALO = P.sbuf("HALO", [128, NFF, 2], F32)
    CS = P.sbuf("CS", [128, NFF, 16, 2], F32)
    STAT = P.sbuf("STAT", [128, 216], F32)
    TMP16 = P.sbuf("TMP16", [128, 16], F32)
    SCR = P.sbuf("SCR", [128, 4], F32)
    SCR2 = P.sbuf("SCR2", [128, 32], F32)
    PS = [P.psum(f"ps{i}", [128, 512], F32) for i in range(8)]

    O_HT, O_YP, O_U, O_VT, O_DD = 0, 9216, 13824, 18432, 23040

    def HT(k, c0, c1):
        return R[:, O_HT + k * TW + c0: O_HT + k * TW + c1]

    def YP(g, c0, c1):
        return R[:, O_YP + g * TW + c0: O_YP + g * TW + c1]

    def U(h, c0, c1):
        return R[:, O_U + h * TW + c0: O_U + h * TW + c1]

    def VT(i, f0, f1):
        return R[:, O_VT + i * 512 + f0: O_VT + i * 512 + f1]

    def DD(b, c0, c1):
        return R[:, O_DD + b * TW + c0: O_DD + b * TW + c1]

    def FF(j, c0, c1):
        return R[:, j * TW + c0: j * TW + c1]

    REG = "REG"

    def MM(out, lhsT, rhs, start, stop, reads, writes):
        P.add("pe", lambda e: e.matmul(out, lhsT=lhsT, rhs=rhs, start=start, stop=stop), reads, writes)

    def TR(out, in_, reads, writes):
        P.add("pe", lambda e: e.transpose(out, in_, IDENT[:]), list(reads) + ["IDENT"], writes)

    def ACT(out, in_, func, reads, writes, **kw):
        P.add("act", lambda e: e.activation(out=out, in_=in_, func=func, **kw), reads, writes)

    def TT(eng, out, in0, in1, op, reads, writes):
        P.add(eng, lambda e: e.tensor_tensor(out=out, in0=in0, in1=in1, op=op), reads, writes)

    def TS(eng, out, in0, s1, s2, op0, op1, reads, writes):
        if s2 is None:
            P.add(eng, lambda e: e.tensor_scalar(out=out, in0=in0, scalar1=s1, scalar2=None, op0=op0),
                  reads, writes)
        else:
            P.add(eng, lambda e: e.tensor_scalar(out=out, in0=in0, scalar1=s1, scalar2=s2, op0=op0, op1=op1),
                  reads, writes)

    def STT(out, in0, scalar, in1, op0, op1, reads, writes):
        P.add("dve", lambda e: e.scalar_tensor_tensor(out=out, in0=in0, scalar=scalar, in1=in1, op0=op0, op1=op1),
              reads, writes)

    def CP(eng, out, in_, reads, writes):
        P.add(eng, lambda e: e.tensor_copy(out=out, in_=in_), reads, writes)

    def MSET(eng, ap, val, writes):
        P.add(eng, lambda e: e.memset(ap, val), (), writes)

    def DMA(eng, out, in_, sem, reads, writes):
        P.add(eng, lambda e: e.dma_start(out=out, in_=in_), reads, writes, dma_sem=sem)

    cnt = {"ps": 0, "tf": 0, "tb": 0, "xn": 0, "stat": 0}

    def psb():
        b = cnt["ps"] % 8
        cnt["ps"] += 1
        return PS[b], ("ps", b)

    def tf():
        b = cnt["tf"] % len(TF)
        cnt["tf"] += 1
        return TF[b], ("TF", b)

    def tb():
        b = cnt["tb"] % len(TB)
        cnt["tb"] += 1
        return TB[b], ("TB", b)

    def xn():
        b = cnt["xn"] % 3
        cnt["xn"] += 1
        return XN[b], ("XN", b)

    def stat3():
        c = cnt["stat"]
        cnt["stat"] += 3
        assert c + 3 <= 216
        return [(STAT[:, c + i: c + i + 1], ("ST", c + i)) for i in range(3)]

    def tile_list():
        tl = []
        tl.append(("win0", win[:, 0:4096], 4096))
        tl.append(("win1", win[:, 4096:8192], 4096))
        tl.append(("poolw", poolw[:, 0:512], 512))
        tl.append(("win2", win[:, 8192:12288], 4096))
        for cg in range(2):
            tl.append((f"pso{cg}", wpso[:, cg * 4096:(cg + 1) * 4096], 4096))
            tl.append((f"wga{cg}", win[:, (3 + cg) * 4096:(4 + cg) * 4096], 4096))
            tl.append((f"wgb{cg}", win[:, (5 + cg) * 4096:(6 + cg) * 4096], 4096))
        tl.append(("wo0", wo[:, 0:4096], 4096))
        tl.append(("wo1", wo[:, 4096:8192], 4096))
        for jg in range(6):
            ncol = 512 if jg < 5 else 256
            tl.append((f"wup{jg}", wup[:, jg * 4096: jg * 4096 + 8 * ncol], 8 * ncol))
            tl.append((f"wgt{jg}", wgt[:, jg * 4096: jg * 4096 + 8 * ncol], 8 * ncol))
        for half in range(2):
            for kg in range(3):
                k0, k1 = 8 * kg, min(8 * kg + 8, NFF)
                tl.append((f"wdn{half}_{kg}",
                           wdn[:, (half * NFF + k0) * 512:(half * NFF + k1) * 512], (k1 - k0) * 512))
        return tl

    st_tiles = tile_list()
    NT_ST = len(st_tiles)
    all_tiles = st_tiles + st_tiles
    ring_state = {"next_load": 0, "cursor": 0}

    def ring_issue(t):
        if t >= len(all_tiles):
            return
        name, src, n = all_tiles[t]
        s = t % NSLOT
        DMA("pool", RING[s][:, 0:n], src, f"w{s}", (), [("w", s)])

    def ring_take(name):
        t = ring_state["cursor"]
        assert all_tiles[t][0] == name, (all_tiles[t][0], name)
        ring_state["cursor"] += 1
        s = t % NSLOT
        return t, RING[s], ("w", s)

    deferred = []

    def ring_release(t, defer=False):
        if defer:
            deferred.append(t + NSLOT)
        else:
            ring_issue(t + NSLOT)

    def flush_deferred():
        for t in deferred:
            ring_issue(t)
        deferred.clear()

    early_x0 = set()
    MSET("pool", NEGH[:], -0.5, ["NEGH"])
    t0, k0 = tf()
    MSET("pool", t0[:, 0:128], 1.0, [k0])
    P.add("pool", lambda e: e.affine_select(out=IDENT[:], in_=t0[:, 0:128], pattern=[[-1, 128]],
                                            compare_op=ALU.is_equal, fill=0.0, base=0, channel_multiplier=1),
          [k0], ["IDENT"])
    DMA("sp", X[:, 0, :], xp[0:128, :], "x0", (), [("X", 0, 0), ("X", 0, 1)])
    DMA("sp", G1B[:], g1bc, "c10", (), ["G1B"])
    early_x0.add(0)
    for i in range(1, 8):
        DMA("sp", X[:, i, :], xp[128 * i: 128 * i + 128, :], f"x{i}", (), [("X", i, 0), ("X", i, 1)])
        early_x0.add(i)
    ring_issue(0)
    for t in range(1, NSLOT):
        deferred.append(t)
    MSET("pool", ONES1[:], 1.0 / 128.0, ["ONES1"])
    MSET("pool", CARRY[:], 0.0, [("CARRY", g) for g in range(4)])
    MSET("pool", HALO[:], 0.0, [("HALO", j) for j in range(NFF)])
    MSET("pool", SCR[:], 0.0, ["SCR"])
    for t in range(15):
        MSET("pool", INVC[:, t:t + 1], 1.0 / (t + 1), ["INVC"])
    DMA("sp", CV[:], cvec, "c0", (), ["CV"])

    def late_setup():
        DMA("sp", GS[:], gsgu, "c1", (), ["GS"])
        DMA("sp", G2B[:], g2bc, "c11", (), ["G2B"])
        for src, dst, key, sem in ((wst_d, WST, "WST", "c3"), (wsts_d, WSTS, "WSTS", "c4")):
            t1, k1 = tf()
            DMA("sp", t1[:, 0:512], src, sem, (), [k1])
            P.add("pool", lambda e, t1=t1, dst=dst: e.affine_select(
                out=dst[:].rearrange("p (h t) -> p h t", h=4),
                in_=t1[:, 0:512].rearrange("p (h t) -> p h t", h=4),
                pattern=[[0, 4], [1, 128]], compare_op=ALU.is_ge, fill=0.0, base=0, channel_multiplier=-1),
                [k1], [key])
        for half in range(2):
            t1, k1 = tf()
            DMA("sp", t1[:, 0:512], sgub[:, half * 512:(half + 1) * 512], f"c{5 + half}", (), [k1])
            CP("dve", SGB[:, half * 512:(half + 1) * 512], t1[:, 0:512], [k1], ["SGB"])
        DMA("sp", GF[:], gfin, "c2", (), ["GF"])
        DMA("sp", PEXTS[:].rearrange("p g q t -> p (g q t)"), spfm, "c7", (), [("PEXTS", g) for g in range(4)])
        DMA("sp", CS[:].rearrange("p j q r -> p (j q r)"), scfm, "c8", (), [("CS", j) for j in range(NFF)])

    DMA("sp", X[:, 8, :], xs, "x8", (), [("X", 8, 0), ("X", 8, 1)])
    early_x = set()
    pending_reload = []

    def reload_x(i):
        DMA("sp", X[:, i, :], xp[1024 + 128 * i: 1024 + 128 * i + 128, :], f"x{i}", (),
            [("X", i, 0), ("X", i, 1)])
        early_x.add(i)

    G1 = lambda k: CV[:, k:k + 1]
    G2 = lambda k: CV[:, 8 + k:9 + k]
    PSC = lambda g: CV[:, 16 + g:17 + g]
    CW = lambda r, j: CV[:, 20 + r * NFF + j: 21 + r * NFF + j]
    CB = lambda j: CV[:, 86 + j: 87 + j]

    def norm_stats(i):
        xk = [("X", i, 0), ("X", i, 1)]
        (ss, kss), (ms, kms), (rs, krs) = stat3()
        xb, kx = xn()
        ACT(xb[:], X[:, i, :], AF.Square, xk, [kx, kss], accum_out=ss)
        TS("pool", ms, ss, 1.0 / D, EPS, ALU.mult, ALU.add, [kss], [kms])
        TT("pool", rs, ms, NEGH[:], ALU.pow, [kms, "NEGH"], [krs])
        return xb, kx, rs, krs

    def norm_scale(i, xb, kx, rs, krs, GBC, gkey):
        xk = [("X", i, 0), ("X", i, 1)]
        STT(xb[:], X[:, i, :], rs, GBC[:], ALU.mult, ALU.mult, xk + [krs, gkey], [kx])

    def norm_pe(eng, xb, kx, dst3d, dst_keys, extra):
        ps, kp = psb()
        psv = ps[:].bitcast(BF16)
        for k in range(8):
            TR(psv[:, k * 128:(k + 1) * 128], xb[:, k * 128:(k + 1) * 128], [kx], [kp])
        src = psv.rearrange("p (k t) -> p k t", k=8)
        if eng == "act":
            ACT(dst3d, src, AF.Identity, [kp] + extra, dst_keys)
        else:
            CP("dve", dst3d, src, [kp] + extra, dst_keys)

    s0 = {"s": 0, "nst": {}}
    order1 = [8, 0, 1, 2, 3, 4, 5, 6, 7]

    def s0_step():
        sidx = s0["s"]
        n = len(order1)
        if sidx < n:
            t = order1[sidx]
            s0["nst"][t] = norm_stats(t)
        if 0 <= sidx - 1 < n:
            t = order1[sidx - 1]
            norm_scale(t, *s0["nst"][t], G1B, "G1B")
        if 0 <= sidx - 2 < n:
            t = order1[sidx - 2]
            pxb, pkx = s0["nst"].pop(t)[:2]
            norm_pe("act" if sidx % 2 == 0 else "dve", pxb, pkx, MH[:, :, 128 * t:128 * t + 128],
                    [("MH", k, t) for k in range(8)], [])
        s0["s"] += 1

    def s0_done():
        return s0["s"] >= len(order1) + 2

    def pool_adds(ext, ek, Lx, w, three_d, add_eng="pool"):
        S = (lambda ap, a, b: ap[:, :, a:b]) if three_d else (lambda ap, a, b: ap[:, a:b])

        def view(t):
            if three_d:
                return t[:, 0:16 * Lx].rearrange("p (q t) -> p q t", q=16)
            return t[:, 0:Lx]
        cur, ck = ext, list(ek)
        have, vf = 1, 0
        while have < w:
            tbuf, tk = tf()
            nxt = view(tbuf)
            lo = vf + have
            TT(add_eng, S(nxt, lo, Lx), S(cur, lo, Lx), S(cur, lo - have, Lx - have), ALU.add, ck, [tk])
            vf = lo
            have *= 2
            cur, ck = nxt, [tk]
        return cur, ck, S

    def pool_finish(state, ext, ek, Hh, T, w, dd_out, dd_keys, fix_first):
        cur, ck, S = state
        STT(dd_out, S(cur, Hh, Hh + T), 1.0 / w, S(ext, Hh, Hh + T), ALU.mult, ALU.subtract,
            ck + list(ek) + [REG], dd_keys)
        if fix_first and w > 1:
            n = w - 1
            TT("dve", TMP16[:, 0:n], cur[:, Hh:Hh + n], INVC[:, 0:n], ALU.mult, ck + ["INVC"], ["TMP16"])
            TT("dve", dd_out[:, 0:n], TMP16[:, 0:n], ext[:, Hh:Hh + n], ALU.subtract,
               ["TMP16"] + list(ek) + [REG], dd_keys)

    GELU = AF.Gelu_apprx_tanh
    Uall = R[:, O_U:O_U + 4 * TW].rearrange("p (h t) -> p h t", h=4)

    P.marks = []

    def stage(n):
        P.marks.append((n, sum(1 for o in P.ops if o.eng == "pe")))
        if STOP_AT is not None and n >= STOP_AT:
            raise _Stop()

    try:
      for st in range(2):
        stage(10 * st + 0)
        ntile = 8 if st == 0 else 9
        tiles = list(range(ntile))
        blocks = [(0, 512, [0, 1, 2, 3], False), (512, 512, [4, 5, 6, 7], False)]
        if st == 1:
            blocks.append((1024, 128, [8], True))

        def col(i):
            return 128 * i, 128 * i + 128

        if st > 0:
            DMA("sp", SCR2[:, 0:16], cvec[:, 0:16], "bar", (), [REG])

        t_w0, W0, kw0 = ring_take("win0")
        W0v = W0[:].rearrange("p (k c) -> p k c", k=8)

        def emit_A2(g):
            db = g % 2
            for bi, (c0, L, tl, smp) in enumerate(blocks):
                ps, kp = psb()
                MM(ps[:, 0:L], PW[:, g * 128:(g + 1) * 128], DD(db, c0, c0 + L), True, True,
                   [kpw, ("DD", db, bi), REG], [kp])
                TS("dve", YP(g, c0, c0 + L), ps[:, 0:L], PSC(g), None, ALU.mult, None,
                   [kp, "CV", REG], [("YP", g, bi)])

        def emit_A1_mm(g, bis, first):
            if first:
                CP("pool", PEXT[:, 0:16], CARRY[:, g, :], [("CARRY", g)], [("PEXT", "h")])
            for bi, (c0, L, tl, smp) in enumerate(blocks):
                if bi not in bis:
                    continue
                ps, kp = psb()
                for k in range(8):
                    MM(ps[:, 0:L], W0v[:, k, g * 128:(g + 1) * 128], HT(k, c0, c0 + L), k == 0, k == 7,
                       [kw0, REG] + [("HT", k, i) for i in tl], [kp])
                if smp:
                    ACT(PEXTS[:, g, :, 15:23], ps[:, 0:128].rearrange("p (q t) -> p q t", q=16), AF.Identity,
                        [kp], [("PEXTS", g)])
                else:
                    ACT(PEXT[:, 16 + c0:16 + c0 + L], ps[:, 0:L], AF.Identity, [kp], [("PEXT", bi)])

        def emit_A1_rest(g):
            w = POOL_W[g]
            db = g % 2
            pk = [("PEXT", "h"), ("PEXT", 0), ("PEXT", 1)]
            states = {}
            for bi, (c0, L, tl, smp) in enumerate(blocks):
                if not smp:
                    states[bi] = pool_adds(PEXT[:, c0:c0 + 528], pk, 528, w, False,
                                           add_eng=("pool" if bi == 0 else "dve"))
            for bi, (c0, L, tl, smp) in enumerate(blocks):
                if not smp:
                    pool_finish(states[bi], PEXT[:, c0:c0 + 528], pk, 16, 512, w,
                                DD(db, c0, c0 + L), [("DD", db, bi)], st == 0 and bi == 0)
            for bi, (c0, L, tl, smp) in enumerate(blocks):
                if smp:
                    stt_ = pool_adds(PEXTS[:, g, :, :], [("PEXTS", g)], 23, w, True)
                    pool_finish(stt_, PEXTS[:, g, :, :], [("PEXTS", g)], 15, 8, w,
                                DD(db, c0, c0 + L).rearrange("p (q t) -> p q t", q=16), [("DD", db, bi)], False)
            CP("pool", CARRY[:, g, :], PEXT[:, 1024:1040], [("PEXT", 1)], [("CARRY", g)])

        def emit_A1(g):
            emit_A1_mm(g, range(len(blocks)), True)
            emit_A1_rest(g)

        def emit_A3(h):
            for bi, (c0, L, tl, smp) in enumerate(blocks):
                ps, kp = psb()
                for k in range(8):
                    MM(ps[:, 0:L], W1v[:, k, h * 128:(h + 1) * 128], HT(k, c0, c0 + L), k == 0, k == 7,
                       [kw1, REG] + [("HT", k, i) for i in tl], [kp])
                ACT(U(h, c0, c0 + L), ps[:, 0:L], GELU, [kp, REG], [("U", h, i) for i in tl])

        HT3 = R[:, O_HT:O_HT + 8 * TW].rearrange("p (k t) -> p k t", k=8)
        if st == 0:
            for i in tiles:
                if i == 8 or i in early_x0:
                    continue
                DMA("sp", X[:, i, :], xp[128 * i: 128 * i + 128, :], f"x{i}", (), [("X", i, 0), ("X", i, 1)])
            nst = {}
            for it in range(ntile + 2):
                if it < ntile:
                    nst[it] = norm_stats(it)
                if 0 <= it - 1 < ntile:
                    norm_scale(it - 1, *nst[it - 1], G1B, "G1B")
                if 0 <= it - 2 < ntile:
                    pi = it - 2
                    c0, c1 = col(pi)
                    pxb, pkx = nst.pop(pi)[:2]
                    norm_pe("act" if pi % 2 == 0 else "dve", pxb, pkx, HT3[:, :, c0:c1],
                            [("HT", k, pi) for k in range(8)], [REG])
        else:
            def copy_blk(bi):
                c0, L, tl, smp = blocks[bi]
                rk = [("MH", k, i) for k in range(8) for i in tl] + [REG]
                wk = [("HT", k, i) for k in range(8) for i in tl]
                if bi == 1:
                    ACT(HT3[:, :, c0:c0 + L], MH[:, :, c0:c0 + L], AF.Identity, rk, wk)
                else:
                    CP("dve", HT3[:, :, c0:c0 + L], MH[:, :, c0:c0 + L], rk, wk)

            copy_blk(0)
            copy_blk(2)
            emit_A1_mm(0, (0, 2), True)
            while not s0_done():
                s0_step()
            copy_blk(1)

        if st == 0:
            late_setup()
            for t in deferred[:2]:
                ring_issue(t)
            del deferred[:2]
        stage(10 * st + 1)
        if st == 0:
            emit_A1(0)
        else:
            emit_A1_mm(0, (1,), False)
            emit_A1_rest(0)
        flush_deferred()
        emit_A1(1)
        t_w1, W1, kw1 = ring_take("win1")
        W1v = W1[:].rearrange("p (k c) -> p k c", k=8)
        emit_A3(0)
        t_pw, PW, kpw = ring_take("poolw")
        emit_A2(0)
        emit_A1(2)
        emit_A3(1)
        emit_A2(1)
        emit_A1(3)
        if st == 1:
            DMA("sp", npp, CARRY[:].rearrange("p g t -> p (g t)"), "o_npp", [("CARRY", g) for g in range(4)], [])
            DMA("sp", nps, PEXTS[:].rearrange("p g q t -> p (g q t)"), "o_nps", [("PEXTS", g) for g in range(4)], [])
        ring_release(t_w0)
        emit_A3(2)
        emit_A2(2)
        emit_A3(3)
        ring_release(t_w1)

        stage(10 * st + 2)
        stage(10 * st + 3)
        t_w2, W2, kw2 = ring_take("win2")
        W2v = W2[:].rearrange("p (k c) -> p k c", k=8)
        def emit_A5(i):
            c0, c1 = col(i)
            smp = (i == 8)
            WS, kws = (WSTS, "WSTS") if smp else (WST, "WST")
            boff = 512 if smp else 0
            ps, kp = psb()
            for h in range(4):
                MM(ps[:, h * 128:(h + 1) * 128], VT(i, h * 128, (h + 1) * 128), WS[:, h * 128:(h + 1) * 128],
                   True, False, [("VT", i), kws, REG], [kp])
                MM(ps[:, h * 128:(h + 1) * 128], ONES1[:], SGB[:, boff + h * 128: boff + (h + 1) * 128],
                   False, True, ["ONES1", "SGB"], [kp])
            uk = [("U", h, i) for h in range(4)]
            TT("dve", Uall[:, :, c0:c1], Uall[:, :, c0:c1], ps[:, :].rearrange("p (h t) -> p h t", h=4), ALU.mult,
               [kp, REG] + uk, uk)

        a4 = {}

        def a4_front(i):
            c0, c1 = col(i)
            ps, kp = psb()
            for k in range(8):
                MM(ps[:, :], HT(k, c0, c1), W2v[:, k, :], k == 0, k == 7, [kw2, REG, ("HT", k, i)], [kp])
            vg, kvg = tf()
            ACT(vg[:, 0:512], ps[:, :], GELU, [kp], [kvg])
            (ss, kss), (ms, kms), (rs, krs) = stat3()
            jb, kj = tb()
            P.add("dve", lambda e, jb=jb, vg=vg, ss=ss: e.scalar_tensor_tensor(
                out=jb[:], in0=vg[:, 0:512], scalar=1.0, in1=vg[:, 0:512],
                op0=ALU.mult, op1=ALU.mult, accum_out=ss), [kvg], [kj, kss])
            TS("pool", ms, ss, 1.0 / 512, EPS, ALU.mult, ALU.add, [kss], [kms])
            TT("pool", rs, ms, NEGH[:], ALU.pow, [kms, "NEGH"], [krs])
            a4[i] = (vg, kvg, rs, krs)

        def a4_back(i):
            vg, kvg, rs, krs = a4.pop(i)
            if i == 8:
                STT(vg[:, 0:512], vg[:, 0:512], rs, GS[:], ALU.mult, ALU.mult, [kvg, krs, "GS"], [kvg])
                DMA("sp", nv, vg[:, 0:512], "o_nv", [kvg], [])
                CP("dve", VT(i, 0, 512), vg[:, 0:512], [kvg, REG], [("VT", i)])
            else:
                STT(VT(i, 0, 512), vg[:, 0:512], rs, GS[:], ALU.mult, ALU.mult, [kvg, krs, "GS", REG], [("VT", i)])

        for it in range(ntile + 3):
            if it == 2:
                emit_A2(3)
                ring_release(t_pw)
            if it < ntile:
                a4_front(it)
            if 0 <= it - 1 < ntile:
                a4_back(it - 1)
            if 0 <= it - 3 < ntile:
                emit_A5(it - 3)
        ring_release(t_w2)

        stage(10 * st + 4)
        stage(10 * st + 5)
        for cg in range(2):
            t_a, WPS, kwpo = ring_take(f"pso{cg}")
            kwso = kwpo
            t_c, WGA, kwga = ring_take(f"wga{cg}")
            t_d, WGB, kwgb = ring_take(f"wgb{cg}")
            WPOv = WPS[:, 0:2048].rearrange("p (k c) -> p k c", k=4)
            WSOv = WPS[:, 2048:4096].rearrange("p (k c) -> p k c", k=4)
            WGAv = WGA[:].rearrange("p (k c) -> p k c", k=8)
            WGBv = WGB[:].rearrange("p (k c) -> p k c", k=8)
            for c4 in range(4):
                c = cg * 4 + c4
                lo, hi = c4 * 128, (c4 + 1) * 128
                for bi, (c0, L, tl, smp) in enumerate(blocks):
                    psA, kA = psb()
                    psB, kB = psb()
                    psGa, kGa = psb()
                    psGb, kGb = psb()
                    for k in range(4):
                        MM(psA[:, 0:L], WPOv[:, k, lo:hi], YP(k, c0, c0 + L), k == 0, k == 3,
                           [kwpo, ("YP", k, bi), REG], [kA])
                    for k in range(4):
                        MM(psB[:, 0:L], WSOv[:, k, lo:hi], U(k, c0, c0 + L), k == 0, k == 3,
                           [kwso, REG] + [("U", k, i) for i in tl], [kB])
                    for k in range(8):
                        MM(psGa[:, 0:L], WGAv[:, k, lo:hi], HT(k, c0, c0 + L), k == 0, k == 7,
                           [kwga, REG] + [("HT", k, i) for i in tl], [kGa])
                    for k in range(8):
                        MM(psGb[:, 0:L], WGBv[:, k, lo:hi], HT(k, c0, c0 + L), k == 0, k == 7,
                           [kwgb, REG] + [("HT", k, i) for i in tl], [kGb])
                    sa, ksa = tb()
                    sb_, ksb = tb()
                    ACT(sa[:, 0:L], psGa[:, 0:L], AF.Sigmoid, [kGa], [ksa])
                    ACT(sb_[:, 0:L], psGb[:, 0:L], AF.Sigmoid, [kGb], [ksb])
                    t1, kt1 = tf()
                    t2, kt2 = tf()
                    TT("dve", t1[:, 0:L], sa[:, 0:L], psA[:, 0:L], ALU.mult, [ksa, kA], [kt1])
                    TT("dve", t2[:, 0:L], sb_[:, 0:L], psB[:, 0:L], ALU.mult, [ksb, kB], [kt2])
                    TT("pool", MH[:, c, c0:c0 + L], t1[:, 0:L], t2[:, 0:L], ALU.add, [kt1, kt2],
                       [("MH", c, i) for i in tl])
            ring_release(t_a)
            ring_release(t_c)
            ring_release(t_d)

        stage(10 * st + 6)
        t_o0, WO0, kwo0 = ring_take("wo0")
        t_o1, WO1, kwo1 = ring_take("wo1")
        WOv = [WO0[:].rearrange("p (k c) -> p k c", k=8), WO1[:].rearrange("p (k c) -> p k c", k=8)]
        kwo = [kwo0, kwo1]
        nst = {}
        for it in range(ntile + 2):
            if it < ntile:
                i = it
                c0, c1 = col(i)
                for half in range(2):
                    ps, kp = psb()
                    for k in range(8):
                        MM(ps[:, :], MH[:, k, c0:c1], WOv[half][:, k, :], k == 0, k == 7,
                           [kwo[half], ("MH", k, i)], [kp])
                    xh = X[:, i, half * 512:(half + 1) * 512]
                    TT("dve", xh, xh, ps[:, :], ALU.add, [kp, ("X", i, half)], [("X", i, half)])
            if it < ntile:
                nst[it] = norm_stats(it)
            if 0 <= it - 1 < ntile:
                norm_scale(it - 1, *nst[it - 1], G2B, "G2B")
            if 0 <= it - 2 < ntile:
                pi = it - 2
                c0, c1 = col(pi)
                pxb, pkx = nst.pop(pi)[:2]
                norm_pe("act" if pi % 2 == 0 else "dve", pxb, pkx, MH[:, :, c0:c1],
                        [("MH", k, pi) for k in range(8)], [])
        ring_release(t_o0)
        ring_release(t_o1)

        stage(10 * st + 7)
        stage(10 * st + 8)
        DMA("sp", SCR2[:, 16:32], cvec[:, 16:32], "bar", (), [REG])
        b2_prev = [None]

        def b2_back(cv, gsrc, fout, kc1, kg, fkeys):
            ACT(cv, cv, GELU, [kc1], [kc1])
            TT("dve", fout, cv, gsrc, ALU.mult, [kc1, kg, REG], fkeys)

        for jg in range(6):
            ncol = 512 if jg < 5 else 256
            t_u, WU, kwu = ring_take(f"wup{jg}")
            t_g, WG, kwg = ring_take(f"wgt{jg}")
            WUv = WU[:, 0:8 * ncol].rearrange("p (k c) -> p k c", k=8)
            WGv = WG[:, 0:8 * ncol].rearrange("p (k c) -> p k c", k=8)
            for j4 in range(ncol // 128):
                j = jg * 4 + j4
                lo, hi = j4 * 128, (j4 + 1) * 128
                for bi, (c0, L, tl, smp) in enumerate(blocks):
                    psa, ka = psb()
                    psg, kg = psb()
                    for k in range(8):
                        MM(psa[:, 0:L], WUv[:, k, lo:hi], MH[:, k, c0:c0 + L], k == 0, k == 7,
                           [kwu] + [("MH", k, i) for i in tl], [ka])
                    for k in range(8):
                        MM(psg[:, 0:L], WGv[:, k, lo:hi], MH[:, k, c0:c0 + L], k == 0, k == 7,
                           [kwg] + [("MH", k, i) for i in tl], [kg])
                    ae, kae = tf()
                    c1b, kc1 = tf()
                    if smp:
                        ae3 = ae[:, 0:160].rearrange("p (q t) -> p q t", q=16)
                        ACT(ae3[:, :, 2:10], psa[:, 0:128].rearrange("p (q t) -> p q t", q=16), AF.Identity,
                            [ka], [kae])
                        CP("pool", ae3[:, :, 0:2], CS[:, j, :, :], [("CS", j), kae], [kae])
                        CP("pool", CS[:, j, :, :], ae3[:, :, 8:10], [kae], [("CS", j)])
                        a0, a1, a2 = ae3[:, :, 0:8], ae3[:, :, 1:9], ae3[:, :, 2:10]
                        cv = c1b[:, 0:128].rearrange("p (q t) -> p q t", q=16)
                        gsrc = psg[:, 0:128].rearrange("p (q t) -> p q t", q=16)
                        fout = FF(j, c0, c0 + L).rearrange("p (q t) -> p q t", q=16)
                    else:
                        ACT(ae[:, 2:2 + L], psa[:, 0:L], AF.Identity, [ka], [kae])
                        CP("pool", ae[:, 0:2], HALO[:, j, :], [("HALO", j), kae], [kae])
                        CP("pool", HALO[:, j, :], ae[:, L:L + 2], [kae], [("HALO", j)])
                        a0, a1, a2 = ae[:, 0:L], ae[:, 1:L + 1], ae[:, 2:L + 2]
                        cv = c1b[:, 0:L]
                        gsrc = psg[:, 0:L]
                        fout = FF(j, c0, c0 + L)
                    TS("pool", cv, a0, CW(0, j), CB(j), ALU.mult, ALU.add, [kae, "CV"], [kc1])
                    STT(cv, a1, CW(1, j), cv, ALU.mult, ALU.add, [kae, kc1, "CV"], [kc1])
                    STT(cv, a2, CW(2, j), cv, ALU.mult, ALU.add, [kae, kc1, "CV"], [kc1])
                    if b2_prev[0] is not None:
                        b2_back(*b2_prev[0])
                    b2_prev[0] = (cv, gsrc, fout, kc1, kg, [("F", j, i) for i in tl])
            ring_release(t_u)
            ring_release(t_g)

        if b2_prev[0] is not None:
            b2_back(*b2_prev[0])
            b2_prev[0] = None
        if st == 1:
            DMA("sp", ncp, HALO[:].rearrange("p j r -> p (j r)"), "o_ncp", [("HALO", j) for j in range(NFF)], [])
            DMA("sp", ncs, CS[:].rearrange("p j q r -> p (j q r)"), "o_ncs", [("CS", j) for j in range(NFF)], [])
        stage(10 * st + 9)
        for half in range(2):
            wd = [ring_take(f"wdn{half}_{kg}") for kg in range(3)]
            for i in tiles:
                c0, c1 = col(i)
                ps, kp = psb()
                for k in range(NFF):
                    t_, Wd, kwd = wd[k // 8]
                    nk = 8 if k // 8 < 2 else NFF - 16
                    Wdv = Wd[:, 0:nk * 512].rearrange("p (k c) -> p k c", c=512)
                    MM(ps[:, :], FF(k, c0, c1), Wdv[:, k % 8, :], k == 0, k == NFF - 1,
                       [kwd, ("F", k, i), REG], [kp])
                xh = X[:, i, half * 512:(half + 1) * 512]
                TT("dve", xh, xh, ps[:, :], ALU.add, [kp, ("X", i, half)], [("X", i, half)])
                if half == 1:
                    xk = [("X", i, 0), ("X", i, 1)]
                    (ss, kss), (ms, kms), (rs, krs) = stat3()
                    jt, kjt = tf()
                    ACT(jt[:].bitcast(BF16)[:, 0:D], X[:, i, :], AF.Square, xk, [kjt, kss], accum_out=ss)
                    TS("pool", ms, ss, 1.0 / D, EPS, ALU.mult, ALU.add, [kss], [kms])
                    TT("pool", rs, ms, NEGH[:], ALU.pow, [kms, "NEGH"], [krs])
                    dst = ys if i == 8 else yp[st * 1024 + 128 * i: st * 1024 + 128 * i + 128, :]
                    if st == 0 and i >= 6:
                        for hh in range(2):
                            to, kto = tf()
                            STT(to[:, 0:512], X[:, i, hh * 512:(hh + 1) * 512], rs, GF[:, hh * 512:(hh + 1) * 512],
                                ALU.mult, ALU.mult, xk + [krs, "GF"], [kto])
                            DMA("sp", dst[:, hh * 512:(hh + 1) * 512], to[:, 0:512], f"y{i}_{hh}", [kto], [])
                    else:
                        STT(X[:, i, :], X[:, i, :], rs, GF[:], ALU.mult, ALU.mult, xk + [krs, "GF"], xk)
                        DMA("sp", dst, X[:, i, :], f"y{i}", xk, [])
                    if st == 0:
                        if 1 <= i <= 5:
                            reload_x(i - 1)
                        elif i == 6:
                            reload_x(5)
                            reload_x(6)
                        elif i == 7:
                            reload_x(7)
                        nxt = order1[s0["s"]] if s0["s"] < len(order1) else None
                        if nxt is None or nxt == 8 or nxt <= i - 2:
                            s0_step()
            for t_, _, _ in wd:
                ring_release(t_, defer=(half == 1))

    except _Stop:
        pass
    assert STOP_AT is not None or ring_state["cursor"] == len(all_tiles)
    P.finalize()
    P.sbuf_left = nc.sbuf_bytes_remaining
    P.close()
    return nc, P


def _tile_w(W, nk, tw):
    N = W.shape[1]
    Wp = W.reshape(nk, 128, N).transpose(1, 0, 2)
    parts = []
    for c0 in range(0, N, tw):
        w = min(tw, N - c0)
        parts.append(Wp[:, :, c0:c0 + w].reshape(128, nk * w))
    return np.ascontiguousarray(np.concatenate(parts, axis=1), dtype=np.float32)


_CACHE = {}


def kernel(**inputs):
    f = lambda k: np.asarray(inputs[k], dtype=np.float32)
    x_prompt, x_sample = f("x_prompt"), f("x_sample")
    state_pool, state_conv = f("state_pool"), f("state_ffn_conv")
    w_in = f("w_in")[0]
    shared = {}
    shared["win"] = _tile_w(w_in, 8, 512)
    wpo_t = _tile_w(f("w_pool_out")[0], 4, 512).reshape(128, 2, 2048)
    wso_t = _tile_w(f("w_sgu_out")[0], 4, 512).reshape(128, 2, 2048)
    shared["wpso"] = np.ascontiguousarray(np.concatenate([wpo_t, wso_t], axis=2).reshape(128, 2 * 4096))
    shared["poolw"] = np.ascontiguousarray(f("pool_w")[0].transpose(1, 0, 2).reshape(128, 512))
    shared["wo"] = _tile_w(f("w_o")[0], 8, 512)
    shared["wup"] = _tile_w(f("ffn_w_up")[0], 8, 512)
    shared["wgt"] = _tile_w(f("ffn_w_gate")[0], 8, 512)
    shared["wdn"] = _tile_w(f("ffn_w_down")[0], NFF, 512)
    cvec = np.zeros((128, 108), np.float32)
    cvec[:, 0:8] = f("norm1_g")[0].reshape(8, 128).T
    cvec[:, 8:16] = f("norm2_g")[0].reshape(8, 128).T
    cvec[:, 16:20] = f("pool_scale")[0].reshape(4, 128).T
    cvec[:, 20:86] = f("ffn_conv_w")[0].reshape(3, NFF, 128).transpose(2, 0, 1).reshape(128, 66)
    cvec[:, 86:108] = f("ffn_conv_b")[0].reshape(NFF, 128).T
    shared["cvec"] = cvec
    shared["gsgu"] = np.ascontiguousarray(np.broadcast_to(f("sgu_norm_g")[0][None, :], (128, 512)))
    shared["gfin"] = np.ascontiguousarray(np.broadcast_to(f("final_norm_g")[None, :], (128, D)))
    shared["g1bc"] = np.ascontiguousarray(np.broadcast_to(f("norm1_g")[0][None, :], (128, D)))
    shared["g2bc"] = np.ascontiguousarray(np.broadcast_to(f("norm2_g")[0][None, :], (128, D)))
    sgu_b = f("sgu_b")[0]
    sgub = np.zeros((1, 1024), np.float32)
    sgub[0, 0:512] = sgu_b.reshape(512)
    sgub[0, 512:1024] = np.tile(sgu_b[:, :8], (1, 16)).reshape(512)
    shared["sgub"] = np.ascontiguousarray(np.broadcast_to(sgub, (128, 1024)))
    sgu_w = f("sgu_w")[0]
    shared["wst"] = np.ascontiguousarray(sgu_w.transpose(2, 0, 1).reshape(128, 512))
    wsts = np.zeros((128, 4, 128), np.float32)
    blk = sgu_w[:, :8, :8].transpose(2, 0, 1)
    for q in range(16):
        wsts[8 * q:8 * q + 8, :, 8 * q:8 * q + 8] = blk
    shared["wsts"] = wsts.reshape(128, 512)

    in_maps = []
    for c in range(8):
        m = dict(shared)
        m["xp"] = np.ascontiguousarray(x_prompt[c])
        m["xs"] = np.ascontiguousarray(x_sample[16 * c:16 * c + 16].reshape(128, D))
        sp = state_pool[0, 16 * c:16 * c + 16]
        spf = np.zeros((128, 4, 16, 23), np.float32)
        spf[:, :, :, 0:15] = sp.reshape(16, 15, 4, 128).transpose(3, 2, 0, 1)
        m["spfm"] = spf.reshape(128, 4 * 16 * 23)
        sc = state_conv[0, 16 * c:16 * c + 16]
        m["scfm"] = np.ascontiguousarray(sc.reshape(16, 2, NFF, 128).transpose(3, 2, 0, 1).reshape(128, NFF * 32))
        in_maps.append(m)

    if "nc" not in _CACHE:
        _CACHE["nc"] = build_program()[0]
    nc = _CACHE["nc"]
    res = run_bass_kernel_spmd(nc, in_maps, core_ids=list(range(8)))
    rs = res.results

    y_prompt = np.zeros((8, 2048, D), np.float32)
    y_sample = np.zeros((128, 8, D), np.float32)
    new_pool_prompt = np.zeros((1, 8, 15, 512), np.float32)
    new_pool_sample = np.zeros((1, 128, 15, 512), np.float32)
    new_conv_prompt = np.zeros((1, 8, 2, DFF), np.float32)
    new_conv_sample = np.zeros((1, 128, 2, DFF), np.float32)
    new_v = np.zeros((1, 128, 8, 512), np.float32)
    for c in range(8):
        r = rs[c]
        sl = slice(16 * c, 16 * c + 16)
        y_prompt[c] = r["yp"]
        y_sample[sl] = r["ys"].reshape(16, 8, D)
        new_pool_prompt[0, c] = r["npp"].reshape(128, 4, 16)[:, :, 1:16].transpose(2, 1, 0).reshape(15, 512)
        new_pool_sample[0, sl] = r["nps"].reshape(128, 4, 16, 23)[:, :, :, 8:23].transpose(2, 3, 1, 0).reshape(16, 15, 512)
        new_conv_prompt[0, c] = r["ncp"].reshape(128, NFF, 2).transpose(2, 1, 0).reshape(2, DFF)
        new_conv_sample[0, sl] = r["ncs"].reshape(128, NFF, 16, 2).transpose(2, 3, 1, 0).reshape(16, 2, DFF)
        new_v[0, sl] = r["nv"].reshape(16, 8, 512)
    return (y_prompt, y_sample, new_pool_prompt, new_pool_sample, new_conv_prompt, new_conv_sample, new_v)
```

```python
import contextlib
import numpy as np
import concourse.bass as bass
import concourse.mybir as mybir
from concourse.bass_utils import run_bass_kernel_spmd

F32 = mybir.dt.float32
BF16 = mybir.dt.bfloat16
AF = mybir.ActivationFunctionType
ALU = mybir.AluOpType

D = 1024
DFF = 2816
NFF = 22
EPS = 1e-6
POOL_W = (2, 4, 8, 16)
NSLOT = 6
STOP_AT = None
SKIP = set()


class _Stop(Exception):
    pass
TW = 1152


class _Op:
    __slots__ = ("id", "eng", "emit", "reads", "writes", "dma", "sem", "target",
                 "deps", "ticket", "signal", "waits")


class Prog:
    ENGS = ("pe", "act", "dve", "pool", "sp")

    def __init__(self, nc):
        self.nc = nc
        self.stack = contextlib.ExitStack()
        self.ops = []
        self.last_writer = {}
        self.readers = {}
        self.cnt_sem = {e: self.stack.enter_context(nc.semaphore("c_" + e)) for e in self.ENGS}
        self.dma_count = {}
        self.dma_sems = {}

    def sbuf(self, name, shape, dt):
        return self.stack.enter_context(self.nc.sbuf_tensor(name, list(shape), dt))

    def psum(self, name, shape, dt):
        return self.stack.enter_context(self.nc.psum_tensor(name, list(shape), dt))

    def add(self, eng, emit, reads=(), writes=(), dma_sem=None):
        op = _Op()
        op.id = len(self.ops)
        op.eng = eng
        op.emit = emit
        op.dma = dma_sem is not None
        op.sem = None
        op.target = None
        op.signal = False
        op.ticket = None
        if op.dma:
            if dma_sem not in self.dma_sems:
                self.dma_sems[dma_sem] = self.stack.enter_context(self.nc.semaphore("d_" + dma_sem))
                self.dma_count[dma_sem] = 0
            self.dma_count[dma_sem] += 16
            op.sem = dma_sem
            op.target = self.dma_count[dma_sem]
        raw = set()
        other = set()
        for k in reads:
            w = self.last_writer.get(k)
            if w is not None:
                raw.add(w)
        for k in writes:
            w = self.last_writer.get(k)
            if w is not None:
                other.add(w)
            rd = self.readers.get(k)
            if rd is not None:
                other.update(rd[0].values())
                other.update(rd[1])
        best = {}
        dmas = set()
        for pid in raw | other:
            p = self.ops[pid]
            if p.dma:
                dmas.add(pid)
                continue
            if (not op.dma) and p.eng == eng:
                if eng == "pe" or (pid not in raw and eng != "pool"):
                    continue
            if best.get(p.eng, -1) < pid:
                best[p.eng] = pid
        op.deps = list(best.values()) + list(dmas)
        for pid in best.values():
            self.ops[pid].signal = True
        for k in reads:
            rd = self.readers.setdefault(k, ({}, []))
            if op.dma:
                rd[1].append(op.id)
            else:
                rd[0][eng] = op.id
        for k in writes:
            self.last_writer[k] = op.id
            self.readers[k] = ({}, [])
        self.ops.append(op)
        return op

    def finalize(self):
        ops = self.ops
        tick = {e: 0 for e in self.ENGS}
        for op in ops:
            if op.signal and not op.dma:
                tick[op.eng] += 1
                op.ticket = tick[op.eng]
        seen = {e: {} for e in self.ENGS}
        for op in ops:
            w = {}
            for pid in op.deps:
                p = ops[pid]
                if p.dma:
                    key, val = ("d", p.sem), p.target
                else:
                    key, val = ("c", p.eng), p.ticket
                if seen[op.eng].get(key, 0) >= val:
                    continue
                if w.get(key, 0) < val:
                    w[key] = val
            for key, val in w.items():
                seen[op.eng][key] = val
            op.waits = list(w.items())
        final_waits = [(("d", n), v) for n, v in self.dma_count.items() if v > 0]
        streams = {e: [op for op in ops if op.eng == e] for e in self.ENGS}

        def sem_of(key):
            kind, n = key
            return self.dma_sems[n] if kind == "d" else self.cnt_sem[n]

        def run(ename, e):
            for op in streams[ename]:
                for key, val in op.waits:
                    e.wait_ge(sem_of(key), val)
                inst = op.emit(e)
                if op.dma:
                    inst.then_inc(self.dma_sems[op.sem], 16)
                elif op.signal:
                    inst.then_inc(self.cnt_sem[ename], 1)
            if ename == "sp":
                for key, val in final_waits:
                    e.wait_ge(sem_of(key), val)

        with self.nc.Block() as block:
            @block.tensor
            def _(e):
                run("pe", e)

            @block.scalar
            def _(e):
                run("act", e)

            @block.vector
            def _(e):
                run("dve", e)

            @block.gpsimd
            def _(e):
                run("pool", e)

            @block.sync
            def _(e):
                run("sp", e)
        self.stats = {e: len(streams[e]) for e in self.ENGS}
        self.stats["waits"] = sum(len(op.waits) for op in ops)
        self.stats["signals"] = sum(1 for op in ops if op.signal)

    def close(self):
        self.stack.close()


def build_program():
    nc = bass.Bass("TRN2", target_bir_lowering=False)
    P = Prog(nc)

    def din(name, shape):
        return nc.dram_tensor(name, list(shape), F32, kind="ExternalInput").ap()

    def dout(name, shape):
        return nc.dram_tensor(name, list(shape), F32, kind="ExternalOutput").ap()

    xp = din("xp", [2048, D])
    xs = din("xs", [128, D])
    spfm = din("spfm", [128, 4 * 16 * 23])
    scfm = din("scfm", [128, NFF * 32])
    win = din("win", [128, 7 * 4096])
    wpso = din("wpso", [128, 2 * 4096])
    poolw = din("poolw", [128, 512])
    wo = din("wo", [128, 2 * 4096])
    wup = din("wup", [128, 8 * DFF])
    wgt = din("wgt", [128, 8 * DFF])
    wdn = din("wdn", [128, 2 * NFF * 512])
    cvec = din("cvec", [128, 108])
    gsgu = din("gsgu", [128, 512])
    gfin = din("gfin", [128, D])
    g1bc = din("g1bc", [128, D])
    g2bc = din("g2bc", [128, D])
    sgub = din("sgub", [128, 1024])
    wst_d = din("wst", [128, 512])
    wsts_d = din("wsts", [128, 512])

    yp = dout("yp", [2048, D])
    ys = dout("ys", [128, D])
    npp = dout("npp", [128, 4 * 16])
    nps = dout("nps", [128, 4 * 16 * 23])
    ncp = dout("ncp", [128, NFF * 2])
    ncs = dout("ncs", [128, NFF * 32])
    nv = dout("nv", [128, 512])

    X = P.sbuf("X", [128, 9, D], F32)
    R = P.sbuf("R", [128, NFF * TW], BF16)
    MH = P.sbuf("MH", [128, 8, TW], BF16)
    PEXT = P.sbuf("PEXT", [128, 1040], F32)
    PEXTS = P.sbuf("PEXTS", [128, 4, 16, 23], F32)
    RING = [P.sbuf(f"RING{i}", [128, 4096], BF16) for i in range(NSLOT)]
    TF = [P.sbuf(f"TF{i}", [128, 528], F32) for i in range(6)]
    TB = [P.sbuf(f"TB{i}", [128, 512], BF16) for i in range(4)]
    XN = [P.sbuf(f"XN{i}", [128, D], BF16) for i in range(3)]
    G1B = P.sbuf("G1B", [128, D], F32)
    G2B = P.sbuf("G2B", [128, D], F32)
    CV = P.sbuf("CV", [128, 108], F32)
    GS = P.sbuf("GS", [128, 512], F32)
    GF = P.sbuf("GF", [128, D], F32)
    SGB = P.sbuf("SGB", [128, 1024], BF16)
    WST = P.sbuf("WST", [128, 512], BF16)
    WSTS = P.sbuf("WSTS", [128, 512], BF16)
    IDENT = P.sbuf("IDENT", [128, 128], BF16)
    ONES1 = P.sbuf("ONES1", [128, 128], BF16)
    NEGH = P.sbuf("NEGH", [128, 1], F32)
    INVC = P.sbuf("INVC", [128, 16], F32)
    CARRY = P.sbuf("CARRY", [128, 4, 16], F32)
    HALO = P.sbuf("HALO", [128, NFF, 2], F32)
    CS = P.sbuf("CS", [128, NFF, 16, 2], F32)
    STAT = P.sbuf("STAT", [128, 216], F32)
    TMP16 = P.sbuf("TMP16", [128, 16], F32)
    SCR = P.sbuf("SCR", [128, 4], F32)
    SCR2 = P.sbuf("SCR2", [128, 32], F32)
    PS = [P.psum(f"ps{i}", [128, 512], F32) for i in range(8)]

    O_HT, O_YP, O_U, O_VT, O_DD = 0, 9216, 13824, 18432, 23040

    def HT(k, c0, c1):
        return R[:, O_HT + k * TW + c0: O_HT + k * TW + c1]

    def YP(g, c0, c1):
        return R[:, O_YP + g * TW + c0: O_YP + g * TW + c1]

    def U(h, c0, c1):
        return R[:, O_U + h * TW + c0: O_U + h * TW + c1]

    def VT(i, f0, f1):
        return R[:, O_VT + i * 512 + f0: O_VT + i * 512 + f1]

    def DD(b, c0, c1):
        return R[:, O_DD + b * TW + c0: O_DD + b * TW + c1]

    def FF(j, c0, c1):
        return R[:, j * TW + c0: j * TW + c1]

    REG = "REG"

    def MM(out, lhsT, rhs, start, stop, reads, writes):
        P.add("pe", lambda e: e.matmul(out, lhsT=lhsT, rhs=rhs, start=start, stop=stop), reads, writes)

    def TR(out, in_, reads, writes):
        P.add("pe", lambda e: e.transpose(out, in_, IDENT[:]), list(reads) + ["IDENT"], writes)

    def ACT(out, in_, func, reads, writes, **kw):
        P.add("act", lambda e: e.activation(out=out, in_=in_, func=func, **kw), reads, writes)

    def TT(eng, out, in0, in1, op, reads, writes):
        P.add(eng, lambda e: e.tensor_tensor(out=out, in0=in0, in1=in1, op=op), reads, writes)

    def TS(eng, out, in0, s1, s2, op0, op1, reads, writes):
        if s2 is None:
            P.add(eng, lambda e: e.tensor_scalar(out=out, in0=in0, scalar1=s1, scalar2=None, op0=op0),
                  reads, writes)
        else:
            P.add(eng, lambda e: e.tensor_scalar(out=out, in0=in0, scalar1=s1, scalar2=s2, op0=op0, op1=op1),
                  reads, writes)

    def STT(out, in0, scalar, in1, op0, op1, reads, writes):
        P.add("dve", lambda e: e.scalar_tensor_tensor(out=out, in0=in0, scalar=scalar, in1=in1, op0=op0, op1=op1),
              reads, writes)

    def CP(eng, out, in_, reads, writes):
        P.add(eng, lambda e: e.tensor_copy(out=out, in_=in_), reads, writes)

    def MSET(eng, ap, val, writes):
        P.add(eng, lambda e: e.memset(ap, val), (), writes)

    def DMA(eng, out, in_, sem, reads, writes):
        P.add(eng, lambda e: e.dma_start(out=out, in_=in_), reads, writes, dma_sem=sem)

    cnt = {"ps": 0, "tf": 0, "tb": 0, "xn": 0, "stat": 0}

    def psb():
        b = cnt["ps"] % 8
        cnt["ps"] += 1
        return PS[b], ("ps", b)

    def tf():
        b = cnt["tf"] % len(TF)
        cnt["tf"] += 1
        return TF[b], ("TF", b)

    def tb():
        b = cnt["tb"] % len(TB)
        cnt["tb"] += 1
        return TB[b], ("TB", b)

    def xn():
        b = cnt["xn"] % 3
        cnt["xn"] += 1
        return XN[b], ("XN", b)

    def stat3():
        c = cnt["stat"]
        cnt["stat"] += 3
        assert c + 3 <= 216
        return [(STAT[:, c + i: c + i + 1], ("ST", c + i)) for i in range(3)]

    def tile_list():
        tl = []
        tl.append(("win0", win[:, 0:4096], 4096))
        tl.append(("win1", win[:, 4096:8192], 4096))
        tl.append(("poolw", poolw[:, 0:512], 512))
        tl.append(("win2", win[:, 8192:12288], 4096))
        for cg in range(2):
            tl.append((f"pso{cg}", wpso[:, cg * 4096:(cg + 1) * 4096], 4096))
            tl.append((f"wga{cg}", win[:, (3 + cg) * 4096:(4 + cg) * 4096], 4096))
            tl.append((f"wgb{cg}", win[:, (5 + cg) * 4096:(6 + cg) * 4096], 4096))
        tl.append(("wo0", wo[:, 0:4096], 4096))
        tl.append(("wo1", wo[:, 4096:8192], 4096))
        for jg in range(6):
            ncol = 512 if jg < 5 else 256
            tl.append((f"wup{jg}", wup[:, jg * 4096: jg * 4096 + 8 * ncol], 8 * ncol))
            tl.append((f"wgt{jg}", wgt[:, jg * 4096: jg * 4096 + 8 * ncol], 8 * ncol))
        for half in range(2):
            for kg in range(3):
                k0, k1 = 8 * kg, min(8 * kg + 8, NFF)
                tl.append((f"wdn{half}_{kg}",
                           wdn[:, (half * NFF + k0) * 512:(half * NFF + k1) * 512], (k1 - k0) * 512))
        return tl

    st_tiles = tile_list()
    NT_ST = len(st_tiles)
    all_tiles = st_tiles + st_tiles
    ring_state = {"next_load": 0, "cursor": 0}

    def ring_issue(t):
        if t >= len(all_tiles):
            return
        name, src, n = all_tiles[t]
        s = t % NSLOT
        DMA("pool", RING[s][:, 0:n], src, f"w{s}", (), [("w", s)])

    def ring_take(name):
        t = ring_state["cursor"]
        assert all_tiles[t][0] == name, (all_tiles[t][0], name)
        ring_state["cursor"] += 1
        s = t % NSLOT
        return t, RING[s], ("w", s)

    deferred = []

    def ring_release(t, defer=False):
        if defer:
            deferred.append(t + NSLOT)
        else:
            ring_issue(t + NSLOT)

    def flush_deferred():
        for t in deferred:
            ring_issue(t)
        deferred.clear()

    early_x0 = set()
    MSET("pool", NEGH[:], -0.5, ["NEGH"])
    t0, k0 = tf()
    MSET("pool", t0[:, 0:128], 1.0, [k0])
    P.add("pool", lambda e: e.affine_select(out=IDENT[:], in_=t0[:, 0:128], pattern=[[-1, 128]],
                                            compare_op=ALU.is_equal, fill=0.0, base=0, channel_multiplier=1),
          [k0], ["IDENT"])
    DMA("sp", X[:, 0, :], xp[0:128, :], "x0", (), [("X", 0, 0), ("X", 0, 1)])
    DMA("sp", G1B[:], g1bc, "c10", (), ["G1B"])
    early_x0.add(0)
    for i in range(1, 8):
        DMA("sp", X[:, i, :], xp[128 * i: 128 * i + 128, :], f"x{i}", (), [("X", i, 0), ("X", i, 1)])
        early_x0.add(i)
    ring_issue(0)
    for t in range(1, NSLOT):
        deferred.append(t)
    MSET("pool", ONES1[:], 1.0 / 128.0, ["ONES1"])
    MSET("pool", CARRY[:], 0.0, [("CARRY", g) for g in range(4)])
    MSET("pool", HALO[:], 0.0, [("HALO", j) for j in range(NFF)])
    MSET("pool", SCR[:], 0.0, ["SCR"])
    for t in range(15):
        MSET("pool", INVC[:, t:t + 1], 1.0 / (t + 1), ["INVC"])
    DMA("sp", CV[:], cvec, "c0", (), ["CV"])

    def late_setup():
        DMA("sp", GS[:], gsgu, "c1", (), ["GS"])
        for src, dst, key, sem in ((wst_d, WST, "WST", "c3"), (wsts_d, WSTS, "WSTS", "c4")):
            t1, k1 = tf()
            DMA("sp", t1[:, 0:512], src, sem, (), [k1])
            P.add("pool", lambda e, t1=t1, dst=dst: e.affine_select(
                out=dst[:].rearrange("p (h t) -> p h t", h=4),
                in_=t1[:, 0:512].rearrange("p (h t) -> p h t", h=4),
                pattern=[[0, 4], [1, 128]], compare_op=ALU.is_ge, fill=0.0, base=0, channel_multiplier=-1),
                [k1], [key])
        for half in range(2):
            t1, k1 = tf()
            DMA("sp", t1[:, 0:512], sgub[:, half * 512:(half + 1) * 512], f"c{5 + half}", (), [k1])
            CP("dve", SGB[:, half * 512:(half + 1) * 512], t1[:, 0:512], [k1], ["SGB"])

    def late_bulk(gate):
        DMA("sp", G2B[:], g2bc, "c11", gate, ["G2B"])
        DMA("sp", X[:, 8, :], xs, "x8", (), [("X", 8, 0), ("X", 8, 1)])
        DMA("sp", GF[:], gfin, "c2", (), ["GF"])
        DMA("sp", PEXTS[:].rearrange("p g q t -> p (g q t)"), spfm, "c7", (), [("PEXTS", g) for g in range(4)])
        DMA("sp", CS[:].rearrange("p j q r -> p (j q r)"), scfm, "c8", (), [("CS", j) for j in range(NFF)])

    early_x = set()
    pending_reload = []

    def reload_x(i):
        DMA("sp", X[:, i, :], xp[1024 + 128 * i: 1024 + 128 * i + 128, :], f"x{i}", (),
            [("X", i, 0), ("X", i, 1)])
        early_x.add(i)

    G1 = lambda k: CV[:, k:k + 1]
    G2 = lambda k: CV[:, 8 + k:9 + k]
    PSC = lambda g: CV[:, 16 + g:17 + g]
    CW = lambda r, j: CV[:, 20 + r * NFF + j: 21 + r * NFF + j]
    CB = lambda j: CV[:, 86 + j: 87 + j]

    def norm_stats(i):
        xk = [("X", i, 0), ("X", i, 1)]
        (ss, kss), (ms, kms), (rs, krs) = stat3()
        xb, kx = xn()
        ACT(xb[:], X[:, i, :], AF.Square, xk, [kx, kss], accum_out=ss)
        TS("pool", ms, ss, 1.0 / D, EPS, ALU.mult, ALU.add, [kss], [kms])
        TT("pool", rs, ms, NEGH[:], ALU.pow, [kms, "NEGH"], [krs])
        return xb, kx, rs, krs

    def norm_scale(i, xb, kx, rs, krs, GBC, gkey):
        xk = [("X", i, 0), ("X", i, 1)]
        STT(xb[:], X[:, i, :], rs, GBC[:], ALU.mult, ALU.mult, xk + [krs, gkey], [kx])

    def norm_pe(eng, xb, kx, dst3d, dst_keys, extra):
        ps, kp = psb()
        psv = ps[:].bitcast(BF16)
        for k in range(8):
            TR(psv[:, k * 128:(k + 1) * 128], xb[:, k * 128:(k + 1) * 128], [kx], [kp])
        src = psv.rearrange("p (k t) -> p k t", k=8)
        if eng == "act":
            ACT(dst3d, src, AF.Identity, [kp] + extra, dst_keys)
        else:
            CP("dve", dst3d, src, [kp] + extra, dst_keys)

    s0 = {"s": 0, "nst": {}}
    order1 = [8, 0, 1, 2, 3, 4, 5, 6, 7]

    def s0_step():
        sidx = s0["s"]
        n = len(order1)
        if sidx < n:
            t = order1[sidx]
            s0["nst"][t] = norm_stats(t)
        if 0 <= sidx - 1 < n:
            t = order1[sidx - 1]
            norm_scale(t, *s0["nst"][t], G1B, "G1B")
        if 0 <= sidx - 2 < n:
            t = order1[sidx - 2]
            pxb, pkx = s0["nst"].pop(t)[:2]
            norm_pe("act" if sidx % 2 == 0 else "dve", pxb, pkx, MH[:, :, 128 * t:128 * t + 128],
                    [("MH", k, t) for k in range(8)], [])
        s0["s"] += 1

    def s0_done():
        return s0["s"] >= len(order1) + 2

    def pool_adds(ext, ek, Lx, w, three_d, add_eng="pool"):
        S = (lambda ap, a, b: ap[:, :, a:b]) if three_d else (lambda ap, a, b: ap[:, a:b])

        def view(t):
            if three_d:
                return t[:, 0:16 * Lx].rearrange("p (q t) -> p q t", q=16)
            return t[:, 0:Lx]
        cur, ck = ext, list(ek)
        have, vf = 1, 0
        while have < w:
            tbuf, tk = tf()
            nxt = view(tbuf)
            lo = vf + have
            TT(add_eng, S(nxt, lo, Lx), S(cur, lo, Lx), S(cur, lo - have, Lx - have), ALU.add, ck, [tk])
            vf = lo
            have *= 2
            cur, ck = nxt, [tk]
        return cur, ck, S

    def pool_finish(state, ext, ek, Hh, T, w, dd_out, dd_keys, fix_first):
        cur, ck, S = state
        STT(dd_out, S(cur, Hh, Hh + T), 1.0 / w, S(ext, Hh, Hh + T), ALU.mult, ALU.subtract,
            ck + list(ek) + [REG], dd_keys)
        if fix_first and w > 1:
            n = w - 1
            TT("dve", TMP16[:, 0:n], cur[:, Hh:Hh + n], INVC[:, 0:n], ALU.mult, ck + ["INVC"], ["TMP16"])
            TT("dve", dd_out[:, 0:n], TMP16[:, 0:n], ext[:, Hh:Hh + n], ALU.subtract,
               ["TMP16"] + list(ek) + [REG], dd_keys)

    GELU = AF.Gelu_apprx_tanh
    Uall = R[:, O_U:O_U + 4 * TW].rearrange("p (h t) -> p h t", h=4)

    P.marks = []

    def stage(n):
        P.marks.append((n, sum(1 for o in P.ops if o.eng == "pe")))
        if STOP_AT is not None and n >= STOP_AT:
            raise _Stop()

    try:
      for st in range(2):
        stage(10 * st + 0)
        ntile = 8 if st == 0 else 9
        tiles = list(range(ntile))
        blocks = [(0, 512, [0, 1, 2, 3], False), (512, 512, [4, 5, 6, 7], False)]
        if st == 1:
            blocks.append((1024, 128, [8], True))

        def col(i):
            return 128 * i, 128 * i + 128

        if st > 0:
            DMA("sp", SCR2[:, 0:16], cvec[:, 0:16], "bar", (), [REG])

        HT3 = R[:, O_HT:O_HT + 8 * TW].rearrange("p (k t) -> p k t", k=8)
        if st == 0:
            for i in tiles:
                if i == 8 or i in early_x0:
                    continue
                DMA("sp", X[:, i, :], xp[128 * i: 128 * i + 128, :], f"x{i}", (), [("X", i, 0), ("X", i, 1)])
            nst = {}
            for it in range(ntile + 2):
                if it < ntile:
                    nst[it] = norm_stats(it)
                if 0 <= it - 1 < ntile:
                    norm_scale(it - 1, *nst[it - 1], G1B, "G1B")
                if 0 <= it - 2 < ntile:
                    pi = it - 2
                    c0, c1 = col(pi)
                    pxb, pkx = nst.pop(pi)[:2]
                    norm_pe("act" if pi % 2 == 0 else "dve", pxb, pkx, HT3[:, :, c0:c1],
                            [("HT", k, pi) for k in range(8)], [REG])
        else:
            while not s0_done():
                s0_step()
            for bi, (c0, L, tl, smp) in enumerate(blocks):
                rk = [("MH", k, i) for k in range(8) for i in tl] + [REG]
                wk = [("HT", k, i) for k in range(8) for i in tl]
                if bi == 1:
                    ACT(HT3[:, :, c0:c0 + L], MH[:, :, c0:c0 + L], AF.Identity, rk, wk)
                else:
                    CP("dve", HT3[:, :, c0:c0 + L], MH[:, :, c0:c0 + L], rk, wk)

        if st == 0:
            late_setup()
            for t in deferred[:2]:
                ring_issue(t)
            del deferred[:2]
        stage(10 * st + 1)
        t_w0, W0, kw0 = ring_take("win0")
        W0v = W0[:].rearrange("p (k c) -> p k c", k=8)

        def emit_A2(g):
            db = g % 2
            for bi, (c0, L, tl, smp) in enumerate(blocks):
                ps, kp = psb()
                MM(ps[:, 0:L], PW[:, g * 128:(g + 1) * 128], DD(db, c0, c0 + L), True, True,
                   [kpw, ("DD", db, bi), REG], [kp])
                TS("dve", YP(g, c0, c0 + L), ps[:, 0:L], PSC(g), None, ALU.mult, None,
                   [kp, "CV", REG], [("YP", g, bi)])

        def emit_A1(g):
            w = POOL_W[g]
            CP("pool", PEXT[:, 0:16], CARRY[:, g, :], [("CARRY", g)], [("PEXT", "h")])
            for bi, (c0, L, tl, smp) in enumerate(blocks):
                ps, kp = psb()
                for k in range(8):
                    MM(ps[:, 0:L], W0v[:, k, g * 128:(g + 1) * 128], HT(k, c0, c0 + L), k == 0, k == 7,
                       [kw0, REG] + [("HT", k, i) for i in tl], [kp])
                if smp:
                    ACT(PEXTS[:, g, :, 15:23], ps[:, 0:128].rearrange("p (q t) -> p q t", q=16), AF.Identity,
                        [kp], [("PEXTS", g)])
                else:
                    ACT(PEXT[:, 16 + c0:16 + c0 + L], ps[:, 0:L], AF.Identity, [kp], [("PEXT", bi)])
            db = g % 2
            pk = [("PEXT", "h"), ("PEXT", 0), ("PEXT", 1)]
            states = {}
            for bi, (c0, L, tl, smp) in enumerate(blocks):
                if not smp:
                    states[bi] = pool_adds(PEXT[:, c0:c0 + 528], pk, 528, w, False,
                                           add_eng=("pool" if bi == 0 else "dve"))
            for bi, (c0, L, tl, smp) in enumerate(blocks):
                if not smp:
                    pool_finish(states[bi], PEXT[:, c0:c0 + 528], pk, 16, 512, w,
                                DD(db, c0, c0 + L), [("DD", db, bi)], st == 0 and bi == 0)
            for bi, (c0, L, tl, smp) in enumerate(blocks):
                if smp:
                    stt_ = pool_adds(PEXTS[:, g, :, :], [("PEXTS", g)], 23, w, True)
                    pool_finish(stt_, PEXTS[:, g, :, :], [("PEXTS", g)], 15, 8, w,
                                DD(db, c0, c0 + L).rearrange("p (q t) -> p q t", q=16), [("DD", db, bi)], False)
            CP("pool", CARRY[:, g, :], PEXT[:, 1024:1040], [("PEXT", 1)], [("CARRY", g)])

        def emit_A3(h):
            for bi, (c0, L, tl, smp) in enumerate(blocks):
                ps, kp = psb()
                for k in range(8):
                    MM(ps[:, 0:L], W1v[:, k, h * 128:(h + 1) * 128], HT(k, c0, c0 + L), k == 0, k == 7,
                       [kw1, REG] + [("HT", k, i) for i in tl], [kp])
                ACT(U(h, c0, c0 + L), ps[:, 0:L], GELU, [kp, REG], [("U", h, i) for i in tl])

        emit_A1(0)
        flush_deferred()
        emit_A1(1)
        t_w1, W1, kw1 = ring_take("win1")
        W1v = W1[:].rearrange("p (k c) -> p k c", k=8)
        emit_A3(0)
        t_pw, PW, kpw = ring_take("poolw")
        emit_A2(0)
        emit_A1(2)
        emit_A3(1)
        emit_A2(1)
        emit_A1(3)
        if st == 1:
            DMA("sp", npp, CARRY[:].rearrange("p g t -> p (g t)"), "o_npp", [("CARRY", g) for g in range(4)], [])
            DMA("sp", nps, PEXTS[:].rearrange("p g q t -> p (g q t)"), "o_nps", [("PEXTS", g) for g in range(4)], [])
        ring_release(t_w0)
        emit_A3(2)
        emit_A2(2)
        emit_A3(3)
        ring_release(t_w1)

        stage(10 * st + 2)
        stage(10 * st + 3)
        t_w2, W2, kw2 = ring_take("win2")
        W2v = W2[:].rearrange("p (k c) -> p k c", k=8)
        def emit_A5(i):
            c0, c1 = col(i)
            smp = (i == 8)
            WS, kws = (WSTS, "WSTS") if smp else (WST, "WST")
            boff = 512 if smp else 0
            ps, kp = psb()
            for h in range(4):
                MM(ps[:, h * 128:(h + 1) * 128], VT(i, h * 128, (h + 1) * 128), WS[:, h * 128:(h + 1) * 128],
                   True, False, [("VT", i), kws, REG], [kp])
                MM(ps[:, h * 128:(h + 1) * 128], ONES1[:], SGB[:, boff + h * 128: boff + (h + 1) * 128],
                   False, True, ["ONES1", "SGB"], [kp])
            uk = [("U", h, i) for h in range(4)]
            TT("dve", Uall[:, :, c0:c1], Uall[:, :, c0:c1], ps[:, :].rearrange("p (h t) -> p h t", h=4), ALU.mult,
               [kp, REG] + uk, uk)

        a4 = {}

        def a4_front(i):
            c0, c1 = col(i)
            ps, kp = psb()
            for k in range(8):
                MM(ps[:, :], HT(k, c0, c1), W2v[:, k, :], k == 0, k == 7, [kw2, REG, ("HT", k, i)], [kp])
            vg, kvg = tf()
            ACT(vg[:, 0:512], ps[:, :], GELU, [kp], [kvg])
            (ss, kss), (ms, kms), (rs, krs) = stat3()
            jb, kj = tb()
            P.add("dve", lambda e, jb=jb, vg=vg, ss=ss: e.scalar_tensor_tensor(
                out=jb[:], in0=vg[:, 0:512], scalar=1.0, in1=vg[:, 0:512],
                op0=ALU.mult, op1=ALU.mult, accum_out=ss), [kvg], [kj, kss])
            TS("pool", ms, ss, 1.0 / 512, EPS, ALU.mult, ALU.add, [kss], [kms])
            TT("pool", rs, ms, NEGH[:], ALU.pow, [kms, "NEGH"], [krs])
            a4[i] = (vg, kvg, rs, krs)

        def a4_back(i):
            vg, kvg, rs, krs = a4.pop(i)
            if i == 8:
                STT(vg[:, 0:512], vg[:, 0:512], rs, GS[:], ALU.mult, ALU.mult, [kvg, krs, "GS"], [kvg])
                DMA("sp", nv, vg[:, 0:512], "o_nv", [kvg], [])
                CP("dve", VT(i, 0, 512), vg[:, 0:512], [kvg, REG], [("VT", i)])
            else:
                STT(VT(i, 0, 512), vg[:, 0:512], rs, GS[:], ALU.mult, ALU.mult, [kvg, krs, "GS", REG], [("VT", i)])

        for it in range(ntile + 3):
            if it == 2:
                emit_A2(3)
                ring_release(t_pw)
            if it < ntile:
                a4_front(it)
            if 0 <= it - 1 < ntile:
                a4_back(it - 1)
            if 0 <= it - 3 < ntile:
                emit_A5(it - 3)
        ring_release(t_w2)

        stage(10 * st + 4)
        stage(10 * st + 5)
        for cg in range(2):
            t_a, WPS, kwpo = ring_take(f"pso{cg}")
            kwso = kwpo
            t_c, WGA, kwga = ring_take(f"wga{cg}")
            t_d, WGB, kwgb = ring_take(f"wgb{cg}")
            WPOv = WPS[:, 0:2048].rearrange("p (k c) -> p k c", k=4)
            WSOv = WPS[:, 2048:4096].rearrange("p (k c) -> p k c", k=4)
            WGAv = WGA[:].rearrange("p (k c) -> p k c", k=8)
            WGBv = WGB[:].rearrange("p (k c) -> p k c", k=8)
            for c4 in range(4):
                c = cg * 4 + c4
                lo, hi = c4 * 128, (c4 + 1) * 128
                for bi, (c0, L, tl, smp) in enumerate(blocks):
                    psA, kA = psb()
                    psB, kB = psb()
                    psGa, kGa = psb()
                    psGb, kGb = psb()
                    for k in range(4):
                        MM(psA[:, 0:L], WPOv[:, k, lo:hi], YP(k, c0, c0 + L), k == 0, k == 3,
                           [kwpo, ("YP", k, bi), REG], [kA])
                    for k in range(4):
                        MM(psB[:, 0:L], WSOv[:, k, lo:hi], U(k, c0, c0 + L), k == 0, k == 3,
                           [kwso, REG] + [("U", k, i) for i in tl], [kB])
                    for k in range(8):
                        MM(psGa[:, 0:L], WGAv[:, k, lo:hi], HT(k, c0, c0 + L), k == 0, k == 7,
                           [kwga, REG] + [("HT", k, i) for i in tl], [kGa])
                    for k in range(8):
                        MM(psGb[:, 0:L], WGBv[:, k, lo:hi], HT(k, c0, c0 + L), k == 0, k == 7,
                           [kwgb, REG] + [("HT", k, i) for i in tl], [kGb])
                    sa, ksa = tb()
                    sb_, ksb = tb()
                    ACT(sa[:, 0:L], psGa[:, 0:L], AF.Sigmoid, [kGa], [ksa])
                    ACT(sb_[:, 0:L], psGb[:, 0:L], AF.Sigmoid, [kGb], [ksb])
                    t1, kt1 = tf()
                    t2, kt2 = tf()
                    TT("dve", t1[:, 0:L], sa[:, 0:L], psA[:, 0:L], ALU.mult, [ksa, kA], [kt1])
                    TT("dve", t2[:, 0:L], sb_[:, 0:L], psB[:, 0:L], ALU.mult, [ksb, kB], [kt2])
                    TT("pool", MH[:, c, c0:c0 + L], t1[:, 0:L], t2[:, 0:L], ALU.add, [kt1, kt2],
                       [("MH", c, i) for i in tl])
                    if st == 0 and c == 0 and bi == 0:
                        late_bulk([("MH", 0, 0)])
            ring_release(t_a)
            ring_release(t_c)
            ring_release(t_d)

        stage(10 * st + 6)
        t_o0, WO0, kwo0 = ring_take("wo0")
        t_o1, WO1, kwo1 = ring_take("wo1")
        WOv = [WO0[:].rearrange("p (k c) -> p k c", k=8), WO1[:].rearrange("p (k c) -> p k c", k=8)]
        kwo = [kwo0, kwo1]
        nst = {}
        for it in range(ntile + 2):
            if it < ntile:
                i = it
                c0, c1 = col(i)
                for half in range(2):
                    ps, kp = psb()
                    for k in range(8):
                        MM(ps[:, :], MH[:, k, c0:c1], WOv[half][:, k, :], k == 0, k == 7,
                           [kwo[half], ("MH", k, i)], [kp])
                    xh = X[:, i, half * 512:(half + 1) * 512]
                    TT("dve", xh, xh, ps[:, :], ALU.add, [kp, ("X", i, half)], [("X", i, half)])
            if it < ntile:
                nst[it] = norm_stats(it)
            if 0 <= it - 1 < ntile:
                norm_scale(it - 1, *nst[it - 1], G2B, "G2B")
            if 0 <= it - 2 < ntile:
                pi = it - 2
                c0, c1 = col(pi)
                pxb, pkx = nst.pop(pi)[:2]
                norm_pe("act" if pi % 2 == 0 else "dve", pxb, pkx, MH[:, :, c0:c1],
                        [("MH", k, pi) for k in range(8)], [])
        ring_release(t_o0)
        ring_release(t_o1)

        stage(10 * st + 7)
        stage(10 * st + 8)
        DMA("sp", SCR2[:, 16:32], cvec[:, 16:32], "bar", (), [REG])
        b2_prev = [None]

        def b2_back(cv, gsrc, fout, kc1, kg, fkeys):
            ACT(cv, cv, GELU, [kc1], [kc1])
            TT("dve", fout, cv, gsrc, ALU.mult, [kc1, kg, REG], fkeys)

        for jg in range(6):
            ncol = 512 if jg < 5 else 256
            t_u, WU, kwu = ring_take(f"wup{jg}")
            t_g, WG, kwg = ring_take(f"wgt{jg}")
            WUv = WU[:, 0:8 * ncol].rearrange("p (k c) -> p k c", k=8)
            WGv = WG[:, 0:8 * ncol].rearrange("p (k c) -> p k c", k=8)
            for j4 in range(ncol // 128):
                j = jg * 4 + j4
                lo, hi = j4 * 128, (j4 + 1) * 128
                for bi, (c0, L, tl, smp) in enumerate(blocks):
                    psa, ka = psb()
                    psg, kg = psb()
                    for k in range(8):
                        MM(psa[:, 0:L], WUv[:, k, lo:hi], MH[:, k, c0:c0 + L], k == 0, k == 7,
                           [kwu] + [("MH", k, i) for i in tl], [ka])
                    for k in range(8):
                        MM(psg[:, 0:L], WGv[:, k, lo:hi], MH[:, k, c0:c0 + L], k == 0, k == 7,
                           [kwg] + [("MH", k, i) for i in tl], [kg])
                    ae, kae = tf()
                    c1b, kc1 = tf()
                    if smp:
                        ae3 = ae[:, 0:160].rearrange("p (q t) -> p q t", q=16)
                        ACT(ae3[:, :, 2:10], psa[:, 0:128].rearrange("p (q t) -> p q t", q=16), AF.Identity,
                            [ka], [kae])
                        CP("pool", ae3[:, :, 0:2], CS[:, j, :, :], [("CS", j), kae], [kae])
                        CP("pool", CS[:, j, :, :], ae3[:, :, 8:10], [kae], [("CS", j)])
                        a0, a1, a2 = ae3[:, :, 0:8], ae3[:, :, 1:9], ae3[:, :, 2:10]
                        cv = c1b[:, 0:128].rearrange("p (q t) -> p q t", q=16)
                        gsrc = psg[:, 0:128].rearrange("p (q t) -> p q t", q=16)
                        fout = FF(j, c0, c0 + L).rearrange("p (q t) -> p q t", q=16)
                    else:
                        ACT(ae[:, 2:2 + L], psa[:, 0:L], AF.Identity, [ka], [kae])
                        CP("pool", ae[:, 0:2], HALO[:, j, :], [("HALO", j), kae], [kae])
                        CP("pool", HALO[:, j, :], ae[:, L:L + 2], [kae], [("HALO", j)])
                        a0, a1, a2 = ae[:, 0:L], ae[:, 1:L + 1], ae[:, 2:L + 2]
                        cv = c1b[:, 0:L]
                        gsrc = psg[:, 0:L]
                        fout = FF(j, c0, c0 + L)
                    TS("pool", cv, a0, CW(0, j), CB(j), ALU.mult, ALU.add, [kae, "CV"], [kc1])
                    STT(cv, a1, CW(1, j), cv, ALU.mult, ALU.add, [kae, kc1, "CV"], [kc1])
                    STT(cv, a2, CW(2, j), cv, ALU.mult, ALU.add, [kae, kc1, "CV"], [kc1])
                    if b2_prev[0] is not None:
                        b2_back(*b2_prev[0])
                    b2_prev[0] = (cv, gsrc, fout, kc1, kg, [("F", j, i) for i in tl])
            ring_release(t_u)
            ring_release(t_g)

        if b2_prev[0] is not None:
            b2_back(*b2_prev[0])
            b2_prev[0] = None
        if st == 1:
            DMA("sp", ncp, HALO[:].rearrange("p j r -> p (j r)"), "o_ncp", [("HALO", j) for j in range(NFF)], [])
            DMA("sp", ncs, CS[:].rearrange("p j q r -> p (j q r)"), "o_ncs", [("CS", j) for j in range(NFF)], [])
        stage(10 * st + 9)
        for half in range(2):
            wd = [ring_take(f"wdn{half}_{kg}") for kg in range(3)]
            for i in tiles:
                c0, c1 = col(i)
                ps, kp = psb()
                for k in range(NFF):
                    t_, Wd, kwd = wd[k // 8]
                    nk = 8 if k // 8 < 2 else NFF - 16
                    Wdv = Wd[:, 0:nk * 512].rearrange("p (k c) -> p k c", c=512)
                    MM(ps[:, :], FF(k, c0, c1), Wdv[:, k % 8, :], k == 0, k == NFF - 1,
                       [kwd, ("F", k, i), REG], [kp])
                xh = X[:, i, half * 512:(half + 1) * 512]
                TT("dve", xh, xh, ps[:, :], ALU.add, [kp, ("X", i, half)], [("X", i, half)])
                if half == 1:
                    xk = [("X", i, 0), ("X", i, 1)]
                    (ss, kss), (ms, kms), (rs, krs) = stat3()
                    jt, kjt = tf()
                    ACT(jt[:].bitcast(BF16)[:, 0:D], X[:, i, :], AF.Square, xk, [kjt, kss], accum_out=ss)
                    TS("pool", ms, ss, 1.0 / D, EPS, ALU.mult, ALU.add, [kss], [kms])
                    TT("pool", rs, ms, NEGH[:], ALU.pow, [kms, "NEGH"], [krs])
                    dst = ys if i == 8 else yp[st * 1024 + 128 * i: st * 1024 + 128 * i + 128, :]
                    if st == 0 and i >= 6:
                        for hh in range(2):
                            to, kto = tf()
                            STT(to[:, 0:512], X[:, i, hh * 512:(hh + 1) * 512], rs, GF[:, hh * 512:(hh + 1) * 512],
                                ALU.mult, ALU.mult, xk + [krs, "GF"], [kto])
                            DMA("sp", dst[:, hh * 512:(hh + 1) * 512], to[:, 0:512], f"y{i}_{hh}", [kto], [])
                    else:
                        STT(X[:, i, :], X[:, i, :], rs, GF[:], ALU.mult, ALU.mult, xk + [krs, "GF"], xk)
                        DMA("sp", dst, X[:, i, :], f"y{i}", xk, [])
                    if st == 0:
                        if 1 <= i <= 5:
                            reload_x(i - 1)
                        elif i == 6:
                            reload_x(5)
                            reload_x(6)
                        elif i == 7:
                            reload_x(7)
                        nxt = order1[s0["s"]] if s0["s"] < len(order1) else None
                        if nxt is None or nxt == 8 or nxt <= i - 2:
                            s0_step()
            for t_, _, _ in wd:
                ring_release(t_, defer=(half == 1))

    except _Stop:
        pass
    assert STOP_AT is not None or ring_state["cursor"] == len(all_tiles)
    P.finalize()
    P.sbuf_left = nc.sbuf_bytes_remaining
    P.close()
    return nc, P


def _tile_w(W, nk, tw):
    N = W.shape[1]
    Wp = W.reshape(nk, 128, N).transpose(1, 0, 2)
    parts = []
    for c0 in range(0, N, tw):
        w = min(tw, N - c0)
        parts.append(Wp[:, :, c0:c0 + w].reshape(128, nk * w))
    return np.ascontiguousarray(np.concatenate(parts, axis=1), dtype=np.float32)


_CACHE = {}


def kernel(**inputs):
    f = lambda k: np.asarray(inputs[k], dtype=np.float32)
    x_prompt, x_sample = f("x_prompt"), f("x_sample")
    state_pool, state_conv = f("state_pool"), f("state_ffn_conv")
    w_in = f("w_in")[0]
    shared = {}
    shared["win"] = _tile_w(w_in, 8, 512)
    wpo_t = _tile_w(f("w_pool_out")[0], 4, 512).reshape(128, 2, 2048)
    wso_t = _tile_w(f("w_sgu_out")[0], 4, 512).reshape(128, 2, 2048)
    shared["wpso"] = np.ascontiguousarray(np.concatenate([wpo_t, wso_t], axis=2).reshape(128, 2 * 4096))
    shared["poolw"] = np.ascontiguousarray(f("pool_w")[0].transpose(1, 0, 2).reshape(128, 512))
    shared["wo"] = _tile_w(f("w_o")[0], 8, 512)
    shared["wup"] = _tile_w(f("ffn_w_up")[0], 8, 512)
    shared["wgt"] = _tile_w(f("ffn_w_gate")[0], 8, 512)
    shared["wdn"] = _tile_w(f("ffn_w_down")[0], NFF, 512)
    cvec = np.zeros((128, 108), np.float32)
    cvec[:, 0:8] = f("norm1_g")[0].reshape(8, 128).T
    cvec[:, 8:16] = f("norm2_g")[0].reshape(8, 128).T
    cvec[:, 16:20] = f("pool_scale")[0].reshape(4, 128).T
    cvec[:, 20:86] = f("ffn_conv_w")[0].reshape(3, NFF, 128).transpose(2, 0, 1).reshape(128, 66)
    cvec[:, 86:108] = f("ffn_conv_b")[0].reshape(NFF, 128).T
    shared["cvec"] = cvec
    shared["gsgu"] = np.ascontiguousarray(np.broadcast_to(f("sgu_norm_g")[0][None, :], (128, 512)))
    shared["gfin"] = np.ascontiguousarray(np.broadcast_to(f("final_norm_g")[None, :], (128, D)))
    shared["g1bc"] = np.ascontiguousarray(np.broadcast_to(f("norm1_g")[0][None, :], (128, D)))
    shared["g2bc"] = np.ascontiguousarray(np.broadcast_to(f("norm2_g")[0][None, :], (128, D)))
    sgu_b = f("sgu_b")[0]
    sgub = np.zeros((1, 1024), np.float32)
    sgub[0, 0:512] = sgu_b.reshape(512)
    sgub[0, 512:1024] = np.tile(sgu_b[:, :8], (1, 16)).reshape(512)
    shared["sgub"] = np.ascontiguousarray(np.broadcast_to(sgub, (128, 1024)))
    sgu_w = f("sgu_w")[0]
    shared["wst"] = np.ascontiguousarray(sgu_w.transpose(2, 0, 1).reshape(128, 512))
    wsts = np.zeros((128, 4, 128), np.float32)
    blk = sgu_w[:, :8, :8].transpose(2, 0, 1)
    for q in range(16):
        wsts[8 * q:8 * q + 8, :, 8 * q:8 * q + 8] = blk
    shared["wsts"] = wsts.reshape(128, 512)

    in_maps = []
    for c in range(8):
        m = dict(shared)
        m["xp"] = np.ascontiguousarray(x_prompt[c])
        m["xs"] = np.ascontiguousarray(x_sample[16 * c:16 * c + 16].reshape(128, D))
        sp = state_pool[0, 16 * c:16 * c + 16]
        spf = np.zeros((128, 4, 16, 23), np.float32)
        spf[:, :, :, 0:15] = sp.reshape(16, 15, 4, 128).transpose(3, 2, 0, 1)
        m["spfm"] = spf.reshape(128, 4 * 16 * 23)
        sc = state_conv[0, 16 * c:16 * c + 16]
        m["scfm"] = np.ascontiguousarray(sc.reshape(16, 2, NFF, 128).transpose(3, 2, 0, 1).reshape(128, NFF * 32))
        in_maps.append(m)

    if "nc" not in _CACHE:
        _CACHE["nc"] = build_program()[0]
    nc = _CACHE["nc"]
    res = run_bass_kernel_spmd(nc, in_maps, core_ids=list(range(8)))
    rs = res.results

    y_prompt = np.zeros((8, 2048, D), np.float32)
    y_sample = np.zeros((128, 8, D), np.float32)
    new_pool_prompt = np.zeros((1, 8, 15, 512), np.float32)
    new_pool_sample = np.zeros((1, 128, 15, 512), np.float32)
    new_conv_prompt = np.zeros((1, 8, 2, DFF), np.float32)
    new_conv_sample = np.zeros((1, 128, 2, DFF), np.float32)
    new_v = np.zeros((1, 128, 8, 512), np.float32)
    for c in range(8):
        r = rs[c]
        sl = slice(16 * c, 16 * c + 16)
        y_prompt[c] = r["yp"]
        y_sample[sl] = r["ys"].reshape(16, 8, D)
        new_pool_prompt[0, c] = r["npp"].reshape(128, 4, 16)[:, :, 1:16].transpose(2, 1, 0).reshape(15, 512)
        new_pool_sample[0, sl] = r["nps"].reshape(128, 4, 16, 23)[:, :, :, 8:23].transpose(2, 3, 1, 0).reshape(16, 15, 512)
        new_conv_prompt[0, c] = r["ncp"].reshape(128, NFF, 2).transpose(2, 1, 0).reshape(2, DFF)
        new_conv_sample[0, sl] = r["ncs"].reshape(128, NFF, 16, 2).transpose(2, 3, 1, 0).reshape(16, 2, DFF)
        new_v[0, sl] = r["nv"].reshape(16, 8, 512)
    return (y_prompt, y_sample, new_pool_prompt, new_pool_sample, new_conv_prompt, new_conv_sample, new_v)
```

```python
import contextlib
import numpy as np
import concourse.bass as bass
import concourse.mybir as mybir
from concourse.bass_utils import run_bass_kernel_spmd

F32 = mybir.dt.float32
BF16 = mybir.dt.bfloat16
AF = mybir.ActivationFunctionType
ALU = mybir.AluOpType

D = 1024
DFF = 2816
NFF = 22
EPS = 1e-6
POOL_W = (2, 4, 8, 16)
NSLOT = 6
STOP_AT = None
SKIP = set()


class _Stop(Exception):
    pass
TW = 1152


class _Op:
    __slots__ = ("id", "eng", "emit", "reads", "writes", "dma", "sem", "target",
                 "deps", "ticket", "signal", "waits")


class Prog:
    ENGS = ("pe", "act", "dve", "pool", "sp")

    def __init__(self, nc):
        self.nc = nc
        self.stack = contextlib.ExitStack()
        self.ops = []
        self.last_writer = {}
        self.readers = {}
        self.cnt_sem = {e: self.stack.enter_context(nc.semaphore("c_" + e)) for e in self.ENGS}
        self.dma_count = {}
        self.dma_sems = {}

    def sbuf(self, name, shape, dt):
        return self.stack.enter_context(self.nc.sbuf_tensor(name, list(shape), dt))

    def psum(self, name, shape, dt):
        return self.stack.enter_context(self.nc.psum_tensor(name, list(shape), dt))

    def add(self, eng, emit, reads=(), writes=(), dma_sem=None):
        op = _Op()
        op.id = len(self.ops)
        op.eng = eng
        op.emit = emit
        op.dma = dma_sem is not None
        op.sem = None
        op.target = None
        op.signal = False
        op.ticket = None
        if op.dma:
            if dma_sem not in self.dma_sems:
                self.dma_sems[dma_sem] = self.stack.enter_context(self.nc.semaphore("d_" + dma_sem))
                self.dma_count[dma_sem] = 0
            self.dma_count[dma_sem] += 16
            op.sem = dma_sem
            op.target = self.dma_count[dma_sem]
        raw = set()
        other = set()
        for k in reads:
            w = self.last_writer.get(k)
            if w is not None:
                raw.add(w)
        for k in writes:
            w = self.last_writer.get(k)
            if w is not None:
                other.add(w)
            rd = self.readers.get(k)
            if rd is not None:
                other.update(rd[0].values())
                other.update(rd[1])
        best = {}
        dmas = set()
        for pid in raw | other:
            p = self.ops[pid]
            if p.dma:
                dmas.add(pid)
                continue
            if (not op.dma) and p.eng == eng:
                if eng == "pe" or (pid not in raw and eng != "pool"):
                    continue
            if best.get(p.eng, -1) < pid:
                best[p.eng] = pid
        op.deps = list(best.values()) + list(dmas)
        for pid in best.values():
            self.ops[pid].signal = True
        for k in reads:
            rd = self.readers.setdefault(k, ({}, []))
            if op.dma:
                rd[1].append(op.id)
            else:
                rd[0][eng] = op.id
        for k in writes:
            self.last_writer[k] = op.id
            self.readers[k] = ({}, [])
        self.ops.append(op)
        return op

    def finalize(self):
        ops = self.ops
        tick = {e: 0 for e in self.ENGS}
        for op in ops:
            if op.signal and not op.dma:
                tick[op.eng] += 1
                op.ticket = tick[op.eng]
        seen = {e: {} for e in self.ENGS}
        for op in ops:
            w = {}
            for pid in op.deps:
                p = ops[pid]
                if p.dma:
                    key, val = ("d", p.sem), p.target
                else:
                    key, val = ("c", p.eng), p.ticket
                if seen[op.eng].get(key, 0) >= val:
                    continue
                if w.get(key, 0) < val:
                    w[key] = val
            for key, val in w.items():
                seen[op.eng][key] = val
            op.waits = list(w.items())
        final_waits = [(("d", n), v) for n, v in self.dma_count.items() if v > 0]
        streams = {e: [op for op in ops if op.eng == e] for e in self.ENGS}

        def sem_of(key):
            kind, n = key
            return self.dma_sems[n] if kind == "d" else self.cnt_sem[n]

        def run(ename, e):
            for op in streams[ename]:
                for key, val in op.waits:
                    e.wait_ge(sem_of(key), val)
                inst = op.emit(e)
                if op.dma:
                    inst.then_inc(self.dma_sems[op.sem], 16)
                elif op.signal:
                    inst.then_inc(self.cnt_sem[ename], 1)
            if ename == "sp":
                for key, val in final_waits:
                    e.wait_ge(sem_of(key), val)

        with self.nc.Block() as block:
            @block.tensor
            def _(e):
                run("pe", e)

            @block.scalar
            def _(e):
                run("act", e)

            @block.vector
            def _(e):
                run("dve", e)

            @block.gpsimd
            def _(e):
                run("pool", e)

            @block.sync
            def _(e):
                run("sp", e)
        self.stats = {e: len(streams[e]) for e in self.ENGS}
        self.stats["waits"] = sum(len(op.waits) for op in ops)
        self.stats["signals"] = sum(1 for op in ops if op.signal)

    def close(self):
        self.stack.close()


def build_program():
    nc = bass.Bass("TRN2", target_bir_lowering=False)
    P = Prog(nc)

    def din(name, shape):
        return nc.dram_tensor(name, list(shape), F32, kind="ExternalInput").ap()

    def dout(name, shape):
        return nc.dram_tensor(name, list(shape), F32, kind="ExternalOutput").ap()

    xp = din("xp", [2048, D])
    xs = din("xs", [128, D])
    spfm = din("spfm", [128, 4 * 16 * 23])
    scfm = din("scfm", [128, NFF * 32])
    win = din("win", [128, 7 * 4096])
    wpso = din("wpso", [128, 2 * 4096])
    poolw = din("poolw", [128, 512])
    wo = din("wo", [128, 2 * 4096])
    wup = din("wup", [128, 8 * DFF])
    wgt = din("wgt", [128, 8 * DFF])
    wdn = din("wdn", [128, 2 * NFF * 512])
    cvec = din("cvec", [128, 108])
    gsgu = din("gsgu", [128, 512])
    gfin = din("gfin", [128, D])
    g1bc = din("g1bc", [128, D])
    g2bc = din("g2bc", [128, D])
    sgub = din("sgub", [128, 1024])
    wst_d = din("wst", [128, 512])
    wsts_d = din("wsts", [128, 512])

    yp = dout("yp", [2048, D])
    ys = dout("ys", [128, D])
    npp = dout("npp", [128, 4 * 16])
    nps = dout("nps", [128, 4 * 16 * 23])
    ncp = dout("ncp", [128, NFF * 2])
    ncs = dout("ncs", [128, NFF * 32])
    nv = dout("nv", [128, 512])

    X = P.sbuf("X", [128, 9, D], F32)
    R = P.sbuf("R", [128, NFF * TW], BF16)
    MH = P.sbuf("MH", [128, 8, TW], BF16)
    PEXT = P.sbuf("PEXT", [128, 1040], F32)
    PEXTS = P.sbuf("PEXTS", [128, 4, 16, 23], F32)
    RING = [P.sbuf(f"RING{i}", [128, 4096], BF16) for i in range(NSLOT)]
    TF = [P.sbuf(f"TF{i}", [128, 528], F32) for i in range(6)]
    TB = [P.sbuf(f"TB{i}", [128, 512], BF16) for i in range(4)]
    XN = [P.sbuf(f"XN{i}", [128, D], BF16) for i in range(3)]
    G1B = P.sbuf("G1B", [128, D], F32)
    G2B = P.sbuf("G2B", [128, D], F32)
    CV = P.sbuf("CV", [128, 108], F32)
    GS = P.sbuf("GS", [128, 512], F32)
    GF = P.sbuf("GF", [128, D], F32)
    SGB = P.sbuf("SGB", [128, 1024], BF16)
    WST = P.sbuf("WST", [128, 512], BF16)
    WSTS = P.sbuf("WSTS", [128, 512], BF16)
    IDENT = P.sbuf("IDENT", [128, 128], BF16)
    ONES1 = P.sbuf("ONES1", [128, 128], BF16)
    NEGH = P.sbuf("NEGH", [128, 1], F32)
    INVC = P.sbuf("INVC", [128, 16], F32)
    CARRY = P.sbuf("CARRY", [128, 4, 16], F32)
    HALO = P.sbuf("HALO", [128, NFF, 2], F32)
    CS = P.sbuf("CS", [128, NFF, 16, 2], F32)
    STAT = P.sbuf("STAT", [128, 216], F32)
    TMP16 = P.sbuf("TMP16", [128, 16], F32)
    SCR = P.sbuf("SCR", [128, 4], F32)
    SCR2 = P.sbuf("SCR2", [128, 32], F32)
    PS = [P.psum(f"ps{i}", [128, 512], F32) for i in range(8)]

    O_HT, O_YP, O_U, O_VT, O_DD = 0, 9216, 13824, 18432, 23040

    def HT(k, c0, c1):
        return R[:, O_HT + k * TW + c0: O_HT + k * TW + c1]

    def YP(g, c0, c1):
        return R[:, O_YP + g * TW + c0: O_YP + g * TW + c1]

    def U(h, c0, c1):
        return R[:, O_U + h * TW + c0: O_U + h * TW + c1]

    def VT(i, f0, f1):
        return R[:, O_VT + i * 512 + f0: O_VT + i * 512 + f1]

    def DD(b, c0, c1):
        return R[:, O_DD + b * TW + c0: O_DD + b * TW + c1]

    def FF(j, c0, c1):
        return R[:, j * TW + c0: j * TW + c1]

    REG = "REG"

    def MM(out, lhsT, rhs, start, stop, reads, writes):
        P.add("pe", lambda e: e.matmul(out, lhsT=lhsT, rhs=rhs, start=start, stop=stop), reads, writes)

    def TR(out, in_, reads, writes):
        P.add("pe", lambda e: e.transpose(out, in_, IDENT[:]), list(reads) + ["IDENT"], writes)

    def ACT(out, in_, func, reads, writes, **kw):
        P.add("act", lambda e: e.activation(out=out, in_=in_, func=func, **kw), reads, writes)

    def TT(eng, out, in0, in1, op, reads, writes):
        P.add(eng, lambda e: e.tensor_tensor(out=out, in0=in0, in1=in1, op=op), reads, writes)

    def TS(eng, out, in0, s1, s2, op0, op1, reads, writes):
        if s2 is None:
            P.add(eng, lambda e: e.tensor_scalar(out=out, in0=in0, scalar1=s1, scalar2=None, op0=op0),
                  reads, writes)
        else:
            P.add(eng, lambda e: e.tensor_scalar(out=out, in0=in0, scalar1=s1, scalar2=s2, op0=op0, op1=op1),
                  reads, writes)

    def STT(out, in0, scalar, in1, op0, op1, reads, writes):
        P.add("dve", lambda e: e.scalar_tensor_tensor(out=out, in0=in0, scalar=scalar, in1=in1, op0=op0, op1=op1),
              reads, writes)

    def CP(eng, out, in_, reads, writes):
        P.add(eng, lambda e: e.tensor_copy(out=out, in_=in_), reads, writes)

    def MSET(eng, ap, val, writes):
        P.add(eng, lambda e: e.memset(ap, val), (), writes)

    def DMA(eng, out, in_, sem, reads, writes):
        P.add(eng, lambda e: e.dma_start(out=out, in_=in_), reads, writes, dma_sem=sem)

    cnt = {"ps": 0, "tf": 0, "tb": 0, "xn": 0, "stat": 0}

    def psb():
        b = cnt["ps"] % 8
        cnt["ps"] += 1
        return PS[b], ("ps", b)

    def tf():
        b = cnt["tf"] % len(TF)
        cnt["tf"] += 1
        return TF[b], ("TF", b)

    def tb():
        b = cnt["tb"] % len(TB)
        cnt["tb"] += 1
        return TB[b], ("TB", b)

    def xn():
        b = cnt["xn"] % 3
        cnt["xn"] += 1
        return XN[b], ("XN", b)

    def stat3():
        c = cnt["stat"]
        cnt["stat"] += 3
        assert c + 3 <= 216
        return [(STAT[:, c + i: c + i + 1], ("ST", c + i)) for i in range(3)]

    def tile_list():
        tl = []
        tl.append(("win0", win[:, 0:4096], 4096))
        tl.append(("win1", win[:, 4096:8192], 4096))
        tl.append(("poolw", poolw[:, 0:512], 512))
        tl.append(("win2", win[:, 8192:12288], 4096))
        for cg in range(2):
            tl.append((f"pso{cg}", wpso[:, cg * 4096:(cg + 1) * 4096], 4096))
            tl.append((f"wga{cg}", win[:, (3 + cg) * 4096:(4 + cg) * 4096], 4096))
            tl.append((f"wgb{cg}", win[:, (5 + cg) * 4096:(6 + cg) * 4096], 4096))
        tl.append(("wo0", wo[:, 0:4096], 4096))
        tl.append(("wo1", wo[:, 4096:8192], 4096))
        for jg in range(6):
            ncol = 512 if jg < 5 else 256
            tl.append((f"wup{jg}", wup[:, jg * 4096: jg * 4096 + 8 * ncol], 8 * ncol))
            tl.append((f"wgt{jg}", wgt[:, jg * 4096: jg * 4096 + 8 * ncol], 8 * ncol))
        for half in range(2):
            for kg in range(3):
                k0, k1 = 8 * kg, min(8 * kg + 8, NFF)
                tl.append((f"wdn{half}_{kg}",
                           wdn[:, (half * NFF + k0) * 512:(half * NFF + k1) * 512], (k1 - k0) * 512))
        return tl

    st_tiles = tile_list()
    NT_ST = len(st_tiles)
    all_tiles = st_tiles + st_tiles
    ring_state = {"next_load": 0, "cursor": 0}

    def ring_issue(t):
        if t >= len(all_tiles):
            return
        name, src, n = all_tiles[t]
        s = t % NSLOT
        DMA("pool", RING[s][:, 0:n], src, f"w{s}", (), [("w", s)])

    def ring_take(name):
        t = ring_state["cursor"]
        assert all_tiles[t][0] == name, (all_tiles[t][0], name)
        ring_state["cursor"] += 1
        s = t % NSLOT
        return t, RING[s], ("w", s)

    deferred = []

    def ring_release(t, defer=False):
        if defer:
            deferred.append(t + NSLOT)
        else:
            ring_issue(t + NSLOT)

    def flush_deferred():
        for t in deferred:
            ring_issue(t)
        deferred.clear()

    early_x0 = set()
    MSET("pool", NEGH[:], -0.5, ["NEGH"])
    ACT(TMP16[:, 15:16], NEGH[:], AF.Square, ["NEGH"], ["ACTWARM"])
    t0, k0 = tf()
    MSET("pool", t0[:, 0:128], 1.0, [k0])
    P.add("pool", lambda e: e.affine_select(out=IDENT[:], in_=t0[:, 0:128], pattern=[[-1, 128]],
                                            compare_op=ALU.is_equal, fill=0.0, base=0, channel_multiplier=1),
          [k0], ["IDENT"])
    DMA("sp", X[:, 0, :], xp[0:128, :], "x0", (), [("X", 0, 0), ("X", 0, 1)])
    DMA("sp", G1B[:], g1bc, "c10", (), ["G1B"])
    early_x0.add(0)
    for i in range(1, 8):
        DMA("sp", X[:, i, :], xp[128 * i: 128 * i + 128, :], f"x{i}", (), [("X", i, 0), ("X", i, 1)])
        early_x0.add(i)
    ring_issue(0)
    for t in range(1, NSLOT):
        deferred.append(t)
    MSET("pool", ONES1[:], 1.0 / 128.0, ["ONES1"])
    MSET("pool", CARRY[:], 0.0, [("CARRY", g) for g in range(4)])
    MSET("pool", HALO[:], 0.0, [("HALO", j) for j in range(NFF)])
    MSET("pool", SCR[:], 0.0, ["SCR"])
    for t in range(15):
        MSET("pool", INVC[:, t:t + 1], 1.0 / (t + 1), ["INVC"])
    DMA("sp", CV[:], cvec, "c0", (), ["CV"])

    def late_setup():
        DMA("sp", GS[:], gsgu, "c1", (), ["GS"])
        for src, dst, key, sem in ((wst_d, WST, "WST", "c3"), (wsts_d, WSTS, "WSTS", "c4")):
            t1, k1 = tf()
            DMA("sp", t1[:, 0:512], src, sem, (), [k1])
            P.add("pool", lambda e, t1=t1, dst=dst: e.affine_select(
                out=dst[:].rearrange("p (h t) -> p h t", h=4),
                in_=t1[:, 0:512].rearrange("p (h t) -> p h t", h=4),
                pattern=[[0, 4], [1, 128]], compare_op=ALU.is_ge, fill=0.0, base=0, channel_multiplier=-1),
                [k1], [key])
        for half in range(2):
            t1, k1 = tf()
            DMA("sp", t1[:, 0:512], sgub[:, half * 512:(half + 1) * 512], f"c{5 + half}", (), [k1])
            CP("dve", SGB[:, half * 512:(half + 1) * 512], t1[:, 0:512], [k1], ["SGB"])

    def late_bulk(gate):
        DMA("sp", G2B[:], g2bc, "c11", gate, ["G2B"])
        DMA("sp", X[:, 8, :], xs, "x8", (), [("X", 8, 0), ("X", 8, 1)])
        DMA("sp", GF[:], gfin, "c2", (), ["GF"])
        DMA("sp", PEXTS[:].rearrange("p g q t -> p (g q t)"), spfm, "c7", (), [("PEXTS", g) for g in range(4)])
        DMA("sp", CS[:].rearrange("p j q r -> p (j q r)"), scfm, "c8", (), [("CS", j) for j in range(NFF)])

    early_x = set()
    pending_reload = []

    def reload_x(i):
        DMA("sp", X[:, i, :], xp[1024 + 128 * i: 1024 + 128 * i + 128, :], f"x{i}", (),
            [("X", i, 0), ("X", i, 1)])
        early_x.add(i)

    G1 = lambda k: CV[:, k:k + 1]
    G2 = lambda k: CV[:, 8 + k:9 + k]
    PSC = lambda g: CV[:, 16 + g:17 + g]
    CW = lambda r, j: CV[:, 20 + r * NFF + j: 21 + r * NFF + j]
    CB = lambda j: CV[:, 86 + j: 87 + j]

    def norm_stats(i):
        xk = [("X", i, 0), ("X", i, 1)]
        (ss, kss), (ms, kms), (rs, krs) = stat3()
        xb, kx = xn()
        ACT(xb[:], X[:, i, :], AF.Square, xk, [kx, kss], accum_out=ss)
        TS("pool", ms, ss, 1.0 / D, EPS, ALU.mult, ALU.add, [kss], [kms])
        TT("pool", rs, ms, NEGH[:], ALU.pow, [kms, "NEGH"], [krs])
        return xb, kx, rs, krs

    def norm_scale(i, xb, kx, rs, krs, GBC, gkey):
        xk = [("X", i, 0), ("X", i, 1)]
        STT(xb[:], X[:, i, :], rs, GBC[:], ALU.mult, ALU.mult, xk + [krs, gkey], [kx])

    def norm_pe(eng, xb, kx, dst3d, dst_keys, extra):
        ps, kp = psb()
        psv = ps[:].bitcast(BF16)
        for k in range(8):
            TR(psv[:, k * 128:(k + 1) * 128], xb[:, k * 128:(k + 1) * 128], [kx], [kp])
        src = psv.rearrange("p (k t) -> p k t", k=8)
        if eng == "act":
            ACT(dst3d, src, AF.Identity, [kp] + extra, dst_keys)
        else:
            CP("dve", dst3d, src, [kp] + extra, dst_keys)

    s0 = {"s": 0, "nst": {}}
    order1 = [8, 0, 1, 2, 3, 4, 5, 6, 7]

    def s0_step():
        sidx = s0["s"]
        n = len(order1)
        if sidx < n:
            t = order1[sidx]
            s0["nst"][t] = norm_stats(t)
        if 0 <= sidx - 1 < n:
            t = order1[sidx - 1]
            norm_scale(t, *s0["nst"][t], G1B, "G1B")
        if 0 <= sidx - 2 < n:
            t = order1[sidx - 2]
            pxb, pkx = s0["nst"].pop(t)[:2]
            norm_pe("act" if sidx % 2 == 0 else "dve", pxb, pkx, MH[:, :, 128 * t:128 * t + 128],
                    [("MH", k, t) for k in range(8)], [])
        s0["s"] += 1

    def s0_done():
        return s0["s"] >= len(order1) + 2

    def pool_adds(ext, ek, Lx, w, three_d, add_eng="pool"):
        S = (lambda ap, a, b: ap[:, :, a:b]) if three_d else (lambda ap, a, b: ap[:, a:b])

        def view(t):
            if three_d:
                return t[:, 0:16 * Lx].rearrange("p (q t) -> p q t", q=16)
            return t[:, 0:Lx]
        cur, ck = ext, list(ek)
        have, vf = 1, 0
        while have < w:
            tbuf, tk = tf()
            nxt = view(tbuf)
            lo = vf + have
            TT(add_eng, S(nxt, lo, Lx), S(cur, lo, Lx), S(cur, lo - have, Lx - have), ALU.add, ck, [tk])
            vf = lo
            have *= 2
            cur, ck = nxt, [tk]
        return cur, ck, S

    def pool_finish(state, ext, ek, Hh, T, w, dd_out, dd_keys, fix_first):
        cur, ck, S = state
        STT(dd_out, S(cur, Hh, Hh + T), 1.0 / w, S(ext, Hh, Hh + T), ALU.mult, ALU.subtract,
            ck + list(ek) + [REG], dd_keys)
        if fix_first and w > 1:
            n = w - 1
            TT("dve", TMP16[:, 0:n], cur[:, Hh:Hh + n], INVC[:, 0:n], ALU.mult, ck + ["INVC"], ["TMP16"])
            TT("dve", dd_out[:, 0:n], TMP16[:, 0:n], ext[:, Hh:Hh + n], ALU.subtract,
               ["TMP16"] + list(ek) + [REG], dd_keys)

    GELU = AF.Gelu_apprx_tanh
    Uall = R[:, O_U:O_U + 4 * TW].rearrange("p (h t) -> p h t", h=4)

    P.marks = []

    def stage(n):
        P.marks.append((n, sum(1 for o in P.ops if o.eng == "pe")))
        if STOP_AT is not None and n >= STOP_AT:
            raise _Stop()

    try:
      for st in range(2):
        stage(10 * st + 0)
        ntile = 8 if st == 0 else 9
        tiles = list(range(ntile))
        blocks = [(0, 512, [0, 1, 2, 3], False), (512, 512, [4, 5, 6, 7], False)]
        if st == 1:
            blocks.append((1024, 128, [8], True))

        def col(i):
            return 128 * i, 128 * i + 128

        if st > 0:
            DMA("sp", SCR2[:, 0:16], cvec[:, 0:16], "bar", (), [REG])

        HT3 = R[:, O_HT:O_HT + 8 * TW].rearrange("p (k t) -> p k t", k=8)
        if st == 0:
            for i in tiles:
                if i == 8 or i in early_x0:
                    continue
                DMA("sp", X[:, i, :], xp[128 * i: 128 * i + 128, :], f"x{i}", (), [("X", i, 0), ("X", i, 1)])
            nst = {}
            for it in range(ntile + 2):
                if it < ntile:
                    nst[it] = norm_stats(it)
                if 0 <= it - 1 < ntile:
                    norm_scale(it - 1, *nst[it - 1], G1B, "G1B")
                if 0 <= it - 2 < ntile:
                    pi = it - 2
                    c0, c1 = col(pi)
                    pxb, pkx = nst.pop(pi)[:2]
                    norm_pe("act" if pi % 2 == 0 else "dve", pxb, pkx, HT3[:, :, c0:c1],
                            [("HT", k, pi) for k in range(8)], [REG])
        else:
            while not s0_done():
                s0_step()
            for bi, (c0, L, tl, smp) in enumerate(blocks):
                rk = [("MH", k, i) for k in range(8) for i in tl] + [REG]
                wk = [("HT", k, i) for k in range(8) for i in tl]
                if bi == 1:
                    ACT(HT3[:, :, c0:c0 + L], MH[:, :, c0:c0 + L], AF.Identity, rk, wk)
                else:
                    CP("dve", HT3[:, :, c0:c0 + L], MH[:, :, c0:c0 + L], rk, wk)

        if st == 0:
            late_setup()
            for t in deferred[:2]:
                ring_issue(t)
            del deferred[:2]
        stage(10 * st + 1)
        t_w0, W0, kw0 = ring_take("win0")
        W0v = W0[:].rearrange("p (k c) -> p k c", k=8)

        def emit_A2(g):
            db = g % 2
            for bi, (c0, L, tl, smp) in enumerate(blocks):
                ps, kp = psb()
                MM(ps[:, 0:L], PW[:, g * 128:(g + 1) * 128], DD(db, c0, c0 + L), True, True,
                   [kpw, ("DD", db, bi), REG], [kp])
                TS("dve", YP(g, c0, c0 + L), ps[:, 0:L], PSC(g), None, ALU.mult, None,
                   [kp, "CV", REG], [("YP", g, bi)])

        def emit_A1(g):
            w = POOL_W[g]
            CP("pool", PEXT[:, 0:16], CARRY[:, g, :], [("CARRY", g)], [("PEXT", "h")])
            for bi, (c0, L, tl, smp) in enumerate(blocks):
                ps, kp = psb()
                for k in range(8):
                    MM(ps[:, 0:L], W0v[:, k, g * 128:(g + 1) * 128], HT(k, c0, c0 + L), k == 0, k == 7,
                       [kw0, REG] + [("HT", k, i) for i in tl], [kp])
                if smp:
                    ACT(PEXTS[:, g, :, 15:23], ps[:, 0:128].rearrange("p (q t) -> p q t", q=16), AF.Identity,
                        [kp], [("PEXTS", g)])
                else:
                    ACT(PEXT[:, 16 + c0:16 + c0 + L], ps[:, 0:L], AF.Identity, [kp], [("PEXT", bi)])
            db = g % 2
            pk = [("PEXT", "h"), ("PEXT", 0), ("PEXT", 1)]
            states = {}
            for bi, (c0, L, tl, smp) in enumerate(blocks):
                if not smp:
                    states[bi] = pool_adds(PEXT[:, c0:c0 + 528], pk, 528, w, False,
                                           add_eng=("pool" if bi == 0 else "dve"))
            for bi, (c0, L, tl, smp) in enumerate(blocks):
                if not smp:
                    pool_finish(states[bi], PEXT[:, c0:c0 + 528], pk, 16, 512, w,
                                DD(db, c0, c0 + L), [("DD", db, bi)], st == 0 and bi == 0)
            for bi, (c0, L, tl, smp) in enumerate(blocks):
                if smp:
                    stt_ = pool_adds(PEXTS[:, g, :, :], [("PEXTS", g)], 23, w, True)
                    pool_finish(stt_, PEXTS[:, g, :, :], [("PEXTS", g)], 15, 8, w,
                                DD(db, c0, c0 + L).rearrange("p (q t) -> p q t", q=16), [("DD", db, bi)], False)
            CP("pool", CARRY[:, g, :], PEXT[:, 1024:1040], [("PEXT", 1)], [("CARRY", g)])

        def emit_A3(h):
            for bi, (c0, L, tl, smp) in enumerate(blocks):
                ps, kp = psb()
                for k in range(8):
                    MM(ps[:, 0:L], W1v[:, k, h * 128:(h + 1) * 128], HT(k, c0, c0 + L), k == 0, k == 7,
                       [kw1, REG] + [("HT", k, i) for i in tl], [kp])
                ACT(U(h, c0, c0 + L), ps[:, 0:L], GELU, [kp, REG], [("U", h, i) for i in tl])

        emit_A1(0)
        flush_deferred()
        emit_A1(1)
        t_w1, W1, kw1 = ring_take("win1")
        W1v = W1[:].rearrange("p (k c) -> p k c", k=8)
        emit_A3(0)
        t_pw, PW, kpw = ring_take("poolw")
        emit_A2(0)
        emit_A1(2)
        emit_A3(1)
        emit_A2(1)
        emit_A1(3)
        if st == 1:
            DMA("sp", npp, CARRY[:].rearrange("p g t -> p (g t)"), "o_npp", [("CARRY", g) for g in range(4)], [])
            DMA("sp", nps, PEXTS[:].rearrange("p g q t -> p (g q t)"), "o_nps", [("PEXTS", g) for g in range(4)], [])
        ring_release(t_w0)
        emit_A3(2)
        emit_A2(2)
        emit_A3(3)
        ring_release(t_w1)

        stage(10 * st + 2)
        stage(10 * st + 3)
        t_w2, W2, kw2 = ring_take("win2")
        W2v = W2[:].rearrange("p (k c) -> p k c", k=8)
        def emit_A5(i):
            c0, c1 = col(i)
            smp = (i == 8)
            WS, kws = (WSTS, "WSTS") if smp else (WST, "WST")
            boff = 512 if smp else 0
            ps, kp = psb()
            for h in range(4):
                MM(ps[:, h * 128:(h + 1) * 128], VT(i, h * 128, (h + 1) * 128), WS[:, h * 128:(h + 1) * 128],
                   True, False, [("VT", i), kws, REG], [kp])
                MM(ps[:, h * 128:(h + 1) * 128], ONES1[:], SGB[:, boff + h * 128: boff + (h + 1) * 128],
                   False, True, ["ONES1", "SGB"], [kp])
            uk = [("U", h, i) for h in range(4)]
            TT("dve", Uall[:, :, c0:c1], Uall[:, :, c0:c1], ps[:, :].rearrange("p (h t) -> p h t", h=4), ALU.mult,
               [kp, REG] + uk, uk)

        a4 = {}

        def a4_front(i):
            c0, c1 = col(i)
            ps, kp = psb()
            for k in range(8):
                MM(ps[:, :], HT(k, c0, c1), W2v[:, k, :], k == 0, k == 7, [kw2, REG, ("HT", k, i)], [kp])
            vg, kvg = tf()
            ACT(vg[:, 0:512], ps[:, :], GELU, [kp], [kvg])
            (ss, kss), (ms, kms), (rs, krs) = stat3()
            jb, kj = tb()
            P.add("dve", lambda e, jb=jb, vg=vg, ss=ss: e.scalar_tensor_tensor(
                out=jb[:], in0=vg[:, 0:512], scalar=1.0, in1=vg[:, 0:512],
                op0=ALU.mult, op1=ALU.mult, accum_out=ss), [kvg], [kj, kss])
            TS("pool", ms, ss, 1.0 / 512, EPS, ALU.mult, ALU.add, [kss], [kms])
            TT("pool", rs, ms, NEGH[:], ALU.pow, [kms, "NEGH"], [krs])
            a4[i] = (vg, kvg, rs, krs)

        def a4_back(i):
            vg, kvg, rs, krs = a4.pop(i)
            if i == 8:
                STT(vg[:, 0:512], vg[:, 0:512], rs, GS[:], ALU.mult, ALU.mult, [kvg, krs, "GS"], [kvg])
                DMA("sp", nv, vg[:, 0:512], "o_nv", [kvg], [])
                CP("dve", VT(i, 0, 512), vg[:, 0:512], [kvg, REG], [("VT", i)])
            else:
                STT(VT(i, 0, 512), vg[:, 0:512], rs, GS[:], ALU.mult, ALU.mult, [kvg, krs, "GS", REG], [("VT", i)])

        for it in range(ntile + 3):
            if it == 2:
                emit_A2(3)
                ring_release(t_pw)
            if it < ntile:
                a4_front(it)
            if 0 <= it - 1 < ntile:
                a4_back(it - 1)
            if 0 <= it - 3 < ntile:
                emit_A5(it - 3)
        ring_release(t_w2)

        stage(10 * st + 4)
        stage(10 * st + 5)
        for cg in range(2):
            t_a, WPS, kwpo = ring_take(f"pso{cg}")
            kwso = kwpo
            t_c, WGA, kwga = ring_take(f"wga{cg}")
            t_d, WGB, kwgb = ring_take(f"wgb{cg}")
            WPOv = WPS[:, 0:2048].rearrange("p (k c) -> p k c", k=4)
            WSOv = WPS[:, 2048:4096].rearrange("p (k c) -> p k c", k=4)
            WGAv = WGA[:].rearrange("p (k c) -> p k c", k=8)
            WGBv = WGB[:].rearrange("p (k c) -> p k c", k=8)
            for c4 in range(4):
                c = cg * 4 + c4
                lo, hi = c4 * 128, (c4 + 1) * 128
                for bi, (c0, L, tl, smp) in enumerate(blocks):
                    psA, kA = psb()
                    psB, kB = psb()
                    psGa, kGa = psb()
                    psGb, kGb = psb()
                    for k in range(4):
                        MM(psA[:, 0:L], WPOv[:, k, lo:hi], YP(k, c0, c0 + L), k == 0, k == 3,
                           [kwpo, ("YP", k, bi), REG], [kA])
                    for k in range(4):
                        MM(psB[:, 0:L], WSOv[:, k, lo:hi], U(k, c0, c0 + L), k == 0, k == 3,
                           [kwso, REG] + [("U", k, i) for i in tl], [kB])
                    for k in range(8):
                        MM(psGa[:, 0:L], WGAv[:, k, lo:hi], HT(k, c0, c0 + L), k == 0, k == 7,
                           [kwga, REG] + [("HT", k, i) for i in tl], [kGa])
                    for k in range(8):
                        MM(psGb[:, 0:L], WGBv[:, k, lo:hi], HT(k, c0, c0 + L), k == 0, k == 7,
                           [kwgb, REG] + [("HT", k, i) for i in tl], [kGb])
                    sa, ksa = tb()
                    sb_, ksb = tb()
                    ACT(sa[:, 0:L], psGa[:, 0:L], AF.Sigmoid, [kGa], [ksa])
                    ACT(sb_[:, 0:L], psGb[:, 0:L], AF.Sigmoid, [kGb], [ksb])
                    t1, kt1 = tf()
                    t2, kt2 = tf()
                    TT("dve", t1[:, 0:L], sa[:, 0:L], psA[:, 0:L], ALU.mult, [ksa, kA], [kt1])
                    TT("dve", t2[:, 0:L], sb_[:, 0:L], psB[:, 0:L], ALU.mult, [ksb, kB], [kt2])
                    TT("pool", MH[:, c, c0:c0 + L], t1[:, 0:L], t2[:, 0:L], ALU.add, [kt1, kt2],
                       [("MH", c, i) for i in tl])
                    if st == 0 and c == 0 and bi == 0:
                        late_bulk([("MH", 0, 0)])
            ring_release(t_a)
            ring_release(t_c)
            ring_release(t_d)

        stage(10 * st + 6)
        t_o0, WO0, kwo0 = ring_take("wo0")
        t_o1, WO1, kwo1 = ring_take("wo1")
        WOv = [WO0[:].rearrange("p (k c) -> p k c", k=8), WO1[:].rearrange("p (k c) -> p k c", k=8)]
        kwo = [kwo0, kwo1]
        nst = {}
        for it in range(ntile + 2):
            if it < ntile:
                i = it
                c0, c1 = col(i)
                for half in range(2):
                    ps, kp = psb()
                    for k in range(8):
                        MM(ps[:, :], MH[:, k, c0:c1], WOv[half][:, k, :], k == 0, k == 7,
                           [kwo[half], ("MH", k, i)], [kp])
                    xh = X[:, i, half * 512:(half + 1) * 512]
                    TT("dve", xh, xh, ps[:, :], ALU.add, [kp, ("X", i, half)], [("X", i, half)])
            if it < ntile:
                nst[it] = norm_stats(it)
            if 0 <= it - 1 < ntile:
                norm_scale(it - 1, *nst[it - 1], G2B, "G2B")
            if 0 <= it - 2 < ntile:
                pi = it - 2
                c0, c1 = col(pi)
                pxb, pkx = nst.pop(pi)[:2]
                norm_pe("act" if pi % 2 == 0 else "dve", pxb, pkx, MH[:, :, c0:c1],
                        [("MH", k, pi) for k in range(8)], [])
        ring_release(t_o0)
        ring_release(t_o1)

        stage(10 * st + 7)
        stage(10 * st + 8)
        DMA("sp", SCR2[:, 16:32], cvec[:, 16:32], "bar", (), [REG])
        b2_prev = [None]

        def b2_back(cv, gsrc, fout, kc1, kg, fkeys):
            ACT(cv, cv, GELU, [kc1], [kc1])
            TT("dve", fout, cv, gsrc, ALU.mult, [kc1, kg, REG], fkeys)

        for jg in range(6):
            ncol = 512 if jg < 5 else 256
            t_u, WU, kwu = ring_take(f"wup{jg}")
            t_g, WG, kwg = ring_take(f"wgt{jg}")
            WUv = WU[:, 0:8 * ncol].rearrange("p (k c) -> p k c", k=8)
            WGv = WG[:, 0:8 * ncol].rearrange("p (k c) -> p k c", k=8)
            for j4 in range(ncol // 128):
                j = jg * 4 + j4
                lo, hi = j4 * 128, (j4 + 1) * 128
                for bi, (c0, L, tl, smp) in enumerate(blocks):
                    psa, ka = psb()
                    psg, kg = psb()
                    for k in range(8):
                        MM(psa[:, 0:L], WUv[:, k, lo:hi], MH[:, k, c0:c0 + L], k == 0, k == 7,
                           [kwu] + [("MH", k, i) for i in tl], [ka])
                    for k in range(8):
                        MM(psg[:, 0:L], WGv[:, k, lo:hi], MH[:, k, c0:c0 + L], k == 0, k == 7,
                           [kwg] + [("MH", k, i) for i in tl], [kg])
                    ae, kae = tf()
                    c1b, kc1 = tf()
                    if smp:
                        ae3 = ae[:, 0:160].rearrange("p (q t) -> p q t", q=16)
                        ACT(ae3[:, :, 2:10], psa[:, 0:128].rearrange("p (q t) -> p q t", q=16), AF.Identity,
                            [ka], [kae])
                        CP("pool", ae3[:, :, 0:2], CS[:, j, :, :], [("CS", j), kae], [kae])
                        CP("pool", CS[:, j, :, :], ae3[:, :, 8:10], [kae], [("CS", j)])
                        a0, a1, a2 = ae3[:, :, 0:8], ae3[:, :, 1:9], ae3[:, :, 2:10]
                        cv = c1b[:, 0:128].rearrange("p (q t) -> p q t", q=16)
                        gsrc = psg[:, 0:128].rearrange("p (q t) -> p q t", q=16)
                        fout = FF(j, c0, c0 + L).rearrange("p (q t) -> p q t", q=16)
                    else:
                        ACT(ae[:, 2:2 + L], psa[:, 0:L], AF.Identity, [ka], [kae])
                        CP("pool", ae[:, 0:2], HALO[:, j, :], [("HALO", j), kae], [kae])
                        CP("pool", HALO[:, j, :], ae[:, L:L + 2], [kae], [("HALO", j)])
                        a0, a1, a2 = ae[:, 0:L], ae[:, 1:L + 1], ae[:, 2:L + 2]
                        cv = c1b[:, 0:L]
                        gsrc = psg[:, 0:L]
                        fout = FF(j, c0, c0 + L)
                    TS("pool", cv, a0, CW(0, j), CB(j), ALU.mult, ALU.add, [kae, "CV"], [kc1])
                    STT(cv, a1, CW(1, j), cv, ALU.mult, ALU.add, [kae, kc1, "CV"], [kc1])
                    STT(cv, a2, CW(2, j), cv, ALU.mult, ALU.add, [kae, kc1, "CV"], [kc1])
                    if b2_prev[0] is not None:
                        b2_back(*b2_prev[0])
                    b2_prev[0] = (cv, gsrc, fout, kc1, kg, [("F", j, i) for i in tl])
            ring_release(t_u)
            ring_release(t_g)

        if b2_prev[0] is not None:
            b2_back(*b2_prev[0])
            b2_prev[0] = None
        if st == 1:
            DMA("sp", ncp, HALO[:].rearrange("p j r -> p (j r)"), "o_ncp", [("HALO", j) for j in range(NFF)], [])
            DMA("sp", ncs, CS[:].rearrange("p j q r -> p (j q r)"), "o_ncs", [("CS", j) for j in range(NFF)], [])
        stage(10 * st + 9)
        for half in range(2):
            wd = [ring_take(f"wdn{half}_{kg}") for kg in range(3)]
            for i in tiles:
                c0, c1 = col(i)
                ps, kp = psb()
                for k in range(NFF):
                    t_, Wd, kwd = wd[k // 8]
                    nk = 8 if k // 8 < 2 else NFF - 16
                    Wdv = Wd[:, 0:nk * 512].rearrange("p (k c) -> p k c", c=512)
                    MM(ps[:, :], FF(k, c0, c1), Wdv[:, k % 8, :], k == 0, k == NFF - 1,
                       [kwd, ("F", k, i), REG], [kp])
                xh = X[:, i, half * 512:(half + 1) * 512]
                TT("dve", xh, xh, ps[:, :], ALU.add, [kp, ("X", i, half)], [("X", i, half)])
                if half == 1:
                    xk = [("X", i, 0), ("X", i, 1)]
                    (ss, kss), (ms, kms), (rs, krs) = stat3()
                    jt, kjt = tf()
                    ACT(jt[:].bitcast(BF16)[:, 0:D], X[:, i, :], AF.Square, xk, [kjt, kss], accum_out=ss)
                    TS("pool", ms, ss, 1.0 / D, EPS, ALU.mult, ALU.add, [kss], [kms])
                    TT("pool", rs, ms, NEGH[:], ALU.pow, [kms, "NEGH"], [krs])
                    dst = ys if i == 8 else yp[st * 1024 + 128 * i: st * 1024 + 128 * i + 128, :]
                    if st == 0 and i >= 6:
                        for hh in range(2):
                            to, kto = tf()
                            STT(to[:, 0:512], X[:, i, hh * 512:(hh + 1) * 512], rs, GF[:, hh * 512:(hh + 1) * 512],
                                ALU.mult, ALU.mult, xk + [krs, "GF"], [kto])
                            DMA("sp", dst[:, hh * 512:(hh + 1) * 512], to[:, 0:512], f"y{i}_{hh}", [kto], [])
                    else:
                        STT(X[:, i, :], X[:, i, :], rs, GF[:], ALU.mult, ALU.mult, xk + [krs, "GF"], xk)
                        DMA("sp", dst, X[:, i, :], f"y{i}", xk, [])
                    if st == 0:
                        if 1 <= i <= 5:
                            reload_x(i - 1)
                        elif i == 6:
                            reload_x(5)
                            reload_x(6)
                        elif i == 7:
                            reload_x(7)
                        nxt = order1[s0["s"]] if s0["s"] < len(order1) else None
                        if nxt is None or nxt == 8 or nxt <= i - 2:
                            s0_step()
            for t_, _, _ in wd:
                ring_release(t_, defer=(half == 1))

    except _Stop:
        pass
    assert STOP_AT is not None or ring_state["cursor"] == len(all_tiles)
    P.finalize()
    P.sbuf_left = nc.sbuf_bytes_remaining
    P.close()
    return nc, P


def _tile_w(W, nk, tw):
    N = W.shape[1]
    Wp = W.reshape(nk, 128, N).transpose(1, 0, 2)
    parts = []
    for c0 in range(0, N, tw):
        w = min(tw, N - c0)
        parts.append(Wp[:, :, c0:c0 + w].reshape(128, nk * w))
    return np.ascontiguousarray(np.concatenate(parts, axis=1), dtype=np.float32)


_CACHE = {}


def kernel(**inputs):
    f = lambda k: np.asarray(inputs[k], dtype=np.float32)
    x_prompt, x_sample = f("x_prompt"), f("x_sample")
    state_pool, state_conv = f("state_pool"), f("state_ffn_conv")
    w_in = f("w_in")[0]
    shared = {}
    shared["win"] = _tile_w(w_in, 8, 512)
    wpo_t = _tile_w(f("w_pool_out")[0], 4, 512).reshape(128, 2, 2048)
    wso_t = _tile_w(f("w_sgu_out")[0], 4, 512).reshape(128, 2, 2048)
    shared["wpso"] = np.ascontiguousarray(np.concatenate([wpo_t, wso_t], axis=2).reshape(128, 2 * 4096))
    shared["poolw"] = np.ascontiguousarray(f("pool_w")[0].transpose(1, 0, 2).reshape(128, 512))
    shared["wo"] = _tile_w(f("w_o")[0], 8, 512)
    shared["wup"] = _tile_w(f("ffn_w_up")[0], 8, 512)
    shared["wgt"] = _tile_w(f("ffn_w_gate")[0], 8, 512)
    shared["wdn"] = _tile_w(f("ffn_w_down")[0], NFF, 512)
    cvec = np.zeros((128, 108), np.float32)
    cvec[:, 0:8] = f("norm1_g")[0].reshape(8, 128).T
    cvec[:, 8:16] = f("norm2_g")[0].reshape(8, 128).T
    cvec[:, 16:20] = f("pool_scale")[0].reshape(4, 128).T
    cvec[:, 20:86] = f("ffn_conv_w")[0].reshape(3, NFF, 128).transpose(2, 0, 1).reshape(128, 66)
    cvec[:, 86:108] = f("ffn_conv_b")[0].reshape(NFF, 128).T
    shared["cvec"] = cvec
    shared["gsgu"] = np.ascontiguousarray(np.broadcast_to(f("sgu_norm_g")[0][None, :], (128, 512)))
    shared["gfin"] = np.ascontiguousarray(np.broadcast_to(f("final_norm_g")[None, :], (128, D)))
    shared["g1bc"] = np.ascontiguousarray(np.broadcast_to(f("norm1_g")[0][None, :], (128, D)))
    shared["g2bc"] = np.ascontiguousarray(np.broadcast_to(f("norm2_g")[0][None, :], (128, D)))
    sgu_b = f("sgu_b")[0]
    sgub = np.zeros((1, 1024), np.float32)
    sgub[0, 0:512] = sgu_b.reshape(512)
    sgub[0, 512:1024] = np.tile(sgu_b[:, :8], (1, 16)).reshape(512)
    shared["sgub"] = np.ascontiguousarray(np.broadcast_to(sgub, (128, 1024)))
    sgu_w = f("sgu_w")[0]
    shared["wst"] = np.ascontiguousarray(sgu_w.transpose(2, 0, 1).reshape(128, 512))
    wsts = np.zeros((128, 4, 128), np.float32)
    blk = sgu_w[:, :8, :8].transpose(2, 0, 1)
    for q in range(16):
        wsts[8 * q:8 * q + 8, :, 8 * q:8 * q + 8] = blk
    shared["wsts"] = wsts.reshape(128, 512)

    in_maps = []
    for c in range(8):
        m = dict(shared)
        m["xp"] = np.ascontiguousarray(x_prompt[c])
        m["xs"] = np.ascontiguousarray(x_sample[16 * c:16 * c + 16].reshape(128, D))
        sp = state_pool[0, 16 * c:16 * c + 16]
        spf = np.zeros((128, 4, 16, 23), np.float32)
        spf[:, :, :, 0:15] = sp.reshape(16, 15, 4, 128).transpose(3, 2, 0, 1)
        m["spfm"] = spf.reshape(128, 4 * 16 * 23)
        sc = state_conv[0, 16 * c:16 * c + 16]
        m["scfm"] = np.ascontiguousarray(sc.reshape(16, 2, NFF, 128).transpose(3, 2, 0, 1).reshape(128, NFF * 32))
        in_maps.append(m)

    if "nc" not in _CACHE:
        _CACHE["nc"] = build_program()[0]
    nc = _CACHE["nc"]
    res = run_bass_kernel_spmd(nc, in_maps, core_ids=list(range(8)))
    rs = res.results

    y_prompt = np.zeros((8, 2048, D), np.float32)
    y_sample = np.zeros((128, 8, D), np.float32)
    new_pool_prompt = np.zeros((1, 8, 15, 512), np.float32)
    new_pool_sample = np.zeros((1, 128, 15, 512), np.float32)
    new_conv_prompt = np.zeros((1, 8, 2, DFF), np.float32)
    new_conv_sample = np.zeros((1, 128, 2, DFF), np.float32)
    new_v = np.zeros((1, 128, 8, 512), np.float32)
    for c in range(8):
        r = rs[c]
        sl = slice(16 * c, 16 * c + 16)
        y_prompt[c] = r["yp"]
        y_sample[sl] = r["ys"].reshape(16, 8, D)
        new_pool_prompt[0, c] = r["npp"].reshape(128, 4, 16)[:, :, 1:16].transpose(2, 1, 0).reshape(15, 512)
        new_pool_sample[0, sl] = r["nps"].reshape(128, 4, 16, 23)[:, :, :, 8:23].transpose(2, 3, 1, 0).reshape(16, 15, 512)
        new_conv_prompt[0, c] = r["ncp"].reshape(128, NFF, 2).transpose(2, 1, 0).reshape(2, DFF)
        new_conv_sample[0, sl] = r["ncs"].reshape(128, NFF, 16, 2).transpose(2, 3, 1, 0).reshape(16, 2, DFF)
        new_v[0, sl] = r["nv"].reshape(16, 8, 512)
    return (y_prompt, y_sample, new_pool_prompt, new_pool_sample, new_conv_prompt, new_conv_sample, new_v)
```

```python
import contextlib
import numpy as np
import concourse.bass as bass
import concourse.mybir as mybir
from concourse.bass_utils import run_bass_kernel_spmd

F32 = mybir.dt.float32
BF16 = mybir.dt.bfloat16
AF = mybir.ActivationFunctionType
ALU = mybir.AluOpType

D = 1024
DFF = 2816
NFF = 22
EPS = 1e-6
POOL_W = (2, 4, 8, 16)
NSLOT = 6
STOP_AT = None
SKIP = set()


class _Stop(Exception):
    pass
TW = 1152


class _Op:
    __slots__ = ("id", "eng", "emit", "reads", "writes", "dma", "sem", "target",
                 "deps", "ticket", "signal", "waits")


class Prog:
    ENGS = ("pe", "act", "dve", "pool", "sp")

    def __init__(self, nc):
        self.nc = nc
        self.stack = contextlib.ExitStack()
        self.ops = []
        self.last_writer = {}
        self.readers = {}
        self.cnt_sem = {e: self.stack.enter_context(nc.semaphore("c_" + e)) for e in self.ENGS}
        self.dma_count = {}
        self.dma_sems = {}

    def sbuf(self, name, shape, dt):
        return self.stack.enter_context(self.nc.sbuf_tensor(name, list(shape), dt))

    def psum(self, name, shape, dt):
        return self.stack.enter_context(self.nc.psum_tensor(name, list(shape), dt))

    def add(self, eng, emit, reads=(), writes=(), dma_sem=None):
        op = _Op()
        op.id = len(self.ops)
        op.eng = eng
        op.emit = emit
        op.dma = dma_sem is not None
        op.sem = None
        op.target = None
        op.signal = False
        op.ticket = None
        if op.dma:
            if dma_sem not in self.dma_sems:
                self.dma_sems[dma_sem] = self.stack.enter_context(self.nc.semaphore("d_" + dma_sem))
                self.dma_count[dma_sem] = 0
            self.dma_count[dma_sem] += 16
            op.sem = dma_sem
            op.target = self.dma_count[dma_sem]
        raw = set()
        other = set()
        for k in reads:
            w = self.last_writer.get(k)
            if w is not None:
                raw.add(w)
        for k in writes:
            w = self.last_writer.get(k)
            if w is not None:
                other.add(w)
            rd = self.readers.get(k)
            if rd is not None:
                other.update(rd[0].values())
                other.update(rd[1])
        best = {}
        dmas = set()
        for pid in raw | other:
            p = self.ops[pid]
            if p.dma:
                dmas.add(pid)
                continue
            if (not op.dma) and p.eng == eng:
                if eng == "pe" or (pid not in raw and eng != "pool"):
                    continue
            if best.get(p.eng, -1) < pid:
                best[p.eng] = pid
        op.deps = list(best.values()) + list(dmas)
        for pid in best.values():
            self.ops[pid].signal = True
        for k in reads:
            rd = self.readers.setdefault(k, ({}, []))
            if op.dma:
                rd[1].append(op.id)
            else:
                rd[0][eng] = op.id
        for k in writes:
            self.last_writer[k] = op.id
            self.readers[k] = ({}, [])
        self.ops.append(op)
        return op

    def finalize(self):
        ops = self.ops
        tick = {e: 0 for e in self.ENGS}
        for op in ops:
            if op.signal and not op.dma:
                tick[op.eng] += 1
                op.ticket = tick[op.eng]
        seen = {e: {} for e in self.ENGS}
        for op in ops:
            w = {}
            for pid in op.deps:
                p = ops[pid]
                if p.dma:
                    key, val = ("d", p.sem), p.target
                else:
                    key, val = ("c", p.eng), p.ticket
                if seen[op.eng].get(key, 0) >= val:
                    continue
                if w.get(key, 0) < val:
                    w[key] = val
            for key, val in w.items():
                seen[op.eng][key] = val
            op.waits = list(w.items())
        final_waits = [(("d", n), v) for n, v in self.dma_count.items() if v > 0]
        streams = {e: [op for op in ops if op.eng == e] for e in self.ENGS}

        def sem_of(key):
            kind, n = key
            return self.dma_sems[n] if kind == "d" else self.cnt_sem[n]

        def run(ename, e):
            for op in streams[ename]:
                for key, val in op.waits:
                    e.wait_ge(sem_of(key), val)
                inst = op.emit(e)
                if op.dma:
                    inst.then_inc(self.dma_sems[op.sem], 16)
                elif op.signal:
                    inst.then_inc(self.cnt_sem[ename], 1)
            if ename == "sp":
                for key, val in final_waits:
                    e.wait_ge(sem_of(key), val)

        with self.nc.Block() as block:
            @block.tensor
            def _(e):
                run("pe", e)

            @block.scalar
            def _(e):
                run("act", e)

            @block.vector
            def _(e):
                run("dve", e)

            @block.gpsimd
            def _(e):
                run("pool", e)

            @block.sync
            def _(e):
                run("sp", e)
        self.stats = {e: len(streams[e]) for e in self.ENGS}
        self.stats["waits"] = sum(len(op.waits) for op in ops)
        self.stats["signals"] = sum(1 for op in ops if op.signal)

    def close(self):
        self.stack.close()


def build_program():
    nc = bass.Bass("TRN2", target_bir_lowering=False)
    P = Prog(nc)

    def din(name, shape):
        return nc.dram_tensor(name, list(shape), F32, kind="ExternalInput").ap()

    def dout(name, shape):
        return nc.dram_tensor(name, list(shape), F32, kind="ExternalOutput").ap()

    xp = din("xp", [2048, D])
    xs = din("xs", [128, D])
    spfm = din("spfm", [128, 4 * 16 * 23])
    scfm = din("scfm", [128, NFF * 32])
    win = din("win", [128, 7 * 4096])
    wpso = din("wpso", [128, 2 * 4096])
    poolw = din("poolw", [128, 512])
    wo = din("wo", [128, 2 * 4096])
    wup = din("wup", [128, 8 * DFF])
    wgt = din("wgt", [128, 8 * DFF])
    wdn = din("wdn", [128, 2 * NFF * 512])
    cvec = din("cvec", [128, 108])
    gsgu = din("gsgu", [128, 512])
    gfin = din("gfin", [128, D])
    g1bc = din("g1bc", [128, D])
    g2bc = din("g2bc", [128, D])
    sgub = din("sgub", [128, 1024])
    wst_d = din("wst", [128, 512])
    wsts_d = din("wsts", [128, 512])

    yp = dout("yp", [2048, D])
    ys = dout("ys", [128, D])
    npp = dout("npp", [128, 4 * 16])
    nps = dout("nps", [128, 4 * 16 * 23])
    ncp = dout("ncp", [128, NFF * 2])
    ncs = dout("ncs", [128, NFF * 32])
    nv = dout("nv", [128, 512])

    X = P.sbuf("X", [128, 9, D], F32)
    R = P.sbuf("R", [128, NFF * TW], BF16)
    MH = P.sbuf("MH", [128, 8, TW], BF16)
    PEXT = P.sbuf("PEXT", [128, 1040], F32)
    PEXTS = P.sbuf("PEXTS", [128, 4, 16, 23], F32)
    RING = [P.sbuf(f"RING{i}", [128, 4096], BF16) for i in range(NSLOT)]
    TF = [P.sbuf(f"TF{i}", [128, 528], F32) for i in range(6)]
    TB = [P.sbuf(f"TB{i}", [128, 512], BF16) for i in range(4)]
    XN = [P.sbuf(f"XN{i}", [128, D], BF16) for i in range(3)]
    G1B = P.sbuf("G1B", [128, D], F32)
    G2B = P.sbuf("G2B", [128, D], F32)
    CV = P.sbuf("CV", [128, 108], F32)
    GS = P.sbuf("GS", [128, 512], F32)
    GF = P.sbuf("GF", [128, D], F32)
    SGB = P.sbuf("SGB", [128, 1024], BF16)
    WST = P.sbuf("WST", [128, 512], BF16)
    WSTS = P.sbuf("WSTS", [128, 512], BF16)
    IDENT = P.sbuf("IDENT", [128, 128], BF16)
    ONES1 = P.sbuf("ONES1", [128, 128], BF16)
    NEGH = P.sbuf("NEGH", [128, 1], F32)
    INVC = P.sbuf("INVC", [128, 16], F32)
    CARRY = P.sbuf("CARRY", [128, 4, 16], F32)
    HALO = P.sbuf("HALO", [128, NFF, 2], F32)
    CS = P.sbuf("CS", [128, NFF, 16, 2], F32)
    STAT = P.sbuf("STAT", [128, 216], F32)
    TMP16 = P.sbuf("TMP16", [128, 16], F32)
    SCR = P.sbuf("SCR", [128, 4], F32)
    SCR2 = P.sbuf("SCR2", [128, 32], F32)
    PS = [P.psum(f"ps{i}", [128, 512], F32) for i in range(8)]

    O_HT, O_YP, O_U, O_VT, O_DD = 0, 9216, 13824, 18432, 23040

    def HT(k, c0, c1):
        return R[:, O_HT + k * TW + c0: O_HT + k * TW + c1]

    def YP(g, c0, c1):
        return R[:, O_YP + g * TW + c0: O_YP + g * TW + c1]

    def U(h, c0, c1):
        return R[:, O_U + h * TW + c0: O_U + h * TW + c1]

    def VT(i, f0, f1):
        return R[:, O_VT + i * 512 + f0: O_VT + i * 512 + f1]

    def DD(b, c0, c1):
        return R[:, O_DD + b * TW + c0: O_DD + b * TW + c1]

    def FF(j, c0, c1):
        return R[:, j * TW + c0: j * TW + c1]

    REG = "REG"

    def MM(out, lhsT, rhs, start, stop, reads, writes):
        P.add("pe", lambda e: e.matmul(out, lhsT=lhsT, rhs=rhs, start=start, stop=stop), reads, writes)

    def TR(out, in_, reads, writes):
        P.add("pe", lambda e: e.transpose(out, in_, IDENT[:]), list(reads) + ["IDENT"], writes)

    def ACT(out, in_, func, reads, writes, **kw):
        P.add("act", lambda e: e.activation(out=out, in_=in_, func=func, **kw), reads, writes)

    def TT(eng, out, in0, in1, op, reads, writes):
        P.add(eng, lambda e: e.tensor_tensor(out=out, in0=in0, in1=in1, op=op), reads, writes)

    def TS(eng, out, in0, s1, s2, op0, op1, reads, writes):
        if s2 is None:
            P.add(eng, lambda e: e.tensor_scalar(out=out, in0=in0, scalar1=s1, scalar2=None, op0=op0),
                  reads, writes)
        else:
            P.add(eng, lambda e: e.tensor_scalar(out=out, in0=in0, scalar1=s1, scalar2=s2, op0=op0, op1=op1),
                  reads, writes)

    def STT(out, in0, scalar, in1, op0, op1, reads, writes):
        P.add("dve", lambda e: e.scalar_tensor_tensor(out=out, in0=in0, scalar=scalar, in1=in1, op0=op0, op1=op1),
              reads, writes)

    def CP(eng, out, in_, reads, writes):
        P.add(eng, lambda e: e.tensor_copy(out=out, in_=in_), reads, writes)

    def MSET(eng, ap, val, writes):
        P.add(eng, lambda e: e.memset(ap, val), (), writes)

    def DMA(eng, out, in_, sem, reads, writes):
        P.add(eng, lambda e: e.dma_start(out=out, in_=in_), reads, writes, dma_sem=sem)

    cnt = {"ps": 0, "tf": 0, "tb": 0, "xn": 0, "stat": 0}

    def psb():
        b = cnt["ps"] % 8
        cnt["ps"] += 1
        return PS[b], ("ps", b)

    def tf():
        b = cnt["tf"] % len(TF)
        cnt["tf"] += 1
        return TF[b], ("TF", b)

    def tb():
        b = cnt["tb"] % len(TB)
        cnt["tb"] += 1
        return TB[b], ("TB", b)

    def xn():
        b = cnt["xn"] % 3
        cnt["xn"] += 1
        return XN[b], ("XN", b)

    def stat3():
        c = cnt["stat"]
        cnt["stat"] += 3
        assert c + 3 <= 216
        return [(STAT[:, c + i: c + i + 1], ("ST", c + i)) for i in range(3)]

    def tile_list():
        tl = []
        tl.append(("win0", win[:, 0:4096], 4096))
        tl.append(("win1", win[:, 4096:8192], 4096))
        tl.append(("poolw", poolw[:, 0:512], 512))
        tl.append(("win2", win[:, 8192:12288], 4096))
        for cg in range(2):
            tl.append((f"pso{cg}", wpso[:, cg * 4096:(cg + 1) * 4096], 4096))
            tl.append((f"wga{cg}", win[:, (3 + cg) * 4096:(4 + cg) * 4096], 4096))
            tl.append((f"wgb{cg}", win[:, (5 + cg) * 4096:(6 + cg) * 4096], 4096))
        tl.append(("wo0", wo[:, 0:4096], 4096))
        tl.append(("wo1", wo[:, 4096:8192], 4096))
        for jg in range(6):
            ncol = 512 if jg < 5 else 256
            tl.append((f"wup{jg}", wup[:, jg * 4096: jg * 4096 + 8 * ncol], 8 * ncol))
            tl.append((f"wgt{jg}", wgt[:, jg * 4096: jg * 4096 + 8 * ncol], 8 * ncol))
        for half in range(2):
            for kg in range(3):
                k0, k1 = 8 * kg, min(8 * kg + 8, NFF)
                tl.append((f"wdn{half}_{kg}",
                           wdn[:, (half * NFF + k0) * 512:(half * NFF + k1) * 512], (k1 - k0) * 512))
        return tl

    st_tiles = tile_list()
    NT_ST = len(st_tiles)
    all_tiles = st_tiles + st_tiles
    ring_state = {"next_load": 0, "cursor": 0}

    def ring_issue(t):
        if t >= len(all_tiles):
            return
        name, src, n = all_tiles[t]
        s = t % NSLOT
        DMA("pool", RING[s][:, 0:n], src, f"w{s}", (), [("w", s)])

    def ring_take(name):
        t = ring_state["cursor"]
        assert all_tiles[t][0] == name, (all_tiles[t][0], name)
        ring_state["cursor"] += 1
        s = t % NSLOT
        return t, RING[s], ("w", s)

    deferred = []

    def ring_release(t, defer=False):
        if defer:
            deferred.append(t + NSLOT)
        else:
            ring_issue(t + NSLOT)

    def flush_deferred():
        for t in deferred:
            ring_issue(t)
        deferred.clear()

    early_x0 = set()
    MSET("pool", NEGH[:], -0.5, ["NEGH"])
    t0, k0 = tf()
    MSET("pool", t0[:, 0:128], 1.0, [k0])
    P.add("pool", lambda e: e.affine_select(out=IDENT[:], in_=t0[:, 0:128], pattern=[[-1, 128]],
                                            compare_op=ALU.is_equal, fill=0.0, base=0, channel_multiplier=1),
          [k0], ["IDENT"])
    DMA("sp", X[:, 0, :], xp[0:128, :], "x0", (), [("X", 0, 0), ("X", 0, 1)])
    DMA("sp", G1B[:], g1bc, "c10", (), ["G1B"])
    early_x0.add(0)
    for i in range(1, 8):
        DMA("sp", X[:, i, :], xp[128 * i: 128 * i + 128, :], f"x{i}", (), [("X", i, 0), ("X", i, 1)])
        early_x0.add(i)
    ring_issue(0)
    for t in range(1, NSLOT):
        deferred.append(t)
    MSET("pool", ONES1[:], 1.0 / 128.0, ["ONES1"])
    MSET("pool", CARRY[:], 0.0, [("CARRY", g) for g in range(4)])
    MSET("pool", HALO[:], 0.0, [("HALO", j) for j in range(NFF)])
    MSET("pool", SCR[:], 0.0, ["SCR"])
    for t in range(15):
        MSET("pool", INVC[:, t:t + 1], 1.0 / (t + 1), ["INVC"])
    DMA("sp", CV[:], cvec, "c0", (), ["CV"])

    def late_setup():
        DMA("sp", GS[:], gsgu, "c1", (), ["GS"])
        for src, dst, key, sem in ((wst_d, WST, "WST", "c3"), (wsts_d, WSTS, "WSTS", "c4")):
            t1, k1 = tf()
            DMA("sp", t1[:, 0:512], src, sem, (), [k1])
            P.add("pool", lambda e, t1=t1, dst=dst: e.affine_select(
                out=dst[:].rearrange("p (h t) -> p h t", h=4),
                in_=t1[:, 0:512].rearrange("p (h t) -> p h t", h=4),
                pattern=[[0, 4], [1, 128]], compare_op=ALU.is_ge, fill=0.0, base=0, channel_multiplier=-1),
                [k1], [key])
        for half in range(2):
            t1, k1 = tf()
            DMA("sp", t1[:, 0:512], sgub[:, half * 512:(half + 1) * 512], f"c{5 + half}", (), [k1])
            CP("dve", SGB[:, half * 512:(half + 1) * 512], t1[:, 0:512], [k1], ["SGB"])

    def late_bulk(gate):
        DMA("sp", G2B[:], g2bc, "c11", gate, ["G2B"])
        DMA("sp", X[:, 8, :], xs, "x8", (), [("X", 8, 0), ("X", 8, 1)])
        DMA("sp", GF[:], gfin, "c2", (), ["GF"])
        DMA("sp", PEXTS[:].rearrange("p g q t -> p (g q t)"), spfm, "c7", (), [("PEXTS", g) for g in range(4)])
        DMA("sp", CS[:].rearrange("p j q r -> p (j q r)"), scfm, "c8", (), [("CS", j) for j in range(NFF)])

    early_x = set()
    pending_reload = []

    def reload_x(i):
        DMA("sp", X[:, i, :], xp[1024 + 128 * i: 1024 + 128 * i + 128, :], f"x{i}", (),
            [("X", i, 0), ("X", i, 1)])
        early_x.add(i)

    G1 = lambda k: CV[:, k:k + 1]
    G2 = lambda k: CV[:, 8 + k:9 + k]
    PSC = lambda g: CV[:, 16 + g:17 + g]
    CW = lambda r, j: CV[:, 20 + r * NFF + j: 21 + r * NFF + j]
    CB = lambda j: CV[:, 86 + j: 87 + j]

    def norm_stats(i):
        xk = [("X", i, 0), ("X", i, 1)]
        (ss, kss), (ms, kms), (rs, krs) = stat3()
        xb, kx = xn()
        ACT(xb[:], X[:, i, :], AF.Square, xk, [kx, kss], accum_out=ss)
        TS("pool", ms, ss, 1.0 / D, EPS, ALU.mult, ALU.add, [kss], [kms])
        TT("pool", rs, ms, NEGH[:], ALU.pow, [kms, "NEGH"], [krs])
        return xb, kx, rs, krs

    def norm_scale(i, xb, kx, rs, krs, GBC, gkey):
        xk = [("X", i, 0), ("X", i, 1)]
        STT(xb[:], X[:, i, :], rs, GBC[:], ALU.mult, ALU.mult, xk + [krs, gkey], [kx])

    def norm_pe(eng, xb, kx, dst3d, dst_keys, extra):
        ps, kp = psb()
        psv = ps[:].bitcast(BF16)
        for k in range(8):
            TR(psv[:, k * 128:(k + 1) * 128], xb[:, k * 128:(k + 1) * 128], [kx], [kp])
        src = psv.rearrange("p (k t) -> p k t", k=8)
        if eng == "act":
            ACT(dst3d, src, AF.Identity, [kp] + extra, dst_keys)
        else:
            CP("dve", dst3d, src, [kp] + extra, dst_keys)

    s0 = {"s": 0, "nst": {}}
    order1 = [8, 0, 1, 2, 3, 4, 5, 6, 7]

    def s0_step():
        sidx = s0["s"]
        n = len(order1)
        if sidx < n:
            t = order1[sidx]
            s0["nst"][t] = norm_stats(t)
        if 0 <= sidx - 1 < n:
            t = order1[sidx - 1]
            norm_scale(t, *s0["nst"][t], G1B, "G1B")
        if 0 <= sidx - 2 < n:
            t = order1[sidx - 2]
            pxb, pkx = s0["nst"].pop(t)[:2]
            norm_pe("act" if sidx % 2 == 0 else "dve", pxb, pkx, MH[:, :, 128 * t:128 * t + 128],
                    [("MH", k, t) for k in range(8)], [])
        s0["s"] += 1

    def s0_done():
        return s0["s"] >= len(order1) + 2

    def pool_adds(ext, ek, Lx, w, three_d, add_eng="pool"):
        S = (lambda ap, a, b: ap[:, :, a:b]) if three_d else (lambda ap, a, b: ap[:, a:b])

        def view(t):
            if three_d:
                return t[:, 0:16 * Lx].rearrange("p (q t) -> p q t", q=16)
            return t[:, 0:Lx]
        cur, ck = ext, list(ek)
        have, vf = 1, 0
        while have < w:
            tbuf, tk = tf()
            nxt = view(tbuf)
            lo = vf + have
            TT(add_eng, S(nxt, lo, Lx), S(cur, lo, Lx), S(cur, lo - have, Lx - have), ALU.add, ck, [tk])
            vf = lo
            have *= 2
            cur, ck = nxt, [tk]
        return cur, ck, S

    def pool_finish(state, ext, ek, Hh, T, w, dd_out, dd_keys, fix_first):
        cur, ck, S = state
        STT(dd_out, S(cur, Hh, Hh + T), 1.0 / w, S(ext, Hh, Hh + T), ALU.mult, ALU.subtract,
            ck + list(ek) + [REG], dd_keys)
        if fix_first and w > 1:
            n = w - 1
            TT("dve", TMP16[:, 0:n], cur[:, Hh:Hh + n], INVC[:, 0:n], ALU.mult, ck + ["INVC"], ["TMP16"])
            TT("dve", dd_out[:, 0:n], TMP16[:, 0:n], ext[:, Hh:Hh + n], ALU.subtract,
               ["TMP16"] + list(ek) + [REG], dd_keys)

    GELU = AF.Gelu_apprx_tanh
    Uall = R[:, O_U:O_U + 4 * TW].rearrange("p (h t) -> p h t", h=4)

    P.marks = []

    def stage(n):
        P.marks.append((n, sum(1 for o in P.ops if o.eng == "pe")))
        if STOP_AT is not None and n >= STOP_AT:
            raise _Stop()

    try:
      for st in range(2):
        stage(10 * st + 0)
        ntile = 8 if st == 0 else 9
        tiles = list(range(ntile))
        blocks = [(0, 512, [0, 1, 2, 3], False), (512, 512, [4, 5, 6, 7], False)]
        if st == 1:
            blocks.append((1024, 128, [8], True))

        def col(i):
            return 128 * i, 128 * i + 128


        HT3 = R[:, O_HT:O_HT + 8 * TW].rearrange("p (k t) -> p k t", k=8)
        if st == 0:
            for i in tiles:
                if i == 8 or i in early_x0:
                    continue
                DMA("sp", X[:, i, :], xp[128 * i: 128 * i + 128, :], f"x{i}", (), [("X", i, 0), ("X", i, 1)])
            nst = {}
            for it in range(ntile + 2):
                if it < ntile:
                    nst[it] = norm_stats(it)
                if 0 <= it - 1 < ntile:
                    norm_scale(it - 1, *nst[it - 1], G1B, "G1B")
                if 0 <= it - 2 < ntile:
                    pi = it - 2
                    c0, c1 = col(pi)
                    pxb, pkx = nst.pop(pi)[:2]
                    norm_pe("act" if pi % 2 == 0 else "dve", pxb, pkx, HT3[:, :, c0:c1],
                            [("HT", k, pi) for k in range(8)], [REG])
        else:
            while not s0_done():
                s0_step()
            for bi, (c0, L, tl, smp) in enumerate(blocks):
                rk = [("MH", k, i) for k in range(8) for i in tl]
                wk = [("HT", k, i) for k in range(8) for i in tl]
                if bi == 0:
                    wk = wk + [REG]
                else:
                    rk = rk + [REG]
                if bi == 1:
                    ACT(HT3[:, :, c0:c0 + L], MH[:, :, c0:c0 + L], AF.Identity, rk, wk)
                else:
                    CP("dve", HT3[:, :, c0:c0 + L], MH[:, :, c0:c0 + L], rk, wk)

        if st == 0:
            late_setup()
            for t in deferred[:2]:
                ring_issue(t)
            del deferred[:2]
        stage(10 * st + 1)
        t_w0, W0, kw0 = ring_take("win0")
        W0v = W0[:].rearrange("p (k c) -> p k c", k=8)

        def emit_A2(g):
            db = g % 2
            for bi, (c0, L, tl, smp) in enumerate(blocks):
                ps, kp = psb()
                MM(ps[:, 0:L], PW[:, g * 128:(g + 1) * 128], DD(db, c0, c0 + L), True, True,
                   [kpw, ("DD", db, bi), REG], [kp])
                TS("dve", YP(g, c0, c0 + L), ps[:, 0:L], PSC(g), None, ALU.mult, None,
                   [kp, "CV", REG], [("YP", g, bi)])

        def emit_A1(g):
            w = POOL_W[g]
            CP("pool", PEXT[:, 0:16], CARRY[:, g, :], [("CARRY", g)], [("PEXT", "h")])
            for bi, (c0, L, tl, smp) in enumerate(blocks):
                ps, kp = psb()
                for k in range(8):
                    MM(ps[:, 0:L], W0v[:, k, g * 128:(g + 1) * 128], HT(k, c0, c0 + L), k == 0, k == 7,
                       [kw0, REG] + [("HT", k, i) for i in tl], [kp])
                if smp:
                    ACT(PEXTS[:, g, :, 15:23], ps[:, 0:128].rearrange("p (q t) -> p q t", q=16), AF.Identity,
                        [kp], [("PEXTS", g)])
                else:
                    ACT(PEXT[:, 16 + c0:16 + c0 + L], ps[:, 0:L], AF.Identity, [kp], [("PEXT", bi)])
            db = g % 2
            pk = [("PEXT", "h"), ("PEXT", 0), ("PEXT", 1)]
            states = {}
            for bi, (c0, L, tl, smp) in enumerate(blocks):
                if not smp:
                    states[bi] = pool_adds(PEXT[:, c0:c0 + 528], pk, 528, w, False,
                                           add_eng=("pool" if bi == 0 else "dve"))
            for bi, (c0, L, tl, smp) in enumerate(blocks):
                if not smp:
                    pool_finish(states[bi], PEXT[:, c0:c0 + 528], pk, 16, 512, w,
                                DD(db, c0, c0 + L), [("DD", db, bi)], st == 0 and bi == 0)
            for bi, (c0, L, tl, smp) in enumerate(blocks):
                if smp:
                    stt_ = pool_adds(PEXTS[:, g, :, :], [("PEXTS", g)], 23, w, True)
                    pool_finish(stt_, PEXTS[:, g, :, :], [("PEXTS", g)], 15, 8, w,
                                DD(db, c0, c0 + L).rearrange("p (q t) -> p q t", q=16), [("DD", db, bi)], False)
            CP("pool", CARRY[:, g, :], PEXT[:, 1024:1040], [("PEXT", 1)], [("CARRY", g)])

        def emit_A3(h):
            for bi, (c0, L, tl, smp) in enumerate(blocks):
                ps, kp = psb()
                for k in range(8):
                    MM(ps[:, 0:L], W1v[:, k, h * 128:(h + 1) * 128], HT(k, c0, c0 + L), k == 0, k == 7,
                       [kw1, REG] + [("HT", k, i) for i in tl], [kp])
                ACT(U(h, c0, c0 + L), ps[:, 0:L], GELU, [kp, REG], [("U", h, i) for i in tl])

        emit_A1(0)
        flush_deferred()
        emit_A1(1)
        t_w1, W1, kw1 = ring_take("win1")
        W1v = W1[:].rearrange("p (k c) -> p k c", k=8)
        emit_A3(0)
        t_pw, PW, kpw = ring_take("poolw")
        emit_A2(0)
        emit_A1(2)
        emit_A3(1)
        emit_A2(1)
        emit_A1(3)
        if st == 1:
            DMA("sp", npp, CARRY[:].rearrange("p g t -> p (g t)"), "o_npp", [("CARRY", g) for g in range(4)], [])
            DMA("sp", nps, PEXTS[:].rearrange("p g q t -> p (g q t)"), "o_nps", [("PEXTS", g) for g in range(4)], [])
        ring_release(t_w0)
        emit_A3(2)
        emit_A2(2)
        emit_A3(3)
        ring_release(t_w1)

        stage(10 * st + 2)
        stage(10 * st + 3)
        t_w2, W2, kw2 = ring_take("win2")
        W2v = W2[:].rearrange("p (k c) -> p k c", k=8)
        def emit_A5(i):
            c0, c1 = col(i)
            smp = (i == 8)
            WS, kws = (WSTS, "WSTS") if smp else (WST, "WST")
            boff = 512 if smp else 0
            ps, kp = psb()
            for h in range(4):
                MM(ps[:, h * 128:(h + 1) * 128], VT(i, h * 128, (h + 1) * 128), WS[:, h * 128:(h + 1) * 128],
                   True, False, [("VT", i), kws, REG], [kp])
                MM(ps[:, h * 128:(h + 1) * 128], ONES1[:], SGB[:, boff + h * 128: boff + (h + 1) * 128],
                   False, True, ["ONES1", "SGB"], [kp])
            uk = [("U", h, i) for h in range(4)]
            TT("dve", Uall[:, :, c0:c1], Uall[:, :, c0:c1], ps[:, :].rearrange("p (h t) -> p h t", h=4), ALU.mult,
               [kp, REG] + uk, uk)

        a4 = {}

        def a4_front(i):
            c0, c1 = col(i)
            ps, kp = psb()
            for k in range(8):
                MM(ps[:, :], HT(k, c0, c1), W2v[:, k, :], k == 0, k == 7, [kw2, REG, ("HT", k, i)], [kp])
            vg, kvg = tf()
            ACT(vg[:, 0:512], ps[:, :], GELU, [kp], [kvg])
            (ss, kss), (ms, kms), (rs, krs) = stat3()
            jb, kj = tb()
            P.add("dve", lambda e, jb=jb, vg=vg, ss=ss: e.scalar_tensor_tensor(
                out=jb[:], in0=vg[:, 0:512], scalar=1.0, in1=vg[:, 0:512],
                op0=ALU.mult, op1=ALU.mult, accum_out=ss), [kvg], [kj, kss])
            TS("pool", ms, ss, 1.0 / 512, EPS, ALU.mult, ALU.add, [kss], [kms])
            TT("pool", rs, ms, NEGH[:], ALU.pow, [kms, "NEGH"], [krs])
            a4[i] = (vg, kvg, rs, krs)

        def a4_back(i):
            vg, kvg, rs, krs = a4.pop(i)
            if i == 8:
                STT(vg[:, 0:512], vg[:, 0:512], rs, GS[:], ALU.mult, ALU.mult, [kvg, krs, "GS"], [kvg])
                DMA("sp", nv, vg[:, 0:512], "o_nv", [kvg], [])
                CP("dve", VT(i, 0, 512), vg[:, 0:512], [kvg, REG], [("VT", i)])
            else:
                STT(VT(i, 0, 512), vg[:, 0:512], rs, GS[:], ALU.mult, ALU.mult, [kvg, krs, "GS", REG], [("VT", i)])

        for it in range(ntile + 3):
            if it == 2:
                emit_A2(3)
                ring_release(t_pw)
            if it < ntile:
                a4_front(it)
            if 0 <= it - 1 < ntile:
                a4_back(it - 1)
            if 0 <= it - 3 < ntile:
                emit_A5(it - 3)
        ring_release(t_w2)

        stage(10 * st + 4)
        stage(10 * st + 5)
        for cg in range(2):
            t_a, WPS, kwpo = ring_take(f"pso{cg}")
            kwso = kwpo
            t_c, WGA, kwga = ring_take(f"wga{cg}")
            t_d, WGB, kwgb = ring_take(f"wgb{cg}")
            WPOv = WPS[:, 0:2048].rearrange("p (k c) -> p k c", k=4)
            WSOv = WPS[:, 2048:4096].rearrange("p (k c) -> p k c", k=4)
            WGAv = WGA[:].rearrange("p (k c) -> p k c", k=8)
            WGBv = WGB[:].rearrange("p (k c) -> p k c", k=8)
            for c4 in range(4):
                c = cg * 4 + c4
                lo, hi = c4 * 128, (c4 + 1) * 128
                for bi, (c0, L, tl, smp) in enumerate(blocks):
                    psA, kA = psb()
                    psB, kB = psb()
                    psGa, kGa = psb()
                    psGb, kGb = psb()
                    for k in range(4):
                        MM(psA[:, 0:L], WPOv[:, k, lo:hi], YP(k, c0, c0 + L), k == 0, k == 3,
                           [kwpo, ("YP", k, bi), REG], [kA])
                    for k in range(4):
                        MM(psB[:, 0:L], WSOv[:, k, lo:hi], U(k, c0, c0 + L), k == 0, k == 3,
                           [kwso, REG] + [("U", k, i) for i in tl], [kB])
                    for k in range(8):
                        MM(psGa[:, 0:L], WGAv[:, k, lo:hi], HT(k, c0, c0 + L), k == 0, k == 7,
                           [kwga, REG] + [("HT", k, i) for i in tl], [kGa])
                    for k in range(8):
                        MM(psGb[:, 0:L], WGBv[:, k, lo:hi], HT(k, c0, c0 + L), k == 0, k == 7,
                           [kwgb, REG] + [("HT", k, i) for i in tl], [kGb])
                    sa, ksa = tb()
                    sb_, ksb = tb()
                    ACT(sa[:, 0:L], psGa[:, 0:L], AF.Sigmoid, [kGa], [ksa])
                    ACT(sb_[:, 0:L], psGb[:, 0:L], AF.Sigmoid, [kGb], [ksb])
                    t1, kt1 = tf()
                    t2, kt2 = tf()
                    TT("dve", t1[:, 0:L], sa[:, 0:L], psA[:, 0:L], ALU.mult, [ksa, kA], [kt1])
                    TT("dve", t2[:, 0:L], sb_[:, 0:L], psB[:, 0:L], ALU.mult, [ksb, kB], [kt2])
                    TT("pool", MH[:, c, c0:c0 + L], t1[:, 0:L], t2[:, 0:L], ALU.add, [kt1, kt2],
                       [("MH", c, i) for i in tl])
                    if st == 0 and c == 0 and bi == 0:
                        late_bulk([("MH", 0, 0)])
            ring_release(t_a)
            ring_release(t_c)
            ring_release(t_d)

        stage(10 * st + 6)
        t_o0, WO0, kwo0 = ring_take("wo0")
        t_o1, WO1, kwo1 = ring_take("wo1")
        WOv = [WO0[:].rearrange("p (k c) -> p k c", k=8), WO1[:].rearrange("p (k c) -> p k c", k=8)]
        kwo = [kwo0, kwo1]
        nst = {}
        for it in range(ntile + 2):
            if it < ntile:
                i = it
                c0, c1 = col(i)
                for half in range(2):
                    ps, kp = psb()
                    for k in range(8):
                        MM(ps[:, :], MH[:, k, c0:c1], WOv[half][:, k, :], k == 0, k == 7,
                           [kwo[half], ("MH", k, i)], [kp])
                    xh = X[:, i, half * 512:(half + 1) * 512]
                    TT("dve", xh, xh, ps[:, :], ALU.add, [kp, ("X", i, half)], [("X", i, half)])
            if it < ntile:
                nst[it] = norm_stats(it)
            if 0 <= it - 1 < ntile:
                norm_scale(it - 1, *nst[it - 1], G2B, "G2B")
            if 0 <= it - 2 < ntile:
                pi = it - 2
                c0, c1 = col(pi)
                pxb, pkx = nst.pop(pi)[:2]
                norm_pe("act" if pi % 2 == 0 else "dve", pxb, pkx, MH[:, :, c0:c1],
                        [("MH", k, pi) for k in range(8)], [])
        ring_release(t_o0)
        ring_release(t_o1)

        stage(10 * st + 7)
        stage(10 * st + 8)
        DMA("sp", SCR2[:, 16:32], cvec[:, 16:32], "bar", (), [REG])
        b2_prev = [None]

        def b2_back(cv, gsrc, fout, kc1, kg, fkeys):
            ACT(cv, cv, GELU, [kc1], [kc1])
            TT("dve", fout, cv, gsrc, ALU.mult, [kc1, kg, REG], fkeys)

        for jg in range(6):
            ncol = 512 if jg < 5 else 256
            t_u, WU, kwu = ring_take(f"wup{jg}")
            t_g, WG, kwg = ring_take(f"wgt{jg}")
            WUv = WU[:, 0:8 * ncol].rearrange("p (k c) -> p k c", k=8)
            WGv = WG[:, 0:8 * ncol].rearrange("p (k c) -> p k c", k=8)
            for j4 in range(ncol // 128):
                j = jg * 4 + j4
                lo, hi = j4 * 128, (j4 + 1) * 128
                for bi, (c0, L, tl, smp) in enumerate(blocks):
                    psa, ka = psb()
                    psg, kg = psb()
                    for k in range(8):
                        MM(psa[:, 0:L], WUv[:, k, lo:hi], MH[:, k, c0:c0 + L], k == 0, k == 7,
                           [kwu] + [("MH", k, i) for i in tl], [ka])
                    for k in range(8):
                        MM(psg[:, 0:L], WGv[:, k, lo:hi], MH[:, k, c0:c0 + L], k == 0, k == 7,
                           [kwg] + [("MH", k, i) for i in tl], [kg])
                    ae, kae = tf()
                    c1b, kc1 = tf()
                    if smp:
                        ae3 = ae[:, 0:160].rearrange("p (q t) -> p q t", q=16)
                        ACT(ae3[:, :, 2:10], psa[:, 0:128].rearrange("p (q t) -> p q t", q=16), AF.Identity,
                            [ka], [kae])
                        CP("pool", ae3[:, :, 0:2], CS[:, j, :, :], [("CS", j), kae], [kae])
                        CP("pool", CS[:, j, :, :], ae3[:, :, 8:10], [kae], [("CS", j)])
                        a0, a1, a2 = ae3[:, :, 0:8], ae3[:, :, 1:9], ae3[:, :, 2:10]
                        cv = c1b[:, 0:128].rearrange("p (q t) -> p q t", q=16)
                        gsrc = psg[:, 0:128].rearrange("p (q t) -> p q t", q=16)
                        fout = FF(j, c0, c0 + L).rearrange("p (q t) -> p q t", q=16)
                    else:
                        ACT(ae[:, 2:2 + L], psa[:, 0:L], AF.Identity, [ka], [kae])
                        CP("pool", ae[:, 0:2], HALO[:, j, :], [("HALO", j), kae], [kae])
                        CP("pool", HALO[:, j, :], ae[:, L:L + 2], [kae], [("HALO", j)])
                        a0, a1, a2 = ae[:, 0:L], ae[:, 1:L + 1], ae[:, 2:L + 2]
                        cv = c1b[:, 0:L]
                        gsrc = psg[:, 0:L]
                        fout = FF(j, c0, c0 + L)
                    TS("pool", cv, a0, CW(0, j), CB(j), ALU.mult, ALU.add, [kae, "CV"], [kc1])
                    STT(cv, a1, CW(1, j), cv, ALU.mult, ALU.add, [kae, kc1, "CV"], [kc1])
                    STT(cv, a2, CW(2, j), cv, ALU.mult, ALU.add, [kae, kc1, "CV"], [kc1])
                    if b2_prev[0] is not None:
                        b2_back(*b2_prev[0])
                    b2_prev[0] = (cv, gsrc, fout, kc1, kg, [("F", j, i) for i in tl])
            ring_release(t_u)
            ring_release(t_g)

        if b2_prev[0] is not None:
            b2_back(*b2_prev[0])
            b2_prev[0] = None
        if st == 1:
            DMA("sp", ncp, HALO[:].rearrange("p j r -> p (j r)"), "o_ncp", [("HALO", j) for j in range(NFF)], [])
            DMA("sp", ncs, CS[:].rearrange("p j q r -> p (j q r)"), "o_ncs", [("CS", j) for j in range(NFF)], [])
        stage(10 * st + 9)
        for half in range(2):
            wd = [ring_take(f"wdn{half}_{kg}") for kg in range(3)]
            for i in tiles:
                c0, c1 = col(i)
                ps, kp = psb()
                for k in range(NFF):
                    t_, Wd, kwd = wd[k // 8]
                    nk = 8 if k // 8 < 2 else NFF - 16
                    Wdv = Wd[:, 0:nk * 512].rearrange("p (k c) -> p k c", c=512)
                    MM(ps[:, :], FF(k, c0, c1), Wdv[:, k % 8, :], k == 0, k == NFF - 1,
                       [kwd, ("F", k, i), REG], [kp])
                xh = X[:, i, half * 512:(half + 1) * 512]
                TT("dve", xh, xh, ps[:, :], ALU.add, [kp, ("X", i, half)], [("X", i, half)])
                if half == 1:
                    xk = [("X", i, 0), ("X", i, 1)]
                    (ss, kss), (ms, kms), (rs, krs) = stat3()
                    jt, kjt = tf()
                    ACT(jt[:].bitcast(BF16)[:, 0:D], X[:, i, :], AF.Square, xk, [kjt, kss], accum_out=ss)
                    TS("pool", ms, ss, 1.0 / D, EPS, ALU.mult, ALU.add, [kss], [kms])
                    TT("pool", rs, ms, NEGH[:], ALU.pow, [kms, "NEGH"], [krs])
                    dst = ys if i == 8 else yp[st * 1024 + 128 * i: st * 1024 + 128 * i + 128, :]
                    if st == 0 and i >= 6:
                        for hh in range(2):
                            to, kto = tf()
                            STT(to[:, 0:512], X[:, i, hh * 512:(hh + 1) * 512], rs, GF[:, hh * 512:(hh + 1) * 512],
                                ALU.mult, ALU.mult, xk + [krs, "GF"], [kto])
                            DMA("sp", dst[:, hh * 512:(hh + 1) * 512], to[:, 0:512], f"y{i}_{hh}", [kto], [])
                    else:
                        STT(X[:, i, :], X[:, i, :], rs, GF[:], ALU.mult, ALU.mult, xk + [krs, "GF"], xk)
                        DMA("sp", dst, X[:, i, :], f"y{i}", xk, [])
                    if st == 0:
                        if 1 <= i <= 5:
                            reload_x(i - 1)
                        elif i == 6:
                            reload_x(5)
                            reload_x(6)
                        elif i == 7:
                            reload_x(7)
                        nxt = order1[s0["s"]] if s0["s"] < len(order1) else None
                        if nxt is None or nxt == 8 or nxt <= i - 2:
                            s0_step()
            for t_, _, _ in wd:
                ring_release(t_, defer=(half == 1))

    except _Stop:
        pass
    assert STOP_AT is not None or ring_state["cursor"] == len(all_tiles)
    P.finalize()
    P.sbuf_left = nc.sbuf_bytes_remaining
    P.close()
    return nc, P


def _tile_w(W, nk, tw):
    N = W.shape[1]
    Wp = W.reshape(nk, 128, N).transpose(1, 0, 2)
    parts = []
    for c0 in range(0, N, tw):
        w = min(tw, N - c0)
        parts.append(Wp[:, :, c0:c0 + w].reshape(128, nk * w))
    return np.ascontiguousarray(np.concatenate(parts, axis=1), dtype=np.float32)


_CACHE = {}


def kernel(**inputs):
    f = lambda k: np.asarray(inputs[k], dtype=np.float32)
    x_prompt, x_sample = f("x_prompt"), f("x_sample")
    state_pool, state_conv = f("state_pool"), f("state_ffn_conv")
    w_in = f("w_in")[0]
    shared = {}
    shared["win"] = _tile_w(w_in, 8, 512)
    wpo_t = _tile_w(f("w_pool_out")[0], 4, 512).reshape(128, 2, 2048)
    wso_t = _tile_w(f("w_sgu_out")[0], 4, 512).reshape(128, 2, 2048)
    shared["wpso"] = np.ascontiguousarray(np.concatenate([wpo_t, wso_t], axis=2).reshape(128, 2 * 4096))
    shared["poolw"] = np.ascontiguousarray(f("pool_w")[0].transpose(1, 0, 2).reshape(128, 512))
    shared["wo"] = _tile_w(f("w_o")[0], 8, 512)
    shared["wup"] = _tile_w(f("ffn_w_up")[0], 8, 512)
    shared["wgt"] = _tile_w(f("ffn_w_gate")[0], 8, 512)
    shared["wdn"] = _tile_w(f("ffn_w_down")[0], NFF, 512)
    cvec = np.zeros((128, 108), np.float32)
    cvec[:, 0:8] = f("norm1_g")[0].reshape(8, 128).T
    cvec[:, 8:16] = f("norm2_g")[0].reshape(8, 128).T
    cvec[:, 16:20] = f("pool_scale")[0].reshape(4, 128).T
    cvec[:, 20:86] = f("ffn_conv_w")[0].reshape(3, NFF, 128).transpose(2, 0, 1).reshape(128, 66)
    cvec[:, 86:108] = f("ffn_conv_b")[0].reshape(NFF, 128).T
    shared["cvec"] = cvec
    shared["gsgu"] = np.ascontiguousarray(np.broadcast_to(f("sgu_norm_g")[0][None, :], (128, 512)))
    shared["gfin"] = np.ascontiguousarray(np.broadcast_to(f("final_norm_g")[None, :], (128, D)))
    shared["g1bc"] = np.ascontiguousarray(np.broadcast_to(f("norm1_g")[0][None, :], (128, D)))
    shared["g2bc"] = np.ascontiguousarray(np.broadcast_to(f("norm2_g")[0][None, :], (128, D)))
    sgu_b = f("sgu_b")[0]
    sgub = np.zeros((1, 1024), np.float32)
    sgub[0, 0:512] = sgu_b.reshape(512)
    sgub[0, 512:1024] = np.tile(sgu_b[:, :8], (1, 16)).reshape(512)
    shared["sgub"] = np.ascontiguousarray(np.broadcast_to(sgub, (128, 1024)))
    sgu_w = f("sgu_w")[0]
    shared["wst"] = np.ascontiguousarray(sgu_w.transpose(2, 0, 1).reshape(128, 512))
    wsts = np.zeros((128, 4, 128), np.float32)
    blk = sgu_w[:, :8, :8].transpose(2, 0, 1)
    for q in range(16):
        wsts[8 * q:8 * q + 8, :, 8 * q:8 * q + 8] = blk
    shared["wsts"] = wsts.reshape(128, 512)

    in_maps = []
    for c in range(8):
        m = dict(shared)
        m["xp"] = np.ascontiguousarray(x_prompt[c])
        m["xs"] = np.ascontiguousarray(x_sample[16 * c:16 * c + 16].reshape(128, D))
        sp = state_pool[0, 16 * c:16 * c + 16]
        spf = np.zeros((128, 4, 16, 23), np.float32)
        spf[:, :, :, 0:15] = sp.reshape(16, 15, 4, 128).transpose(3, 2, 0, 1)
        m["spfm"] = spf.reshape(128, 4 * 16 * 23)
        sc = state_conv[0, 16 * c:16 * c + 16]
        m["scfm"] = np.ascontiguousarray(sc.reshape(16, 2, NFF, 128).transpose(3, 2, 0, 1).reshape(128, NFF * 32))
        in_maps.append(m)

    if "nc" not in _CACHE:
        _CACHE["nc"] = build_program()[0]
    nc = _CACHE["nc"]
    res = run_bass_kernel_spmd(nc, in_maps, core_ids=list(range(8)))
    rs = res.results

    y_prompt = np.zeros((8, 2048, D), np.float32)
    y_sample = np.zeros((128, 8, D), np.float32)
    new_pool_prompt = np.zeros((1, 8, 15, 512), np.float32)
    new_pool_sample = np.zeros((1, 128, 15, 512), np.float32)
    new_conv_prompt = np.zeros((1, 8, 2, DFF), np.float32)
    new_conv_sample = np.zeros((1, 128, 2, DFF), np.float32)
    new_v = np.zeros((1, 128, 8, 512), np.float32)
    for c in range(8):
        r = rs[c]
        sl = slice(16 * c, 16 * c + 16)
        y_prompt[c] = r["yp"]
        y_sample[sl] = r["ys"].reshape(16, 8, D)
        new_pool_prompt[0, c] = r["npp"].reshape(128, 4, 16)[:, :, 1:16].transpose(2, 1, 0).reshape(15, 512)
        new_pool_sample[0, sl] = r["nps"].reshape(128, 4, 16, 23)[:, :, :, 8:23].transpose(2, 3, 1, 0).reshape(16, 15, 512)
        new_conv_prompt[0, c] = r["ncp"].reshape(128, NFF, 2).transpose(2, 1, 0).reshape(2, DFF)
        new_conv_sample[0, sl] = r["ncs"].reshape(128, NFF, 16, 2).transpose(2, 3, 1, 0).reshape(16, 2, DFF)
        new_v[0, sl] = r["nv"].reshape(16, 8, 512)
    return (y_prompt, y_sample, new_pool_prompt, new_pool_sample, new_conv_prompt, new_conv_sample, new_v)
```
